# Optimizing a Trainium2 kernel written in Bass

```python
import jax, jax.numpy as jnp
from jax import lax
import numpy as np

D_MODEL = 2048
BATCH = 8
SEQ = 2048
DEPTH = 2

N_EVEN = (DEPTH + 1) // 2
N_ODD = DEPTH // 2

RET_HEADS = 4
RET_DK = 256
RET_DV = 256
RET_THETA_BASE = 10000.0
GLA_HEADS = 4
GLA_DK = 128
GLA_DV = 256
GLA_GATE_RANK = 16
GLA_GATE_NORM = 16.0
CHUNK = 64

RET_QK = RET_HEADS * RET_DK
RET_V = RET_HEADS * RET_DV
GLA_QK = GLA_HEADS * GLA_DK
GLA_V = GLA_HEADS * GLA_DV
MIX_WIDTH = RET_V + GLA_V
IN_SIZES = (RET_QK, RET_QK, RET_V, RET_V, GLA_QK, GLA_QK, GLA_V, GLA_V, GLA_GATE_RANK)
IN_WIDTH = sum(IN_SIZES)
IN_OFFSETS = tuple(int(o) for o in np.cumsum(IN_SIZES)[:-1])

ATT_HEADS = 16
ATT_HEAD_DIM = D_MODEL // ATT_HEADS
ATT_WIDTH = ATT_HEADS * ATT_HEAD_DIM
DILATED_BRANCHES = ((128, 1), (512, 4), (2048, 16))
ATT_BLOCK = 128

FFN_HIDDEN = -(-8 * D_MODEL // (3 * 256)) * 256
PLE_DIM = 256
NORM_EPS = 1e-6

kernel_name = 'hybrid_retention_gla_dilated_trunk'


def rms_norm(x, w):
    x32 = x.astype(jnp.float32)
    y = x32 * lax.rsqrt(jnp.mean(x32 * x32, axis=-1, keepdims=True) + NORM_EPS)
    return (y * w.astype(jnp.float32)).astype(x.dtype)


def xpos_rotate(t, positions):
    half = t.shape[-1] // 2
    inv_freq = 1.0 / jnp.power(RET_THETA_BASE, jnp.linspace(0.0, 1.0, half, dtype=jnp.float32))
    ang = positions.astype(jnp.float32)[..., None] * inv_freq
    cos, sin = jnp.cos(ang)[:, :, None, :], jnp.sin(ang)[:, :, None, :]
    t2 = t.reshape(*t.shape[:-1], half, 2)
    te, to = t2[..., 0], t2[..., 1]
    return jnp.stack([te * cos - to * sin, te * sin + to * cos], axis=-1).reshape(t.shape)


def to_chunks(t):
    b, s, h, d = t.shape
    return t.reshape(b, s // CHUNK, CHUNK, h, d).transpose(1, 0, 2, 3, 4)


def from_chunks(t):
    nc, b, c, h, d = t.shape
    return t.transpose(1, 0, 2, 3, 4).reshape(b, nc * c, h, d)


def retention(q, k, v):
    b, s, h, dk = q.shape
    dv = v.shape[-1]
    log_g = jnp.log1p(-jnp.exp2(-5.0 - jnp.arange(h, dtype=jnp.float32)))
    idx = jnp.arange(CHUNK, dtype=jnp.float32)
    rel = idx[:, None] - idx[None, :]
    intra_decay = jnp.where(rel[None] >= 0,
                            jnp.exp(jnp.maximum(rel, 0.0)[None] * log_g[:, None, None]), 0.0)
    q_decay = jnp.exp((idx + 1.0)[:, None] * log_g[None, :])
    k_decay = jnp.exp((CHUNK - 1.0 - idx)[:, None] * log_g[None, :])
    chunk_decay = jnp.exp(CHUNK * log_g)
    k = k * dk ** -0.5

    def step(state, inp):
        qc, kc, vc = inp
        sc = jnp.einsum('bihd,bjhd->bhij', qc, kc) * intra_decay
        o = (jnp.einsum('bhij,bjhe->bihe', sc, vc)
             + jnp.einsum('bihd,bhde->bihe', qc * q_decay[..., None], state))
        state = (state * chunk_decay[:, None, None]
                 + jnp.einsum('bjhd,bjhe->bhde', kc * k_decay[..., None], vc))
        return state, o

    init = jnp.zeros((b, h, dk, dv), jnp.float32)
    _, o = lax.scan(step, init, (to_chunks(q), to_chunks(k), to_chunks(v)))
    return from_chunks(o)


def gated_linear_attention(q, k, v, log_a):
    b, s, h, dk = q.shape
    dv = v.shape[-1]
    q = q * dk ** -0.5
    causal = jnp.tril(jnp.ones((CHUNK, CHUNK), dtype=bool))

    def step(state, inp):
        qc, kc, vc, ac = inp
        cum = jnp.cumsum(ac, axis=1)
        last = cum[:, -1]
        q_t = qc * jnp.exp(cum)
        k_t = kc * jnp.exp(-cum)
        sc = jnp.where(causal, jnp.einsum('bihd,bjhd->bhij', q_t, k_t), 0.0)
        o = (jnp.einsum('bhij,bjhe->bihe', sc, vc)
             + jnp.einsum('bihd,bhde->bihe', q_t, state))
        state = (jnp.exp(last)[..., None] * state
                 + jnp.einsum('bjhd,bjhe->bhde', kc * jnp.exp(last[:, None] - cum), vc))
        return state, o

    init = jnp.zeros((b, h, dk, dv), jnp.float32)
    _, o = lax.scan(step, init, (to_chunks(q), to_chunks(k), to_chunks(v), to_chunks(log_a)))
    return from_chunks(o)


def dilated_branch(q, k, v, window, dilation):
    b, s, h, d = q.shape
    span = window // dilation
    sub_len = s // dilation
    n_blk = -(-sub_len // ATT_BLOCK)
    pad_len = n_blk * ATT_BLOCK

    def strided(t):
        t = t.reshape(b, sub_len, dilation, h, d)
        return jnp.pad(t, ((0, 0), (0, pad_len - sub_len), (0, 0), (0, 0), (0, 0)))

    def band(t):
        t = jnp.pad(strided(t), ((0, 0), (ATT_BLOCK, 0), (0, 0), (0, 0), (0, 0)))
        prev = t[:, :pad_len].reshape(b, n_blk, ATT_BLOCK, dilation, h, d)
        cur = t[:, ATT_BLOCK:].reshape(b, n_blk, ATT_BLOCK, dilation, h, d)
        return jnp.concatenate([prev, cur], axis=2)

    qb = strided(q).reshape(b, n_blk, ATT_BLOCK, dilation, h, d)
    kb, vb = band(k), band(v)
    scores = jnp.einsum('bnqrhd,bnkrhd->bnrhqk', qb, kb) * d ** -0.5
    qi = jnp.arange(ATT_BLOCK)[:, None]
    kj = jnp.arange(2 * ATT_BLOCK)[None, :]
    dist = qi + ATT_BLOCK - kj
    key_pos = jnp.arange(n_blk)[:, None, None] * ATT_BLOCK + kj[None] - ATT_BLOCK
    valid = (dist >= 0)[None] & (dist <= span)[None] & (key_pos >= 0)
    scores = jnp.where(valid[None, :, None, None], scores, -jnp.inf)
    m = jnp.max(scores, axis=-1, keepdims=True)
    pr = jnp.exp(scores - m)
    den = jnp.sum(pr, axis=-1)
    out = jnp.einsum('bnrhqk,bnkrhd->bnqrhd', pr, vb) / jnp.transpose(den, (0, 1, 4, 2, 3))[..., None]
    lse = jnp.transpose(m[..., 0] + jnp.log(den), (0, 1, 4, 2, 3))
    out = out.reshape(b, pad_len, dilation, h, d)[:, :sub_len].reshape(b, s, h, d)
    lse = lse.reshape(b, pad_len, dilation, h)[:, :sub_len].reshape(b, s, h)
    return out, lse


def dilated_attention(q, k, v):
    outs, lses = [], []
    for window, dilation in DILATED_BRANCHES:
        o, l = dilated_branch(q, k, v, window, dilation)
        outs.append(o)
        lses.append(l)
    wts = jax.nn.softmax(jnp.stack(lses, axis=0), axis=0)
    return jnp.sum(wts[..., None] * jnp.stack(outs, axis=0), axis=0)


def retention_gla_mixer(xn, positions, w_in, gate_up, gate_b, ret_norm_w, gla_norm_w, w_out):
    b, s, _ = xn.shape
    z = (xn @ w_in).astype(jnp.float32)
    rq, rk, rv, rg, gq, gk, gv, gg, glr = jnp.split(z, IN_OFFSETS, axis=-1)
    rq = xpos_rotate(rq.reshape(b, s, RET_HEADS, RET_DK), positions)
    rk = xpos_rotate(rk.reshape(b, s, RET_HEADS, RET_DK), positions)
    ret = retention(rq, rk, rv.reshape(b, s, RET_HEADS, RET_DV))
    ret = rms_norm(ret, ret_norm_w.reshape(RET_HEADS, RET_DV)).astype(jnp.float32) \
        * jax.nn.silu(rg.reshape(b, s, RET_HEADS, RET_DV))
    log_a = jax.nn.log_sigmoid(glr @ gate_up.astype(jnp.float32) + gate_b.astype(jnp.float32)) / GLA_GATE_NORM
    gla = gated_linear_attention(gq.reshape(b, s, GLA_HEADS, GLA_DK), gk.reshape(b, s, GLA_HEADS, GLA_DK),
                                 gv.reshape(b, s, GLA_HEADS, GLA_DV), log_a.reshape(b, s, GLA_HEADS, GLA_DK))
    gla = rms_norm(gla, gla_norm_w.reshape(GLA_HEADS, GLA_DV)).astype(jnp.float32) \
        * jax.nn.silu(gg.reshape(b, s, GLA_HEADS, GLA_DV))
    o = jnp.concatenate([ret.reshape(b, s, RET_V), gla.reshape(b, s, GLA_V)], axis=-1)
    return o.astype(xn.dtype) @ w_out


def dilated_mixer(xn, w_qkv, w_out):
    b, s, _ = xn.shape
    q, k, v = jnp.split((xn @ w_qkv).astype(jnp.float32), 3, axis=-1)
    heads = lambda t: t.reshape(b, s, ATT_HEADS, ATT_HEAD_DIM)
    o = dilated_attention(heads(q), heads(k), heads(v))
    return o.reshape(b, s, ATT_WIDTH).astype(xn.dtype) @ w_out


def swiglu(xn, w_gate, w_up, w_down):
    return (jax.nn.silu(xn @ w_gate) * (xn @ w_up)) @ w_down


def setup_inputs(seed: int = 0) -> dict:
    key = jax.random.key(seed)
    ks = jax.random.split(key, 24)
    f32 = jnp.float32

    def w(k, shape, fan_in):
        return jax.random.normal(k, shape, f32) * fan_in ** -0.5

    def gain(k, shape):
        return 1.0 + 0.05 * jax.random.normal(k, shape, f32)

    x = jax.random.normal(ks[0], (BATCH, SEQ, D_MODEL), f32)
    p = jax.random.normal(ks[1], (DEPTH, BATCH, SEQ, PLE_DIM), f32)
    positions = (jnp.arange(SEQ, dtype=jnp.int32)[None, :]
                 + jax.random.randint(ks[2], (BATCH, 1), 0, 4096, dtype=jnp.int32))
    return {
        'x': x,
        'p': p,
        'positions': positions,
        'attn_norm_w': gain(ks[3], (DEPTH, D_MODEL)),
        'ffn_norm_w': gain(ks[4], (DEPTH, D_MODEL)),
        'ple_norm_w': gain(ks[5], (DEPTH, D_MODEL)),
        'final_norm_w': gain(ks[6], (D_MODEL,)),
        'ab_w_in': w(ks[7], (N_EVEN, D_MODEL, IN_WIDTH), D_MODEL),
        'ab_gla_gate_up': w(ks[8], (N_EVEN, GLA_GATE_RANK, GLA_QK), GLA_GATE_RANK),
        'ab_gla_gate_b': 0.1 * jax.random.normal(ks[9], (N_EVEN, GLA_QK), f32),
        'ab_ret_norm_w': gain(ks[10], (N_EVEN, RET_V)),
        'ab_gla_norm_w': gain(ks[11], (N_EVEN, GLA_V)),
        'ab_w_out': w(ks[12], (N_EVEN, MIX_WIDTH, D_MODEL), MIX_WIDTH),
        'c_w_qkv': w(ks[13], (N_ODD, D_MODEL, 3 * ATT_WIDTH), D_MODEL),
        'c_w_out': w(ks[14], (N_ODD, ATT_WIDTH, D_MODEL), ATT_WIDTH),
        'ffn_w_gate': w(ks[15], (DEPTH, D_MODEL, FFN_HIDDEN), D_MODEL),
        'ffn_w_up': w(ks[16], (DEPTH, D_MODEL, FFN_HIDDEN), D_MODEL),
        'ffn_w_down': w(ks[17], (DEPTH, FFN_HIDDEN, D_MODEL), FFN_HIDDEN),
        'ple_w_proj': w(ks[18], (DEPTH, PLE_DIM, D_MODEL), PLE_DIM),
        'ple_w_gate': w(ks[19], (DEPTH, D_MODEL, D_MODEL), D_MODEL),
    }


def reference(x, p, positions, attn_norm_w, ffn_norm_w, ple_norm_w, final_norm_w,
              ab_w_in, ab_gla_gate_up, ab_gla_gate_b, ab_ret_norm_w, ab_gla_norm_w, ab_w_out,
              c_w_qkv, c_w_out, ffn_w_gate, ffn_w_up, ffn_w_down, ple_w_proj, ple_w_gate):
    for i in range(DEPTH):
        j = i // 2
        xn = rms_norm(x, attn_norm_w[i])
        if i % 2 == 0:
            mix = retention_gla_mixer(xn, positions, ab_w_in[j], ab_gla_gate_up[j], ab_gla_gate_b[j],
                                      ab_ret_norm_w[j], ab_gla_norm_w[j], ab_w_out[j])
        else:
            mix = dilated_mixer(xn, c_w_qkv[j], c_w_out[j])
        h = x + mix
        h = h + swiglu(rms_norm(h, ffn_norm_w[i]), ffn_w_gate[i], ffn_w_up[i], ffn_w_down[i])
        gate = jax.nn.sigmoid(rms_norm(h, ple_norm_w[i]) @ ple_w_gate[i])
        x = h + gate * (p[i] @ ple_w_proj[i])
    return rms_norm(x, final_norm_w)
```

```python
import math
from contextlib import ExitStack

import numpy as np
import concourse.bass as bass
import concourse.mybir as mybir
from concourse.bass_utils import run_bass_kernel_spmd

F32 = mybir.dt.float32
BF16 = mybir.dt.bfloat16
I32 = mybir.dt.int32
AF = mybir.ActivationFunctionType
ALU = mybir.AluOpType
AX = mybir.AxisListType

T = 2048
D = 2048
KC = 16
NTG = 4
NTT = 16
FF = 5632
FKC = 44
EPS = 1e-6
NCORES = 8
WSLOT = 8192
NWS = 2

ENGS = ("pe", "act", "dve", "pool", "sp")
BLOCKFN = {"pe": "tensor", "act": "scalar", "dve": "vector", "pool": "gpsimd", "sp": "sync"}
SAME_SYNC = True


class Op:
    __slots__ = ("eng", "fn", "deps", "needs_inc", "ticket", "key", "done")

    def __init__(self, eng, fn, key=None):
        self.eng = eng
        self.fn = fn
        self.deps = set()
        self.needs_inc = False
        self.ticket = None
        self.key = key
        self.done = False


class Buf:
    __slots__ = ("name", "w", "r", "war")

    def __init__(self, name=""):
        self.name = name
        self.w = []
        self.r = []
        self.war = []

    def reset(self):
        self.w = []
        self.r = []
        self.war = []


def _add(lst, op):
    if op.key is None:
        for i, o in enumerate(lst):
            if o.key is None and o.eng == op.eng:
                lst[i] = op
                return
    lst.append(op)


class Prog:
    def __init__(self, nc, stack):
        self.nc = nc
        self.stack = stack
        self.ops = {e: [] for e in ENGS}
        self.nsem = 0
        self.esem = {}
        self.ecount = {}
        for e in ENGS:
            self._new_esem(e)
        self.dsem = {}
        self.dcount = {}
        self.waited = {e: {} for e in ENGS}
        self.bufs = []
        self.persist_keys = set()
        self.persist_bufs = set()

    def _new_esem(self, e):
        self.nsem += 1
        self.esem[e] = self.stack.enter_context(self.nc.semaphore(f"se{self.nsem}"))
        self.ecount[e] = 0

    def buf(self, name=""):
        b = Buf(name)
        self.bufs.append(b)
        return b

    def _record(self, op, reads, writes, waw):
        for b in reads:
            for o in b.w:
                if o is not op:
                    op.deps.add(o)
            _add(b.r, op)
        for b in writes:
            rd = [o for o in b.r if o is not op]
            if rd:
                for o in rd:
                    op.deps.add(o)
                for o in b.w:
                    if o is not op:
                        op.deps.add(o)
                b.war = rd
                b.r = [o for o in b.r if o is op]
                b.w = [op]
            else:
                for o in b.war:
                    op.deps.add(o)
                if waw:
                    for o in b.w:
                        if o is not op:
                            op.deps.add(o)
                _add(b.w, op)
        for d in op.deps:
            d.needs_inc = True
        self.ops[op.eng].append(op)

    def op(self, eng, fn, reads=(), writes=(), waw=True):
        o = Op(eng, fn)
        self._record(o, reads, writes, waw)
        return o

    def dma(self, eng, out, in_, reads=(), writes=(), key="d", waw=False):
        if key not in self.dsem:
            self.nsem += 1
            self.dsem[key] = self.stack.enter_context(self.nc.semaphore(f"sd{self.nsem}"))
            self.dcount[key] = 0
        o = Op(eng, lambda e, out=out, in_=in_: e.dma_start(out=out, in_=in_), key=key)
        self.dcount[key] += 16
        o.ticket = self.dcount[key]
        self._record(o, reads, writes, waw)
        return o

    def flush(self):
        nc = self.nc
        for e in ENGS:
            last = None
            for o in self.ops[e]:
                if o.key is None:
                    last = o
            if last is not None:
                last.needs_inc = True
            for o in self.ops[e]:
                if o.key is None and o.needs_inc:
                    self.ecount[e] += 1
                    o.ticket = self.ecount[e]
        finals = {e: (self.esem[e], self.ecount[e]) for e in ENGS}
        dfinals = {k: (self.dsem[k], self.dcount[k]) for k in self.dsem if k not in self.persist_keys}
        with nc.Block() as block:
            for e in ENGS:
                def body(engine, e=e):
                    wd = self.waited[e]
                    for o in self.ops[e]:
                        waits = {}
                        for d in o.deps:
                            if d.done:
                                continue
                            if d.key is None:
                                if d.eng == e and (e == "pe" or not SAME_SYNC):
                                    continue
                                sem = self.esem[d.eng]
                            else:
                                sem = self.dsem[d.key]
                            v = d.ticket
                            if waits.get(sem, 0) < v:
                                waits[sem] = v
                        for sem, v in waits.items():
                            if wd.get(sem, 0) >= v:
                                continue
                            engine.wait_ge(sem, v)
                            wd[sem] = v
                        ins = o.fn(engine)
                        if o.key is not None:
                            ins.then_inc(self.dsem[o.key], 16)
                        elif o.needs_inc:
                            ins.then_inc(self.esem[e], 1)
                    for e2 in ENGS:
                        sem, v = finals[e2]
                        if e2 == e or v == 0:
                            continue
                        if wd.get(sem, 0) < v:
                            engine.wait_ge(sem, v)
                            wd[sem] = v
                    for k, (sem, v) in dfinals.items():
                        if v and wd.get(sem, 0) < v:
                            engine.wait_ge(sem, v)
                            wd[sem] = v
                getattr(block, BLOCKFN[e])(body)
        for e in ENGS:
            for o in self.ops[e]:
                if o.key is None or o.key not in self.persist_keys:
                    o.done = True
                o.deps = None
                o.fn = None
            self.ops[e] = []
            if self.ecount[e] > 12000:
                self._new_esem(e)
        for b in self.bufs:
            if b not in self.persist_bufs:
                b.reset()
            else:
                b.r = []
                b.war = []
                b.w = [o for o in b.w if not o.done]


def _blk(W, nbw):
    K, N = W.shape
    kc = K // 128
    nb = N // nbw
    return np.ascontiguousarray(W.reshape(kc, 128, nb, nbw).transpose(2, 1, 0, 3)).reshape(nb, 128, kc * nbw)


def _host_constants():
    c = {}
    half = 128
    lin = np.linspace(0.0, 1.0, half, dtype=np.float32)
    inv = (np.float32(1.0) / np.power(np.float32(10000.0), lin)).astype(np.float32)
    c["invf"] = inv.reshape(128, 1)
    c["ident"] = np.eye(128, dtype=np.float32)
    hh = np.arange(4, dtype=np.float64)
    gam = 1.0 - np.exp2(-5.0 - hh)
    jj = np.arange(128)[:, None].astype(np.float64)
    ii = np.arange(512)[None, :].astype(np.float64)
    rm = np.zeros((4, 128, 5, 512), np.float32)
    for h in range(4):
        rm[h, :, 0, :] = (gam[h] ** (ii - jj)) * 256 ** -0.5
        for m in range(4):
            d = ii - (m * 128 + jj)
            rm[h, :, 1 + m, :] = np.where(d >= 0, gam[h] ** np.maximum(d, 0), 0.0) * 256 ** -0.5
    c["rmask"] = rm.reshape(4, 128, 5 * 512)
    c["gam"] = gam
    tp = np.arange(128)[:, None]
    tt = np.arange(128)[None, :]
    same = (tp // 64) == (tt // 64)
    L = np.where(same & (tp <= tt), -1.0 / 16.0, 0.0)
    U = np.where(same & (tp > tt), -1.0 / 16.0, 0.0)
    M2 = np.where(same & (tp <= tt), 1.0, 0.0)
    c["tri"] = np.concatenate([L, U, M2], axis=1).astype(np.float32)
    am = np.zeros((128, 9, 512), np.float32)
    for idx, dl in enumerate([-3, -2, -1, 0, 1, 2, 3, 4, 5]):
        d = (dl * 128 + ii - jj).astype(np.int64)
        cnt = ((d >= 0) & (d <= 128)).astype(np.float32) + ((d >= 0) & (d % 4 == 0) & (d <= 512)).astype(np.float32) \
            + ((d >= 0) & (d % 16 == 0)).astype(np.float32)
        am[:, idx, :] = cnt
    c["amask"] = am.reshape(128, 9 * 512)
    return c


_CONST = None


def _prep_shared(inp):
    global _CONST
    if _CONST is None:
        _CONST = _host_constants()
    c = _CONST
    sh = {"invf": c["invf"], "ident": c["ident"], "rmask": c["rmask"], "tri": c["tri"], "amask": c["amask"]}
    f = np.float32
    nws = [inp["attn_norm_w"][0], inp["ffn_norm_w"][0], inp["ple_norm_w"][0],
           inp["attn_norm_w"][1], inp["ffn_norm_w"][1], inp["ple_norm_w"][1], inp["final_norm_w"]]
    sh["nw"] = np.ascontiguousarray(np.stack([np.asarray(v, f).reshape(16, 128).T for v in nws], axis=1)).reshape(128, 7 * 16)
    w_in = np.asarray(inp["ab_w_in"][0], f)
    perm = np.concatenate([np.arange(0, 256, 2), np.arange(1, 256, 2)])
    cols = []
    for h in range(4):
        cols.append(w_in[:, 0 + h * 256 + perm])
        cols.append(w_in[:, 1024 + h * 256 + perm])
        cols.append(w_in[:, 2048 + h * 256: 2048 + (h + 1) * 256])
        cols.append(w_in[:, 3072 + h * 256: 3072 + (h + 1) * 256])
    cols.append(w_in[:, 4096:4608])
    cols.append(w_in[:, 4608:5120])
    cols.append(w_in[:, 5120:6144])
    cols.append(w_in[:, 6144:7168])
    sh["w0"] = _blk(np.concatenate(cols, axis=1), 512)
    sh["wglr"] = _blk(w_in[:, 7168:7184], 16).reshape(128, 256)
    sh["gu"] = np.concatenate([np.asarray(inp["ab_gla_gate_up"][0], f), np.asarray(inp["ab_gla_gate_b"][0], f)[None, :]], axis=0)
    rw = np.concatenate([np.asarray(inp["ab_ret_norm_w"][0], f), np.asarray(inp["ab_gla_norm_w"][0], f)])
    sh["retw"] = np.ascontiguousarray(np.broadcast_to(rw[None, :], (128, 2048)))
    sh["wo0"] = _blk(np.asarray(inp["ab_w_out"][0], f), 512)
    wq = np.asarray(inp["c_w_qkv"][0], f)
    cols = []
    for h in range(16):
        cols.append(wq[:, h * 128:(h + 1) * 128])
        cols.append(wq[:, 2048 + h * 128: 2048 + (h + 1) * 128])
        cols.append(wq[:, 4096 + h * 128: 4096 + (h + 1) * 128])
    sh["w1"] = _blk(np.concatenate(cols, axis=1), 384)
    sh["wo1"] = _blk(np.asarray(inp["c_w_out"][0], f), 512)
    for i in range(2):
        g = np.asarray(inp["ffn_w_gate"][i], f)
        u = np.asarray(inp["ffn_w_up"][i], f)
        gu = np.concatenate([g.reshape(D, 22, 256), u.reshape(D, 22, 256)], axis=2).reshape(D, 22 * 512)
        sh[f"wgu{i}"] = _blk(gu, 512)
        sh[f"wd{i}"] = _blk(np.asarray(inp["ffn_w_down"][i], f), 128)
        sh[f"wpg{i}"] = _blk(np.asarray(inp["ple_w_gate"][i], f), 512)
        sh[f"wpp{i}"] = _blk(np.asarray(inp["ple_w_proj"][i], f), 2048).reshape(128, 2 * 2048)
    return sh


SHARED_SHAPES = {
    "invf": [128, 1], "ident": [128, 128], "rmask": [4, 128, 2560], "tri": [128, 384], "amask": [128, 4608],
    "nw": [128, 112], "w0": [14, 128, 8192], "wglr": [128, 256], "gu": [17, 512], "retw": [128, 2048],
    "wo0": [4, 128, 8192], "w1": [16, 128, 6144], "wo1": [4, 128, 8192],
    "wgu0": [22, 128, 8192], "wd0": [16, 128, 5632], "wpg0": [4, 128, 8192], "wpp0": [128, 4096],
    "wgu1": [22, 128, 8192], "wd1": [16, 128, 5632], "wpg1": [4, 128, 8192], "wpp1": [128, 4096],
}


def build_program(phases=None, dbg=False):
    nc = bass.Bass("TRN2", target_bir_lowering=False)
    es = ExitStack()
    with es:
        dr = {}
        for k, shp in SHARED_SHAPES.items():
            dr[k] = nc.dram_tensor(k, list(shp), F32, kind="ExternalInput").ap()
        xT = nc.dram_tensor("xT", [D, T], F32, kind="ExternalInput").ap()
        pT = nc.dram_tensor("pT", [2, 256, T], F32, kind="ExternalInput").ap()
        posr = nc.dram_tensor("posr", [128, T], I32, kind="ExternalInput").ap()
        outT = nc.dram_tensor("outT", [D, T], F32, kind="ExternalOutput").ap()
        skind = "ExternalOutput" if dbg else "Internal"
        R = nc.dram_tensor("R", [D, T], F32, kind=skind).ap()
        MIXT = nc.dram_tensor("MIXT", [D, T], BF16, kind=skind).ap()
        WDB = [nc.dram_tensor(f"wdb{i}", [KC, 128, FF], BF16, kind="Internal").ap() for i in range(2)]
        wdcast_n = [0, 0]

        def wd_cast(layer, n):
            issued = 0
            for _ in range(n):
                c = wdcast_n[layer]
                if c >= KC:
                    break
                wdcast_n[layer] += 1
                issued += 1
                P.dma("pool", WDB[layer][c], dr[f"wd{layer}"][c], key=f"wdc{c % 4}")
            return issued

        P = Prog(nc, es)

        tctr = [0]

        def sb(shape, dt, stack=es):
            tctr[0] += 1
            return stack.enter_context(nc.sbuf_tensor(f"t{tctr[0]}", list(shape), dt))

        A = sb([128, KC, T], BF16)
        BA = [P.buf(f"A{tg}") for tg in range(NTG)]
        wslots = [sb([128, WSLOT], BF16) for _ in range(NWS)]
        Bw = [P.buf(f"w{i}") for i in range(NWS)]
        wctr = [0]
        nwt = sb([128, 112], F32)
        identb = sb([128, 128], BF16)
        onesf = sb([128, 128], F32)
        cbias = sb([128, 4], F32)
        Bconst = P.buf("const")
        psf = [es.enter_context(nc.psum_tensor(f"psf{i}", [128, 512], F32)) for i in range(7)]
        psb = [es.enter_context(nc.psum_tensor(f"psb{i}", [128, 1024], BF16)) for i in range(1)]
        Bpsf = [P.buf(f"psf{i}") for i in range(7)]
        Bpsb = [P.buf(f"psb{i}") for i in range(1)]
        pctr = [0, 0]
        nrot = [7]

        def next_ps():
            i = pctr[0] % nrot[0]
            pctr[0] += 1
            return psf[i], Bpsf[i]

        def next_psb():
            return psb[0], Bpsb[0]

        prefetched = {}
        pending_pf = []
        for i_ in range(NWS):
            P.persist_keys.add(f"w{i_}pool")
            P.persist_keys.add(f"w{i_}sp")
            P.persist_bufs.add(Bw[i_])

        def wload(src_ap, ncols, eng="pool", key=None):
            if key is not None and key in prefetched:
                return prefetched.pop(key)
            s = wctr[0] % NWS
            wctr[0] += 1
            P.dma(eng, wslots[s][:, 0:ncols], src_ap, writes=[Bw[s]], key=f"w{s}{eng}")
            return wslots[s], Bw[s]

        def do_prefetch():
            while pending_pf:
                key, ap, ncols, eng = pending_pf.pop(0)
                prefetched[key] = wload(ap, ncols, eng)

        def wstream(items):
            st = {"next": 0, "slots": {}}

            def get(i):
                while st["next"] <= min(i + 1, len(items) - 1):
                    n = st["next"]
                    st["slots"][n] = wload(*items[n])
                    st["next"] += 1
                return st["slots"].pop(i)
            return get

        P.dma("sp", nwt[:], dr["nw"], writes=[Bconst], key="c0")
        P.dma("pool", identb[:], dr["ident"], writes=[Bconst], key="c1")
        P.op("dve", lambda e: e.memset(onesf[:], 1.0), writes=[Bconst])
        P.op("dve", lambda e: e.memset(cbias[:, 0:1], math.pi), writes=[Bconst])
        P.op("dve", lambda e: e.memset(cbias[:, 1:2], 1.0), writes=[Bconst])
        P.op("dve", lambda e: e.memset(cbias[:, 2:3], EPS), writes=[Bconst])
        P.flush()

        def tsl(tg):
            return slice(tg * 512, (tg + 1) * 512)

        rstdg = [None] * NTG
        Brg = [P.buf(f"rstdg{i}") for i in range(NTG)]
        accg = [None] * NTG
        Bag = [P.buf(f"accg{i}") for i in range(NTG)]
        chain = [None]

        def chain_open():
            chain[0] = ExitStack()
            for i in range(NTG):
                rstdg[i] = sb([128, 512], F32, chain[0])

        def chain_close():
            chain[0].close()
            chain[0] = None

        def make_stats(ls):
            sq = [sb([128, 512], F32, ls) for _ in range(2)]
            Bsq = [P.buf() for _ in range(2)]
            for i in range(NTG):
                accg[i] = sb([128, 512], F32, ls)
            cn = [0]

            def stat(h_ap, Bh, c, tg, widx, first, write_a=True):
                q = cn[0] % 2
                cn[0] += 1
                if write_a:
                    wcol = nwt[:, widx * 16 + c: widx * 16 + c + 1]
                    P.op("act", lambda e, h_ap=h_ap, c=c, tg=tg, wcol=wcol: e.activation(out=A[:, c, tsl(tg)], in_=h_ap, func=AF.Copy, scale=wcol),
                         reads=[Bh, Bconst], writes=[BA[tg]])
                P.op("act", lambda e, h_ap=h_ap, q=q: e.activation(out=sq[q][:], in_=h_ap, func=AF.Square), reads=[Bh], writes=[Bsq[q]])
                if first:
                    P.op("pool", lambda e, q=q, tg=tg: e.tensor_copy(out=accg[tg][:], in_=sq[q][:]), reads=[Bsq[q]], writes=[Bag[tg]])
                else:
                    P.op("pool", lambda e, q=q, tg=tg: e.tensor_tensor(out=accg[tg][:], in0=accg[tg][:], in1=sq[q][:], op=ALU.add),
                         reads=[Bsq[q], Bag[tg]], writes=[Bag[tg]])
            return stat

        def finish_stats(tg):
            ps, Bp = next_ps()
            P.op("pe", lambda e, ps=ps, tg=tg: e.matmul(ps[:], lhsT=onesf[:], rhs=accg[tg][:], start=True, stop=True),
                 reads=[Bag[tg], Bconst], writes=[Bp])
            P.op("act", lambda e, ps=ps, tg=tg: e.activation(out=rstdg[tg][:], in_=ps[:], func=AF.Sqrt, scale=1.0 / D, bias=cbias[:, 2:3]),
                 reads=[Bp, Bconst], writes=[Brg[tg]])
            P.op("dve", lambda e, tg=tg: e.reciprocal(out=rstdg[tg][:], in_=rstdg[tg][:]), reads=[Brg[tg]], writes=[Brg[tg]])

        def phase_norm(src, widx, to_out=None, pre_rstd=False):
            nrot[0] = 7
            with ExitStack() as ls:
                xt = [sb([128, KC, 512], F32, ls) for _ in range(2)]
                Bxt = [P.buf() for _ in range(2)]
                sq = [sb([128, 512], F32, ls) for _ in range(2)]
                Bsq = [P.buf() for _ in range(2)]
                acc = [sb([128, 512], F32, ls) for _ in range(2)]
                Bacc = [P.buf() for _ in range(2)]
                rstd = [sb([128, 512], F32, ls) for _ in range(2)]
                Brs = [P.buf() for _ in range(2)]
                srcv = src.rearrange("(c p) t -> p c t", p=128)
                outv = to_out.rearrange("(c p) t -> p c t", p=128) if to_out is not None else None
                for tg in range(NTG):
                    s = tg % 2
                    P.dma("sp", xt[s][:], srcv[:, :, tsl(tg)], writes=[Bxt[s]], key=f"nx{s}")
                    rs_t, rs_B = (rstdg[tg], Brg[tg]) if pre_rstd else (rstd[s], Brs[s])
                    if not pre_rstd:
                        for c in range(KC):
                            q = c % 2
                            P.op("act", lambda e, s=s, c=c, q=q: e.activation(out=sq[q][:], in_=xt[s][:, c, :], func=AF.Square),
                                 reads=[Bxt[s]], writes=[Bsq[q]])
                            if c == 0:
                                P.op("pool", lambda e, s=s, q=q: e.tensor_copy(out=acc[s][:], in_=sq[q][:]),
                                     reads=[Bsq[q]], writes=[Bacc[s]])
                            else:
                                P.op("pool", lambda e, s=s, q=q: e.tensor_tensor(out=acc[s][:], in0=acc[s][:], in1=sq[q][:], op=ALU.add),
                                     reads=[Bsq[q], Bacc[s]], writes=[Bacc[s]])
                        ps, Bp = next_ps()
                        P.op("pe", lambda e, ps=ps, s=s: e.matmul(ps[:], lhsT=onesf[:], rhs=acc[s][:], start=True, stop=True),
                             reads=[Bacc[s], Bconst], writes=[Bp])
                        P.op("dve", lambda e, ps=ps, s=s: e.tensor_scalar(out=rstd[s][:], in0=ps[:], scalar1=1.0 / D, scalar2=EPS,
                                                                         op0=ALU.mult, op1=ALU.add),
                             reads=[Bp], writes=[Brs[s]])
                        P.op("act", lambda e, s=s: e.activation(out=rstd[s][:], in_=rstd[s][:], func=AF.Sqrt),
                             reads=[Brs[s]], writes=[Brs[s]])
                        P.op("dve", lambda e, s=s: e.reciprocal(out=rstd[s][:], in_=rstd[s][:]),
                             reads=[Brs[s]], writes=[Brs[s]])
                    for c in range(KC):
                        wcol = nwt[:, widx * 16 + c: widx * 16 + c + 1]
                        eng_ = "dve"
                        if to_out is None:
                            P.op(eng_, lambda e, s=s, c=c, tg=tg, wcol=wcol, rs_t=rs_t: e.scalar_tensor_tensor(
                                out=A[:, c, tsl(tg)], in0=xt[s][:, c, :], scalar=wcol, in1=rs_t[:],
                                op0=ALU.mult, op1=ALU.mult),
                                reads=[Bxt[s], rs_B, Bconst], writes=[BA[tg]])
                        else:
                            P.op(eng_, lambda e, s=s, c=c, wcol=wcol, rs_t=rs_t: e.scalar_tensor_tensor(
                                out=xt[s][:, c, :], in0=xt[s][:, c, :], scalar=wcol, in1=rs_t[:],
                                op0=ALU.mult, op1=ALU.mult),
                                reads=[Bxt[s], rs_B, Bconst], writes=[Bxt[s]])
                    if to_out is not None:
                        P.dma("act", outv[:, :, tsl(tg)], xt[s][:], reads=[Bxt[s]], key=f"no{s}")
                do_prefetch()
                P.flush()

        def fm_proj(wdram, nblocks, kc, evac, act=None, ncols=WSLOT, chunks_per_block=4, nbw=512, tgs=range(NTG), wkey=None):
            act = A if act is None else act
            ws = wstream([(wdram[b], ncols, "pool", f"{wkey}_{b}" if wkey else None) for b in range(nblocks)])
            for b in range(nblocks):
                slot, Bs = ws(b)
                sv = slot[:, 0:kc * nbw].rearrange("p (k n) -> p k n", k=kc)
                for j in range(chunks_per_block):
                    for tg in tgs:
                        ps, Bp = next_ps()
                        for k in range(kc):
                            P.op("pe", lambda e, ps=ps, sv=sv, k=k, j=j, tg=tg: e.matmul(
                                ps[:], lhsT=sv[:, k, j * 128:(j + 1) * 128], rhs=act[:, k, tsl(tg)],
                                start=(k == 0), stop=(k == kc - 1)),
                                reads=[Bs, BA[tg]], writes=[Bp])
                        evac(ps, Bp, b * chunks_per_block + j, tg)

        def phase_wout(wdram, res_src, widx_next, wkey):
            nrot[0] = 7
            with ExitStack() as ls:
                M = sb([128, KC, T], BF16, ls)
                BM = [P.buf() for _ in range(NTG)]
                mv = MIXT.rearrange("(c p) t -> p c t", p=128)
                for tg in range(NTG):
                    P.dma("sp", M[:, :, tsl(tg)], mv[:, :, tsl(tg)], writes=[BM[tg]], key=f"ml{tg}")
                xr = [sb([128, 512], F32, ls) for _ in range(3)]
                Bxr = [P.buf() for _ in range(3)]
                ho = [sb([128, 512], F32, ls) for _ in range(3)]
                Bho = [P.buf() for _ in range(3)]
                rv = res_src.rearrange("(c p) t -> p c t", p=128)
                Rv = R.rearrange("(c p) t -> p c t", p=128)
                stat = make_stats(ls)
                nxt = [None]
                ctr = [0]
                for b in range(4):
                    slot, Bs = nxt[0] if nxt[0] is not None else wload(wdram[b], WSLOT, "pool", f"{wkey}_{b}")
                    nxt[0] = wload(wdram[b + 1], WSLOT, "pool", f"{wkey}_{b + 1}") if b + 1 < 4 else None
                    sv = slot[:, :].rearrange("p (k n) -> p k n", k=KC)
                    for j in range(4):
                        c = b * 4 + j
                        for tg in range(NTG):
                            s_ = ctr[0] % 3
                            ctr[0] += 1
                            P.dma("sp", xr[s_][:], rv[:, c, tsl(tg)], writes=[Bxr[s_]], key=f"xr{s_}")
                            ps, Bp = next_ps()
                            for k in range(KC):
                                P.op("pe", lambda e, ps=ps, sv=sv, k=k, j=j, tg=tg: e.matmul(
                                    ps[:], lhsT=sv[:, k, j * 128:(j + 1) * 128], rhs=M[:, k, tsl(tg)],
                                    start=(k == 0), stop=(k == KC - 1)), reads=[Bs, BM[tg]], writes=[Bp])
                            P.op("dve", lambda e, ps=ps, s_=s_: e.tensor_tensor(out=ho[s_][:], in0=ps[:], in1=xr[s_][:], op=ALU.add),
                                 reads=[Bp, Bxr[s_]], writes=[Bho[s_]])
                            stat(ho[s_][:], Bho[s_], c, tg, widx_next, c == 0)
                            P.dma("act", Rv[:, c, tsl(tg)], ho[s_][:], reads=[Bho[s_]], key=f"ho{s_}")
                for tg in range(NTG):
                    finish_stats(tg)
                do_prefetch()
                P.flush()

        def phase_ffn(wgu, wd, widx_next, wkey):
            nrot[0] = 7
            with ExitStack() as ls:
                actT = sb([128, FKC, 512], BF16, ls)
                Bact = P.buf()
                sl = [sb([128, 512], F32, ls) for _ in range(2)]
                Bsl = [P.buf() for _ in range(2)]
                ul = [sb([128, 512], F32, ls) for _ in range(2)]
                Bul = [P.buf() for _ in range(2)]
                stat = make_stats(ls)
                hr = [sb([128, 512], F32, ls) for _ in range(3)]
                Bhr = [P.buf() for _ in range(3)]
                hn = [sb([128, 512], F32, ls) for _ in range(3)]
                Bhn = [P.buf() for _ in range(3)]
                Rv = R.rearrange("(c p) t -> p c t", p=128)
                cnt = [0]
                items = []
                for tg in range(NTG):
                    items += [(wgu[b], WSLOT, "pool", f"{wkey}_{b}" if (tg == 0 and b < 2) else None) for b in range(22)] + [(wd[c], FF, "sp") for c in range(KC)]
                ws = wstream(items)
                for tg in range(NTG):
                    P.dma("sp", hr[0][:], Rv[:, 0, tsl(tg)], writes=[Bhr[0]], key="hr0")
                    for b in range(22):
                        slot, Bs = ws(tg * 38 + b)
                        sv = slot[:, :].rearrange("p (k n) -> p k n", k=KC)
                        for j in range(2):
                            pg, Bg = next_ps()
                            for k in range(KC):
                                P.op("pe", lambda e, pg=pg, sv=sv, k=k, j=j, tg=tg: e.matmul(
                                    pg[:], lhsT=sv[:, k, j * 128:(j + 1) * 128], rhs=A[:, k, tsl(tg)],
                                    start=(k == 0), stop=(k == KC - 1)), reads=[Bs, BA[tg]], writes=[Bg])
                            pu, Bu = next_ps()
                            for k in range(KC):
                                P.op("pe", lambda e, pu=pu, sv=sv, k=k, j=j, tg=tg: e.matmul(
                                    pu[:], lhsT=sv[:, k, 256 + j * 128:256 + (j + 1) * 128], rhs=A[:, k, tsl(tg)],
                                    start=(k == 0), stop=(k == KC - 1)), reads=[Bs, BA[tg]], writes=[Bu])
                            q = cnt[0] % 2
                            cnt[0] += 1
                            P.op("dve", lambda e, pg=pg, q=q, tg=tg: e.tensor_tensor(out=sl[q][:], in0=pg[:], in1=rstdg[tg][:], op=ALU.mult),
                                 reads=[Bg, Brg[tg]], writes=[Bsl[q]])
                            P.op("act", lambda e, q=q: e.activation(out=sl[q][:], in_=sl[q][:], func=AF.Silu),
                                 reads=[Bsl[q]], writes=[Bsl[q]])
                            P.op("dve", lambda e, pu=pu, q=q, tg=tg: e.tensor_tensor(out=ul[q][:], in0=pu[:], in1=rstdg[tg][:], op=ALU.mult),
                                 reads=[Bu, Brg[tg]], writes=[Bul[q]])
                            P.op("dve", lambda e, q=q, b=b, j=j: e.tensor_tensor(
                                out=actT[:, b * 2 + j, :], in0=sl[q][:], in1=ul[q][:], op=ALU.mult),
                                reads=[Bul[q], Bsl[q]], writes=[Bact])
                    for c in range(KC):
                        slot, Bs = ws(tg * 38 + 22 + c)
                        sv = slot[:, 0:FF].rearrange("p (k n) -> p k n", k=FKC)
                        s = c % 3
                        if c + 1 < KC:
                            s1 = (c + 1) % 3
                            P.dma("sp", hr[s1][:], Rv[:, c + 1, tsl(tg)], writes=[Bhr[s1]], key=f"hr{s1}")
                        ps, Bp = next_ps()
                        for k in range(FKC):
                            P.op("pe", lambda e, ps=ps, sv=sv, k=k: e.matmul(
                                ps[:], lhsT=sv[:, k, :], rhs=actT[:, k, :], start=(k == 0), stop=(k == FKC - 1)),
                                reads=[Bs, Bact], writes=[Bp])
                        P.op("dve", lambda e, ps=ps, s=s: e.tensor_tensor(out=hn[s][:], in0=ps[:], in1=hr[s][:], op=ALU.add),
                             reads=[Bp, Bhr[s]], writes=[Bhn[s]])
                        stat(hn[s][:], Bhn[s], c, tg, widx_next, c == 0)
                        P.dma("act", Rv[:, c, tsl(tg)], hn[s][:], reads=[Bhn[s]], key=f"hn{s}")
                    finish_stats(tg)
                do_prefetch()
                P.flush()

        def phase_ple(wpg, wpp, pTl, wkey):
            nrot[0] = 7
            with ExitStack() as ls:
                ppb = sb([128, 2, T], BF16, ls)
                wppb = sb([128, 2, T], BF16, ls)
                Bpp = P.buf()
                P.dma("pool", ppb[:], pTl.rearrange("(k p) t -> p k t", p=128), writes=[Bpp], key="pp0")
                P.dma("pool", wppb[:], wpp.rearrange("p (k n) -> p k n", k=2), writes=[Bpp], key="pp1")
                xr = [sb([128, T], F32, ls) for _ in range(2)]
                Bxr = [P.buf() for _ in range(2)]
                ho = [sb([128, T], F32, ls) for _ in range(2)]
                Bho = [P.buf() for _ in range(2)]
                gt = [sb([128, 512], F32, ls) for _ in range(2)]
                Bgt = [P.buf() for _ in range(2)]
                Rv = R.rearrange("(c p) t -> p c t", p=128)
                cnt = [0]
                stat = make_stats(ls)

                def evac(ps, Bp, c, tg):
                    s = c % 2
                    if tg == 0:
                        P.dma("sp", xr[s][:], Rv[:, c, :], writes=[Bxr[s]], key=f"xr{s}")
                    q = cnt[0] % 2
                    cnt[0] += 1
                    P.op("dve", lambda e, ps=ps, q=q, tg=tg: e.tensor_tensor(out=gt[q][:], in0=ps[:], in1=rstdg[tg][:], op=ALU.mult),
                         reads=[Bp, Brg[tg]], writes=[Bgt[q]])
                    P.op("act", lambda e, q=q: e.activation(out=gt[q][:], in_=gt[q][:], func=AF.Sigmoid),
                         reads=[Bgt[q]], writes=[Bgt[q]])
                    p2, Bp2 = next_ps()
                    for k in range(2):
                        P.op("pe", lambda e, p2=p2, k=k, c=c, tg=tg: e.matmul(
                            p2[:], lhsT=wppb[:, k, c * 128:(c + 1) * 128], rhs=ppb[:, k, tsl(tg)],
                            start=(k == 0), stop=(k == 1)), reads=[Bpp], writes=[Bp2])
                    P.op("dve", lambda e, p2=p2, q=q: e.tensor_tensor(out=gt[q][:], in0=gt[q][:], in1=p2[:], op=ALU.mult),
                         reads=[Bp2, Bgt[q]], writes=[Bgt[q]])
                    P.op("pool", lambda e, q=q, s=s, tg=tg: e.tensor_tensor(out=ho[s][:, tsl(tg)], in0=gt[q][:], in1=xr[s][:, tsl(tg)], op=ALU.add),
                         reads=[Bgt[q], Bxr[s]], writes=[Bho[s]])
                    stat(ho[s][:, tsl(tg)], Bho[s], c, tg, None, c == 0, write_a=False)
                    if tg == NTG - 1:
                        P.dma("act", Rv[:, c, :], ho[s][:], reads=[Bho[s]], key=f"ho{s}")
                fm_proj(wpg, 4, KC, evac, wkey=wkey)
                for tg in range(NTG):
                    finish_stats(tg)
                do_prefetch()
                P.flush()

        def make_finisher(ls):
            osb = [sb([128, 256], F32, ls) for _ in range(3)]
            Bosb = [P.buf() for _ in range(3)]
            sqt = [sb([128, 256], F32, ls) for _ in range(2)]
            Bsqt = [P.buf() for _ in range(2)]
            ss = [sb([128, 1], F32, ls) for _ in range(3)]
            Bss = [P.buf() for _ in range(3)]
            yb = [sb([128, 256], BF16, ls) for _ in range(3)]
            Byb = [P.buf() for _ in range(3)]
            Bquad = [Bpsb[0]] * 4
            cnt = [0, 0]

            def finish(O_ap, Bo, sg_ap, Bsg, dst_ap, Bdst):
                q = cnt[0] % 3
                q2 = cnt[0] % 2
                cnt[0] += 1
                P.op("pool", lambda e, q=q: e.memset(ss[q][:], 0.0), writes=[Bss[q]])
                P.op("act", lambda e, q=q: e.activation(out=osb[q][:], in_=O_ap, func=AF.Copy), reads=[Bo], writes=[Bosb[q]])
                P.op("act", lambda e, q=q, q2=q2: e.activation(out=sqt[q2][:], in_=osb[q][:], func=AF.Square, accum_out=ss[q][:]),
                     reads=[Bosb[q], Bss[q]], writes=[Bsqt[q2], Bss[q]])
                P.op("act", lambda e, q=q: e.activation(out=ss[q][:], in_=ss[q][:], func=AF.Sqrt, scale=1.0 / 256.0, bias=cbias[:, 2:3]),
                     reads=[Bss[q], Bconst], writes=[Bss[q]])

                def stage_a2(q=q):
                    P.op("dve", lambda e, q=q: e.reciprocal(out=ss[q][:], in_=ss[q][:]),
                         reads=[Bss[q]], writes=[Bss[q]])
                    P.op("dve", lambda e, q=q: e.scalar_tensor_tensor(out=yb[q][:], in0=osb[q][:], scalar=ss[q][:, 0:1], in1=sg_ap,
                                                                     op0=ALU.mult, op1=ALU.mult),
                         reads=[Bosb[q], Bss[q], Bsg], writes=[Byb[q]])

                def stage_b(q=q):
                    r = cnt[1] % 4
                    cnt[1] += 1
                    pt = psb[0]
                    for j in range(2):
                        P.op("pe", lambda e, q=q, j=j, r=r: e.transpose(out=pt[:, r * 256 + j * 128:r * 256 + (j + 1) * 128],
                                                                        in_=yb[q][:, j * 128:(j + 1) * 128], identity=identb[:]),
                             reads=[Byb[q], Bconst], writes=[Bquad[r]])
                    P.op("act", lambda e, r=r: e.activation(out=dst_ap, in_=pt[:, r * 256:(r + 1) * 256].rearrange("p (j t) -> p j t", j=2), func=AF.Copy),
                         reads=[Bquad[r]], writes=[Bdst])
                return stage_a2, stage_b
            return finish

        def phase_ret():
            nrot[0] = 3
            gam = _CONST["gam"]
            with ExitStack() as ls:
                MV = MIXT.rearrange("(c p) t -> p c t", p=128)
                cosT = sb([128, T], F32, ls)
                sinT = sb([128, T], F32, ls)
                Bcs = P.buf()
                with ExitStack() as l2:
                    posi = sb([128, T], I32, l2)
                    ang = sb([128, T], F32, l2)
                    invf = sb([128, 1], F32, l2)
                    Bt = P.buf()
                    P.dma("sp", posi[:], posr, writes=[Bt], key="pos0")
                    P.dma("sp", invf[:], dr["invf"], writes=[Bt], key="pos1")
                    P.op("dve", lambda e: e.tensor_copy(out=ang[:], in_=posi[:]), reads=[Bt], writes=[Bt])
                    P.op("dve", lambda e: e.tensor_scalar(out=ang[:], in0=ang[:], scalar1=invf[:, 0:1], scalar2=None, op0=ALU.mult),
                         reads=[Bt], writes=[Bt])
                    ni = sb([128, T], I32, l2)
                    nf = sb([128, T], F32, l2)
                    C1 = 6.28125
                    C2 = 2.0 * math.pi - C1
                    for dst, shift in ((sinT, 0.0), (cosT, 0.5 * math.pi)):
                        if shift != 0.0:
                            P.op("dve", lambda e, shift=shift: e.tensor_scalar(out=ang[:], in0=ang[:], scalar1=shift, scalar2=None, op0=ALU.add),
                                 reads=[Bt], writes=[Bt])
                        P.op("dve", lambda e: e.tensor_scalar(out=nf[:], in0=ang[:], scalar1=1.0 / (2.0 * math.pi), scalar2=None, op0=ALU.mult),
                             reads=[Bt], writes=[Bt])
                        P.op("dve", lambda e: e.tensor_copy(out=ni[:], in_=nf[:]), reads=[Bt], writes=[Bt])
                        P.op("dve", lambda e: e.tensor_copy(out=nf[:], in_=ni[:]), reads=[Bt], writes=[Bt])
                        P.op("dve", lambda e, dst=dst: e.scalar_tensor_tensor(out=dst[:], in0=nf[:], scalar=-C1, in1=ang[:], op0=ALU.mult, op1=ALU.add),
                             reads=[Bt], writes=[Bcs])
                        P.op("dve", lambda e, dst=dst: e.scalar_tensor_tensor(out=dst[:], in0=nf[:], scalar=-C2, in1=dst[:], op0=ALU.mult, op1=ALU.add),
                             reads=[Bt, Bcs], writes=[Bcs])
                        P.op("dve", lambda e, dst=dst: e.tensor_single_scalar(out=nf[:], in_=dst[:], scalar=math.pi, op=ALU.is_gt),
                             reads=[Bcs], writes=[Bt])
                        P.op("dve", lambda e, dst=dst: e.scalar_tensor_tensor(out=dst[:], in0=nf[:], scalar=-2.0 * math.pi, in1=dst[:], op0=ALU.mult, op1=ALU.add),
                             reads=[Bt, Bcs], writes=[Bcs])
                        P.op("dve", lambda e, dst=dst: e.tensor_single_scalar(out=nf[:], in_=dst[:], scalar=-math.pi, op=ALU.is_lt),
                             reads=[Bcs], writes=[Bt])
                        P.op("dve", lambda e, dst=dst: e.scalar_tensor_tensor(out=dst[:], in0=nf[:], scalar=2.0 * math.pi, in1=dst[:], op0=ALU.mult, op1=ALU.add),
                             reads=[Bt, Bcs], writes=[Bcs])
                        P.op("dve", lambda e, dst=dst: e.tensor_scalar(out=dst[:], in0=dst[:], scalar1=-3.1415925, scalar2=3.1415925, op0=ALU.max, op1=ALU.min),
                             reads=[Bcs], writes=[Bcs])
                        P.op("act", lambda e, dst=dst: e.activation(out=dst[:], in_=dst[:], func=AF.Sin), reads=[Bcs], writes=[Bcs])
                    P.flush()
                retw = sb([128, 1024], F32, ls)
                P.dma("sp", retw[:], dr["retw"][:, 0:1024], writes=[Bcs], key="pos2")
                rm = sb([128, 5, 512], F32, ls)
                Brm = P.buf()
                qT = sb([128, 2, T], BF16, ls)
                kT = sb([128, 2, T], BF16, ls)
                Bq = P.buf()
                Bk = P.buf()
                vtm = sb([128, NTT, 256], BF16, ls)
                Bv = P.buf()
                sg = sb([128, NTT, 256], F32, ls)
                Bsg = P.buf()
                ev = [sb([128, 512], F32, ls) for _ in range(2)]
                od = [sb([128, 512], F32, ls) for _ in range(2)]
                Bev = [P.buf() for _ in range(2)]
                Bod = [P.buf() for _ in range(2)]
                tmp = [sb([128, 512], F32, ls) for _ in range(4)]
                Btmp = [P.buf() for _ in range(4)]
                sgt = [sb([128, 256], F32, ls) for _ in range(2)]
                Bsgt = [P.buf() for _ in range(2)]
                pt = [sb([128, 512], BF16, ls) for _ in range(3)]
                Bptt = [P.buf() for _ in range(3)]
                yT = sb([128, 2, T], BF16, ls)
                ByT = P.buf()
                finish = make_finisher(ls)
                rc = [0, 0]
                import os
                RST = int(os.environ.get("RET_STAGE", "9"))
                NH = int(os.environ.get("RET_HEADS", "4"))
                for h in range(NH if RST > 0 else 0):
                    P.dma("sp", rm[:], dr["rmask"][h].rearrange("p (m n) -> p m n", m=5), writes=[Brm], key="rm")
                    if h == 0:
                        nxt_fm = wload(dr["w0"][0], WSLOT, "pool", "w0_0")
                        nxt_tm = wload(dr["w0"][1], WSLOT, "pool", "w0_1")
                    slot, Bs = nxt_fm
                    sv = slot[:, :].rearrange("p (k n) -> p k n", k=KC)
                    for qk in range(2):
                        dst, Bd = (qT, Bq) if qk == 0 else (kT, Bk)
                        for tg in range(NTG):
                            s = rc[0] % 2
                            rc[0] += 1
                            for eo in range(2):
                                ps, Bp = next_ps()
                                j = qk * 2 + eo
                                for k in range(KC):
                                    P.op("pe", lambda e, ps=ps, sv=sv, k=k, j=j, tg=tg: e.matmul(
                                        ps[:], lhsT=sv[:, k, j * 128:(j + 1) * 128], rhs=A[:, k, tsl(tg)],
                                        start=(k == 0), stop=(k == KC - 1)), reads=[Bs, BA[tg]], writes=[Bp])
                                tgt, Btg = (ev[s], Bev[s]) if eo == 0 else (od[s], Bod[s])
                                P.op("act", lambda e, ps=ps, tgt=tgt: e.activation(out=tgt[:], in_=ps[:], func=AF.Copy),
                                     reads=[Bp], writes=[Btg])
                            E_, O_ = ev[s], od[s]
                            cs, sn = cosT[:, tsl(tg)], sinT[:, tsl(tg)]
                            P.op("dve", lambda e, E_=E_, cs=cs: e.tensor_tensor(out=tmp[0][:], in0=E_[:], in1=cs, op=ALU.mult),
                                 reads=[Bev[s], Bcs], writes=[Btmp[0]])
                            P.op("pool", lambda e, O_=O_, sn=sn: e.tensor_tensor(out=tmp[1][:], in0=O_[:], in1=sn, op=ALU.mult),
                                 reads=[Bod[s], Bcs], writes=[Btmp[1]])
                            P.op("dve", lambda e, dst=dst, tg=tg: e.tensor_tensor(out=dst[:, 0, tsl(tg)], in0=tmp[0][:], in1=tmp[1][:], op=ALU.subtract),
                                 reads=[Btmp[0], Btmp[1]], writes=[Bd])
                            P.op("pool", lambda e, E_=E_, sn=sn: e.tensor_tensor(out=tmp[2][:], in0=E_[:], in1=sn, op=ALU.mult),
                                 reads=[Bev[s], Bcs], writes=[Btmp[2]])
                            P.op("dve", lambda e, O_=O_, cs=cs: e.tensor_tensor(out=tmp[3][:], in0=O_[:], in1=cs, op=ALU.mult),
                                 reads=[Bod[s], Bcs], writes=[Btmp[3]])
                            P.op("pool", lambda e, dst=dst, tg=tg: e.tensor_tensor(out=dst[:, 1, tsl(tg)], in0=tmp[2][:], in1=tmp[3][:], op=ALU.add),
                                 reads=[Btmp[2], Btmp[3]], writes=[Bd])
                    if RST < 2:
                        continue
                    if h + 1 < NH:
                        nxt_fm = wload(dr["w0"][2 * h + 2], WSLOT)
                    wd_cast(0, 2)
                    slot, Bs = nxt_tm
                    sv = slot[:, :].rearrange("p (k n) -> p k n", k=KC)
                    for tt in range(NTT):
                        ps, Bp = next_ps()
                        tg = tt // 4
                        for k in range(KC):
                            P.op("pe", lambda e, ps=ps, sv=sv, k=k, tt=tt: e.matmul(
                                ps[:], lhsT=A[:, k, tt * 128:(tt + 1) * 128], rhs=sv[:, k, :],
                                start=(k == 0), stop=(k == KC - 1)), reads=[Bs, BA[tg]], writes=[Bp])
                        P.op("act", lambda e, ps=ps, tt=tt: e.activation(out=vtm[:, tt, :], in_=ps[:, 0:256], func=AF.Copy),
                             reads=[Bp], writes=[Bv])
                        q = tt % 2
                        P.op("act", lambda e, ps=ps, q=q: e.activation(out=sgt[q][:], in_=ps[:, 256:512], func=AF.Silu),
                             reads=[Bp], writes=[Bsgt[q]])
                        P.op("pool", lambda e, q=q, tt=tt, h=h: e.tensor_tensor(out=sg[:, tt, :], in0=sgt[q][:], in1=retw[:, h * 256:(h + 1) * 256], op=ALU.mult),
                             reads=[Bsgt[q], Bcs], writes=[Bsg])
                    if RST < 3:
                        continue
                    if h + 1 < NH:
                        nxt_tm = wload(dr["w0"][2 * h + 3], WSLOT)
                    wd_cast(0, 2)
                    LA = 2
                    blocks = [(G, J) for G in range(4) for J in range(4 * G + 4)]
                    Sps = {}
                    issued = 0
                    pending = []
                    Ot = [(psf[3 + i], Bpsf[3 + i]) for i in range(4)]
                    for bi, (G, J) in enumerate(blocks):
                        while issued < min(bi + 1 + LA, len(blocks)):
                            G2, J2 = blocks[issued]
                            ps2, Bp2 = next_ps()
                            q02 = max(0, J2 - 4 * G2) * 128
                            for eo in range(2):
                                P.op("pe", lambda e, ps2=ps2, eo=eo, J2=J2, G2=G2, q02=q02: e.matmul(
                                    ps2[:, q02:512], lhsT=kT[:, eo, J2 * 128:(J2 + 1) * 128], rhs=qT[:, eo, G2 * 512 + q02:(G2 + 1) * 512],
                                    start=(eo == 0), stop=(eo == 1)), reads=[Bq, Bk], writes=[Bp2])
                            Sps[issued] = (ps2, Bp2)
                            issued += 1
                        ps, Bp = Sps.pop(bi)
                        if J < 4 * G:
                            scal = float(gam[h] ** ((4 * G - J) * 128))
                            mk = rm[:, 0, :]
                        else:
                            scal = 1.0
                            mk = rm[:, 1 + (J - 4 * G), :]
                        s = rc[1] % 3
                        rc[1] += 1
                        q0 = max(0, J - 4 * G) * 128
                        P.op("dve", lambda e, ps=ps, s=s, scal=scal, mk=mk, q0=q0: e.scalar_tensor_tensor(
                            out=pt[s][:, q0:512], in0=ps[:, q0:512], scalar=scal, in1=mk[:, q0:512], op0=ALU.mult, op1=ALU.mult),
                            reads=[Bp, Brm], writes=[Bptt[s]])
                        for i in range(4):
                            if 4 * G + i >= J:
                                ob, Bob = Ot[i]
                                P.op("pe", lambda e, ob=ob, s=s, i=i, J=J, G=G: e.matmul(
                                    ob[:, 0:256], lhsT=pt[s][:, i * 128:(i + 1) * 128], rhs=vtm[:, J, :],
                                    start=(J == 0), stop=(J == 4 * G + i)), reads=[Bptt[s], Bv], writes=[Bob])
                        pending.sort(key=lambda t: (t[0], t[1]))
                        while pending and pending[0][0] <= bi:
                            pending.pop(0)[2]()
                        if J >= 4 * G:
                            i = J - 4 * G
                            ob, Bob = Ot[i]
                            tt = 4 * G + i
                            sta2, stb = finish(ob[:, 0:256], Bob, sg[:, tt, :], Bsg, yT[:, :, tt * 128:(tt + 1) * 128], ByT)
                            pending.append((bi + 1, 2 * bi, sta2))
                            pending.append((bi + 3, 2 * bi + 1, stb))
                    pending.sort(key=lambda t: (t[0], t[1]))
                    while pending:
                        pending.pop(0)[2]()
                    P.dma("sp", MV[:, 2 * h:2 * h + 2, :], yT[:], reads=[ByT], key="yT")
                do_prefetch()
                P.flush()

        def phase_gla():
            nrot[0] = 3
            with ExitStack() as ls:
                MV = MIXT.rearrange("(c p) t -> p c t", p=128)
                Bg = P.buf()
                retw = sb([128, 1024], F32, ls)
                P.dma("sp", retw[:], dr["retw"][:, 1024:2048], writes=[Bg], key="g0a")
                tri = sb([128, 3, 128], F32, ls)
                P.dma("sp", tri[:], dr["tri"].rearrange("p (m n) -> p m n", m=3), writes=[Bg], key="g0b")
                GU = sb([32, 512], F32, ls)
                P.dma("sp", GU[0:17, :], dr["gu"], writes=[Bg], key="g0c")
                wg = sb([128, KC, 16], BF16, ls)
                P.dma("pool", wg[:], dr["wglr"].rearrange("p (k n) -> p k n", k=KC), writes=[Bg], key="g1")
                glrT = sb([32, T], F32, ls)
                Bglr = P.buf()
                P.op("dve", lambda e: e.memset(glrT[:], 1.0), writes=[Bglr])
                for tg in range(NTG):
                    ps, Bp = next_ps()
                    for k in range(KC):
                        P.op("pe", lambda e, ps=ps, k=k, tg=tg: e.matmul(ps[0:16, :], lhsT=wg[:, k, :], rhs=A[:, k, tsl(tg)],
                                                                       start=(k == 0), stop=(k == KC - 1)),
                             reads=[Bg, BA[tg]], writes=[Bp])
                    P.op("act", lambda e, ps=ps, tg=tg: e.activation(out=glrT[0:16, tsl(tg)], in_=ps[0:16, :], func=AF.Copy),
                         reads=[Bp], writes=[Bglr])
                L_ = tri[:, 0, :]
                U_ = tri[:, 1, :]
                M2 = tri[:, 2, :]
                sp_tm = sb([128, 4, 512], F32, ls)
                Bsp = P.buf()
                ext = [sb([128, 512], F32, ls) for _ in range(2)]
                Bext = [P.buf() for _ in range(2)]

                epos = sb([128, 4, 512], F32, ls)
                Bep = P.buf()
                qfull = sb([128, 4, 512], BF16, ls)
                Bqf = P.buf()
                qa = sb([128, 4, 4, 128], BF16, ls)
                qb = sb([128, 4, 4, 128], BF16, ls)
                Bqa = P.buf()
                ktl = sb([128, 4, 512], BF16, ls)
                Bkt = P.buf()
                kk = sb([128, 4, 512], BF16, ls)
                Bkk = P.buf()
                vtm = sb([128, 4, 1024], BF16, ls)
                Bv = P.buf()
                sgg = sb([128, 4, 1024], BF16, ls)
                Bsgg = P.buf()
                St = sb([128, 4, 256], F32, ls)
                BSt = [P.buf() for _ in range(4)]
                Sbf = [sb([128, 4, 256], BF16, ls) for _ in range(2)]
                BSbf = [[P.buf() for _ in range(4)] for _ in range(2)]
                ptile = [sb([128, 128], BF16, ls) for _ in range(4)]
                Bpti = [P.buf() for _ in range(4)]
                yTg = sb([128, 8, 512], BF16, ls)
                ByT = P.buf()
                finish = make_finisher(ls)
                P.op("pool", lambda e: e.memset(qa[:], 0.0), writes=[Bqa])
                P.op("pool", lambda e: e.memset(qb[:], 0.0), writes=[Bqa])
                P.op("pool", lambda e: e.memset(St[:], 0.0), writes=BSt)
                P.op("pool", lambda e: e.memset(Sbf[0][:], 0.0), writes=BSbf[0])
                gws = wstream([(dr["w0"][8 + i], WSLOT, "pool", f"w0_{8 + i}" if (t_ == 0 and i < 2) else None) for t_ in range(NTG) for i in range(6)])
                sbi = [0, 0, 0, 0]
                gpend = []
                cnt = [0, 0]
                for tg in range(NTG):
                    for u in range(4):
                        tt = 4 * tg + u
                        ps, Bp = next_ps()
                        P.op("pe", lambda e, ps=ps, tt=tt: e.matmul(ps[:], lhsT=glrT[0:17, tt * 128:(tt + 1) * 128], rhs=GU[0:17, :],
                                                                   start=True, stop=True), reads=[Bglr, Bg], writes=[Bp])
                        q = cnt[0] % 2
                        cnt[0] += 1
                        P.op("act", lambda e, ps=ps, q=q: e.activation(out=ext[q][:], in_=ps[:], func=AF.Exp, scale=-1.0),
                             reads=[Bp], writes=[Bext[q]])
                        P.op("act", lambda e, q=q, u=u: e.activation(out=sp_tm[:, u, :], in_=ext[q][:], func=AF.Ln, bias=cbias[:, 1:2]),
                             reads=[Bext[q], Bconst], writes=[Bsp])
                    for h in range(4):
                        ps, Bp = next_ps()
                        for u in range(4):
                            P.op("pe", lambda e, ps=ps, u=u, h=h: e.matmul(ps[:, u * 128:(u + 1) * 128], lhsT=sp_tm[:, u, h * 128:(h + 1) * 128],
                                                                          rhs=L_, start=True, stop=True), reads=[Bsp, Bg], writes=[Bp])
                        P.op("act", lambda e, ps=ps, h=h: e.activation(out=epos[:, h, :], in_=ps[:], func=AF.Exp), reads=[Bp], writes=[Bep])
                    slot, Bs = gws(tg * 6 + 0)
                    sv = slot[:, :].rearrange("p (k n) -> p k n", k=KC)
                    for h in range(4):
                        ps, Bp = next_ps()
                        for k in range(KC):
                            P.op("pe", lambda e, ps=ps, sv=sv, k=k, h=h, tg=tg: e.matmul(
                                ps[:], lhsT=sv[:, k, h * 128:(h + 1) * 128], rhs=A[:, k, tsl(tg)], start=(k == 0), stop=(k == KC - 1)),
                                reads=[Bs, BA[tg]], writes=[Bp])
                        P.op("dve", lambda e, ps=ps, h=h: e.scalar_tensor_tensor(out=qfull[:, h, :], in0=ps[:], scalar=128 ** -0.5, in1=epos[:, h, :],
                                                                                op0=ALU.mult, op1=ALU.mult), reads=[Bp, Bep], writes=[Bqf])
                        qv = qfull[:, h, :].rearrange("p (u t) -> p u t", u=4)
                        P.op("pool", lambda e, qv=qv, h=h: e.tensor_copy(out=qa[:, h, :, 0:64], in_=qv[:, :, 0:64]), reads=[Bqf], writes=[Bqa])
                        P.op("pool", lambda e, qv=qv, h=h: e.tensor_copy(out=qb[:, h, :, 64:128], in_=qv[:, :, 64:128]), reads=[Bqf], writes=[Bqa])
                    slotk, Bsk = gws(tg * 6 + 1)
                    svk = slotk[:, :].rearrange("p (k n) -> p k n", k=KC)
                    for h in range(4):
                        ps, Bp = next_ps()
                        for k in range(KC):
                            P.op("pe", lambda e, ps=ps, svk=svk, k=k, h=h, tg=tg: e.matmul(
                                ps[:], lhsT=svk[:, k, h * 128:(h + 1) * 128], rhs=A[:, k, tsl(tg)], start=(k == 0), stop=(k == KC - 1)),
                                reads=[Bsk, BA[tg]], writes=[Bp])
                        q = cnt[0] % 2
                        cnt[0] += 1
                        P.op("dve", lambda e, q=q, h=h: e.reciprocal(out=ext[q][:], in_=epos[:, h, :]), reads=[Bep], writes=[Bext[q]])
                        P.op("dve", lambda e, ps=ps, h=h, q=q: e.tensor_tensor(out=ktl[:, h, :], in0=ps[:], in1=ext[q][:], op=ALU.mult),
                             reads=[Bp, Bext[q]], writes=[Bkt])
                    for u in range(4):
                        tt = 4 * tg + u
                        ps, Bp = next_ps()
                        for k in range(KC):
                            P.op("pe", lambda e, ps=ps, svk=svk, k=k, tt=tt: e.matmul(
                                ps[:], lhsT=A[:, k, tt * 128:(tt + 1) * 128], rhs=svk[:, k, :], start=(k == 0), stop=(k == KC - 1)),
                                reads=[Bsk, BA[tg]], writes=[Bp])
                        ps2, Bp2 = next_ps()
                        P.op("pe", lambda e, ps2=ps2, u=u: e.matmul(ps2[:], lhsT=U_, rhs=sp_tm[:, u, :], start=True, stop=True),
                             reads=[Bsp, Bg], writes=[Bp2])
                        q = cnt[0] % 2
                        cnt[0] += 1
                        P.op("act", lambda e, ps2=ps2, q=q: e.activation(out=ext[q][:], in_=ps2[:], func=AF.Exp), reads=[Bp2], writes=[Bext[q]])
                        P.op("dve", lambda e, ps=ps, q=q, u=u: e.tensor_tensor(out=kk[:, u, :], in0=ps[:], in1=ext[q][:], op=ALU.mult),
                             reads=[Bp, Bext[q]], writes=[Bkk])
                    for half in range(2):
                        slot, Bs = gws(tg * 6 + 2 + half)
                        sv = slot[:, :].rearrange("p (k n) -> p k n", k=KC)
                        for u in range(4):
                            tt = 4 * tg + u
                            ps, Bp = next_ps()
                            for k in range(KC):
                                P.op("pe", lambda e, ps=ps, sv=sv, k=k, tt=tt: e.matmul(
                                    ps[:], lhsT=A[:, k, tt * 128:(tt + 1) * 128], rhs=sv[:, k, :], start=(k == 0), stop=(k == KC - 1)),
                                    reads=[Bs, BA[tg]], writes=[Bp])
                            P.op("act", lambda e, ps=ps, u=u, half=half: e.activation(out=vtm[:, u, half * 512:(half + 1) * 512], in_=ps[:], func=AF.Copy),
                                 reads=[Bp], writes=[Bv])
                    for half in range(2):
                        slot, Bs = gws(tg * 6 + 4 + half)
                        sv = slot[:, :].rearrange("p (k n) -> p k n", k=KC)
                        for u in range(4):
                            tt = 4 * tg + u
                            ps, Bp = next_ps()
                            for k in range(KC):
                                P.op("pe", lambda e, ps=ps, sv=sv, k=k, tt=tt: e.matmul(
                                    ps[:], lhsT=A[:, k, tt * 128:(tt + 1) * 128], rhs=sv[:, k, :], start=(k == 0), stop=(k == KC - 1)),
                                    reads=[Bs, BA[tg]], writes=[Bp])
                            q = cnt[0] % 2
                            cnt[0] += 1
                            P.op("act", lambda e, ps=ps, q=q: e.activation(out=ext[q][:], in_=ps[:], func=AF.Silu), reads=[Bp], writes=[Bext[q]])
                            P.op("pool", lambda e, q=q, u=u, half=half: e.tensor_tensor(out=sgg[:, u, half * 512:(half + 1) * 512], in0=ext[q][:],
                                                                                        in1=retw[:, half * 512:(half + 1) * 512], op=ALU.mult),
                                 reads=[Bext[q], Bg], writes=[Bsgg])
                    for u in range(4):
                        tt = 4 * tg + u
                        psS, BpS = psf[0], Bpsf[0]
                        for h in range(4):
                            P.op("pe", lambda e, u=u, h=h: e.matmul(psS[:, h * 128:(h + 1) * 128], lhsT=ktl[:, h, u * 128:(u + 1) * 128],
                                                                   rhs=qfull[:, h, u * 128:(u + 1) * 128], start=True, stop=True),
                                 reads=[Bkt, Bqf], writes=[BpS])
                        for h in range(4):
                            P.op("dve", lambda e, h=h: e.tensor_tensor(out=ptile[h][:], in0=psS[:, h * 128:(h + 1) * 128], in1=M2, op=ALU.mult),
                                 reads=[BpS, Bg], writes=[Bpti[h]])
                        pos_ = [(psf[3 + h], Bpsf[3 + h]) for h in range(4)]
                        for h in range(4):
                            po, Bpo = pos_[h]
                            vv = vtm[:, u, h * 256:(h + 1) * 256]
                            cur = sbi[h]
                            P.op("pe", lambda e, po=po, h=h, vv=vv: e.matmul(po[:, 0:256], lhsT=ptile[h][:], rhs=vv, start=True, stop=False),
                                 reads=[Bpti[h], Bv], writes=[Bpo])
                            P.op("pe", lambda e, po=po, u=u, h=h, cur=cur: e.matmul(po[:, 0:256], lhsT=qa[:, h, u, :], rhs=Sbf[cur][:, h, :],
                                                                                 start=False, stop=False),
                                 reads=[Bqa, BSbf[cur][h]], writes=[Bpo])
                        for half in range(2):
                            lo = half * 64
                            for h in range(4):
                                pst, Bpst = psf[1 + h // 2], Bpsf[1 + h // 2]
                                off = (h % 2) * 256
                                P.op("pe", lambda e, pst=pst, off=off, u=u, h=h, lo=lo: e.matmul(
                                    pst[:, off:off + 256], lhsT=kk[lo:lo + 64, u, h * 128:(h + 1) * 128], rhs=vtm[lo:lo + 64, u, h * 256:(h + 1) * 256],
                                    start=True, stop=True), reads=[Bkk, Bv], writes=[Bpst])
                            for h in range(4):
                                pst, Bpst = psf[1 + h // 2], Bpsf[1 + h // 2]
                                off = (h % 2) * 256
                                col = u * 128 + lo + 63
                                P.op("dve", lambda e, pst=pst, off=off, h=h, col=col: e.scalar_tensor_tensor(
                                    out=St[:, h, :], in0=St[:, h, :], scalar=epos[:, h, col:col + 1], in1=pst[:, off:off + 256],
                                    op0=ALU.mult, op1=ALU.add), reads=[Bpst, Bep, BSt[h]], writes=[BSt[h]])
                                nxt = 1 - sbi[h]
                                P.op("act", lambda e, h=h, nxt=nxt: e.activation(out=Sbf[nxt][:, h, :], in_=St[:, h, :], func=AF.Copy),
                                     reads=[BSt[h]], writes=[BSbf[nxt][h]])
                                sbi[h] = nxt
                            if half == 0:
                                for h in range(4):
                                    po, Bpo = pos_[h]
                                    nxt = sbi[h]
                                    P.op("pe", lambda e, po=po, u=u, h=h, nxt=nxt: e.matmul(po[:, 0:256], lhsT=qb[:, h, u, :], rhs=Sbf[nxt][:, h, :],
                                                                                         start=False, stop=True),
                                         reads=[Bqa, BSbf[nxt][h]], writes=[Bpo])
                                for h in range(4):
                                    po, Bpo = pos_[h]
                                    gi = cnt[1]
                                    cnt[1] += 1
                                    gpend.sort(key=lambda t: (t[0], t[1]))
                                    while gpend and gpend[0][0] <= gi:
                                        gpend.pop(0)[2]()
                                    sta2, stb = finish(po[:, 0:256], Bpo, sgg[:, u, h * 256:(h + 1) * 256], Bsgg,
                                                       yTg[:, 2 * h:2 * h + 2, u * 128:(u + 1) * 128], ByT)
                                    gpend.append((gi + 2, 2 * gi, sta2))
                                    gpend.append((gi + 4, 2 * gi + 1, stb))
                    gpend.sort(key=lambda t: (t[0], t[1]))
                    while gpend:
                        gpend.pop(0)[2]()
                    P.dma("sp", MV[:, 8:16, tsl(tg)], yTg[:], reads=[ByT], key="yTg")
                do_prefetch()
                P.flush()

        def phase_attn():
            nrot[0] = 3
            with ExitStack() as ls:
                MV = MIXT.rearrange("(c p) t -> p c t", p=128)
                am = sb([128, 9, 512], BF16, ls)
                Bam = P.buf()
                P.dma("pool", am[:], dr["amask"].rearrange("p (m n) -> p m n", m=9), writes=[Bam], key="am")
                qTs = [sb([128, T], BF16, ls) for _ in range(2)]
                kTs = [sb([128, T], BF16, ls) for _ in range(2)]
                Bqs = [P.buf() for _ in range(2)]
                Bks = [P.buf() for _ in range(2)]
                vtms = [sb([128, NTT, 130], BF16, ls) for _ in range(2)]
                Bvs = [P.buf() for _ in range(2)]
                for i in range(2):
                    P.op("dve", lambda e, i=i: e.memset(vtms[i][:], 0.0), writes=[Bvs[i]])
                    P.op("dve", lambda e, i=i: e.memset(vtms[i][:, :, 128:129], 1.0), writes=[Bvs[i]])
                ex = [sb([128, 512], BF16, ls) for _ in range(3)]
                Bex = [P.buf() for _ in range(3)]
                pt = [sb([128, 512], BF16, ls) for _ in range(3)]
                Bptt = [P.buf() for _ in range(3)]
                rden = [sb([128, 1], F32, ls) for _ in range(2)]
                Brd = [P.buf() for _ in range(2)]
                yb = [sb([128, 128], BF16, ls) for _ in range(2)]
                Byb = [P.buf() for _ in range(2)]
                yT = [sb([128, T], BF16, ls) for _ in range(2)]
                ByT = [P.buf() for _ in range(2)]
                Bhalf = [Bpsb[0]] * 2
                cnt = [0, 0]
                LA = 2
                blocks = [(G, J) for G in range(4) for J in range(4 * G + 4)]
                nxt_w = wload(dr["w1"][0], 6144, "pool", "w1_0")
                for h in range(16):
                    slot, Bs = nxt_w
                    hb = h % 2
                    qT, kT, vtm, Bq, Bk, Bv = qTs[hb], kTs[hb], vtms[hb], Bqs[hb], Bks[hb], Bvs[hb]
                    sv = slot[:, 0:6144].rearrange("p (k n) -> p k n", k=KC)
                    for j in range(2):
                        dst, Bd = (qT, Bq) if j == 0 else (kT, Bk)
                        sc = 128 ** -0.5 if j == 0 else 1.0
                        for tg in range(NTG):
                            ps, Bp = next_ps()
                            for k in range(KC):
                                P.op("pe", lambda e, ps=ps, sv=sv, k=k, j=j, tg=tg: e.matmul(
                                    ps[:], lhsT=sv[:, k, j * 128:(j + 1) * 128], rhs=A[:, k, tsl(tg)], start=(k == 0), stop=(k == KC - 1)),
                                    reads=[Bs, BA[tg]], writes=[Bp])
                            P.op("act", lambda e, ps=ps, dst=dst, tg=tg, sc=sc: e.activation(out=dst[:, tsl(tg)], in_=ps[:], func=AF.Copy, scale=sc),
                                 reads=[Bp], writes=[Bd])
                    for g4 in range(4):
                        ps, Bp = next_ps()
                        for u in range(4):
                            tt = g4 * 4 + u
                            for k in range(KC):
                                P.op("pe", lambda e, ps=ps, sv=sv, k=k, tt=tt, u=u: e.matmul(
                                    ps[:, u * 128:(u + 1) * 128], lhsT=A[:, k, tt * 128:(tt + 1) * 128], rhs=sv[:, k, 256:384],
                                    start=(k == 0), stop=(k == KC - 1)), reads=[Bs, BA[g4]], writes=[Bp])
                        P.op("act", lambda e, ps=ps, g4=g4, vtm=vtm: e.activation(out=vtm[:, g4 * 4:(g4 + 1) * 4, 0:128],
                                                                                 in_=ps[:].rearrange("p (u n) -> p u n", u=4), func=AF.Copy),
                             reads=[Bp], writes=[Bv])
                    if h + 1 < 16:
                        nxt_w = wload(dr["w1"][h + 1], 6144, "pool", f"w1_{h + 1}")
                    wd_cast(1, 1)
                    yq = h % 2
                    Sps = {}
                    issued = 0
                    apend = []
                    for bi, (G, J) in enumerate(blocks):
                        while issued < min(bi + 1 + LA, len(blocks)):
                            G2, J2 = blocks[issued]
                            ps2, Bp2 = next_ps()
                            q02 = max(0, J2 - 4 * G2) * 128
                            P.op("pe", lambda e, ps2=ps2, J2=J2, G2=G2, kT=kT, qT=qT, q02=q02: e.matmul(
                                ps2[:, q02:512], lhsT=kT[:, J2 * 128:(J2 + 1) * 128], rhs=qT[:, G2 * 512 + q02:(G2 + 1) * 512],
                                start=True, stop=True), reads=[Bq, Bk], writes=[Bp2])
                            Sps[issued] = (ps2, Bp2)
                            issued += 1
                        ps, Bp = Sps.pop(bi)
                        Ot = [(psf[3 + i], Bpsf[3 + i]) for i in range(4)]
                        s = cnt[0] % 3
                        cnt[0] += 1
                        q0 = max(0, J - 4 * G) * 128
                        P.op("act", lambda e, ps=ps, s=s, q0=q0: e.activation(out=ex[s][:, q0:512], in_=ps[:, q0:512], func=AF.Exp),
                             reads=[Bp], writes=[Bex[s]])
                        mi = min(4 * G - J, 5) + 3
                        P.op("dve", lambda e, s=s, mi=mi, q0=q0: e.tensor_tensor(out=pt[s][:, q0:512], in0=ex[s][:, q0:512], in1=am[:, mi, q0:512], op=ALU.mult),
                             reads=[Bex[s], Bam], writes=[Bptt[s]])
                        apend.sort(key=lambda t: (t[0], t[1]))
                        while apend and apend[0][0] <= bi:
                            apend.pop(0)[2]()
                        for i in range(4):
                            if 4 * G + i >= J:
                                ob, Bob = Ot[i]
                                P.op("pe", lambda e, ob=ob, s=s, i=i, J=J, G=G, vtm=vtm: e.matmul(
                                    ob[:, 0:130], lhsT=pt[s][:, i * 128:(i + 1) * 128], rhs=vtm[:, J, :],
                                    start=(J == 0), stop=(J == 4 * G + i)), reads=[Bptt[s], Bv], writes=[Bob])
                        if J >= 4 * G:
                            i = J - 4 * G
                            ob, Bob = Ot[i]
                            q = cnt[1] % 2
                            cnt[1] += 1
                            hf = G % 2

                            def fa(ob=ob, Bob=Bob, q=q):
                                P.op("dve", lambda e, ob=ob, q=q: e.reciprocal(out=rden[q][:], in_=ob[:, 128:129]),
                                     reads=[Bob], writes=[Brd[q]])
                                P.op("dve", lambda e, ob=ob, q=q: e.tensor_scalar(out=yb[q][:], in0=ob[:, 0:128], scalar1=rden[q][:, 0:1], scalar2=None,
                                                                                 op0=ALU.mult),
                                     reads=[Bob, Brd[q]], writes=[Byb[q]])

                            def fb(i=i, q=q, hf=hf, G=G, yq=yq):
                                ptp = psb[0]
                                P.op("pe", lambda e, ptp=ptp, i=i, q=q, hf=hf: e.transpose(out=ptp[:, hf * 512 + i * 128:hf * 512 + (i + 1) * 128],
                                                                                          in_=yb[q][:], identity=identb[:]),
                                     reads=[Byb[q], Bconst], writes=[Bhalf[hf]])
                                if i == 3:
                                    P.op("dve", lambda e, ptp=ptp, G=G, yq=yq, hf=hf: e.tensor_copy(out=yT[yq][:, tsl(G)], in_=ptp[:, hf * 512:(hf + 1) * 512]),
                                         reads=[Bhalf[hf]], writes=[ByT[yq]])
                            apend.append((bi + 1, 2 * bi, fa))
                            apend.append((bi + 2, 2 * bi + 1, fb))
                    apend.sort(key=lambda t: (t[0], t[1]))
                    while apend:
                        apend.pop(0)[2]()
                    P.dma("sp", MV[:, h, :], yT[yq][:], reads=[ByT[yq]], key=f"yTa{yq}")
                do_prefetch()
                P.flush()

        allp = ["norm0a", "ret", "gla", "wout0", "ffn0", "ple0",
                "norm1a", "attn", "wout1", "ffn1", "ple1", "final"]
        run = allp if phases is None else phases
        PF = {
            "norm0a": [("w0_0", dr["w0"][0], WSLOT, "pool"), ("w0_1", dr["w0"][1], WSLOT, "pool")],
            "ret": [("w0_8", dr["w0"][8], WSLOT, "pool"), ("w0_9", dr["w0"][9], WSLOT, "pool")],
            "gla": [("wo0_0", dr["wo0"][0], WSLOT, "pool"), ("wo0_1", dr["wo0"][1], WSLOT, "pool")],
            "wout0": [("wgu0_0", dr["wgu0"][0], WSLOT, "pool"), ("wgu0_1", dr["wgu0"][1], WSLOT, "pool")],
            "ffn0": [("wpg0_0", dr["wpg0"][0], WSLOT, "pool"), ("wpg0_1", dr["wpg0"][1], WSLOT, "pool")],
            "ple0": [("w1_0", dr["w1"][0], 6144, "pool"), ("w1_1", dr["w1"][1], 6144, "pool")],
            "attn": [("wo1_0", dr["wo1"][0], WSLOT, "pool"), ("wo1_1", dr["wo1"][1], WSLOT, "pool")],
            "wout1": [("wgu1_0", dr["wgu1"][0], WSLOT, "pool"), ("wgu1_1", dr["wgu1"][1], WSLOT, "pool")],
            "ffn1": [("wpg1_0", dr["wpg1"][0], WSLOT, "pool"), ("wpg1_1", dr["wpg1"][1], WSLOT, "pool")],
        }
        for pi, ph in enumerate(run):
            nxt_ph = [p2 for p2 in run[pi + 1:] if not p2.startswith("norm")][:1]
            want = {"norm0a": "ret", "ret": "gla", "gla": "wout0", "wout0": "ffn0", "ffn0": "ple0", "ple0": "attn",
                    "attn": "wout1", "wout1": "ffn1", "ffn1": "ple1"}.get(ph)
            if want is not None and nxt_ph and nxt_ph[0] == want:
                pending_pf.extend(PF[ph])
            if ph == "norm0a":
                phase_norm(xT, 0)
            elif ph == "ret":
                phase_ret()
            elif ph == "gla":
                phase_gla()
            elif ph == "wout0":
                chain_open()
                phase_wout(dr["wo0"], xT, 1, "wo0")
            elif ph == "ffn0":
                if wd_cast(0, KC):
                    P.flush()
                phase_ffn(dr["wgu0"], WDB[0], 2, "wgu0")
            elif ph == "ple0":
                phase_ple(dr["wpg0"], dr["wpp0"], pT[0], "wpg0")
                if not (pi + 1 < len(run) and run[pi + 1] == "norm1a"):
                    chain_close()
            elif ph == "norm1a":
                if chain[0] is not None:
                    phase_norm(R, 3, pre_rstd=True)
                    chain_close()
                else:
                    phase_norm(R, 3)
            elif ph == "attn":
                phase_attn()
            elif ph == "wout1":
                chain_open()
                phase_wout(dr["wo1"], R, 4, "wo1")
            elif ph == "ffn1":
                if wd_cast(1, KC):
                    P.flush()
                phase_ffn(dr["wgu1"], WDB[1], 5, "wgu1")
            elif ph == "ple1":
                phase_ple(dr["wpg1"], dr["wpp1"], pT[1], "wpg1")
                if not (pi + 1 < len(run) and run[pi + 1] == "final"):
                    chain_close()
            elif ph == "final":
                if chain[0] is not None:
                    phase_norm(R, 6, to_out=outT, pre_rstd=True)
                    chain_close()
                else:
                    phase_norm(R, 6, to_out=outT)
            elif ph == "copyx":
                with ExitStack() as ls:
                    t = sb([128, KC, 512], F32, ls)
                    Bt = P.buf()
                    for tg in range(NTG):
                        P.dma("sp", t[:], xT.rearrange("(c p) t -> p c t", p=128)[:, :, tsl(tg)], writes=[Bt], key="cx", waw=True)
                        P.dma("sp", R.rearrange("(c p) t -> p c t", p=128)[:, :, tsl(tg)], t[:], reads=[Bt], key="cy")
                    P.flush()
    return nc


_PROG = None


def kernel(**inputs):
    global _PROG
    inp = {k: np.asarray(v) for k, v in inputs.items()}
    sh = _prep_shared(inp)
    if _PROG is None:
        _PROG = build_program()
    in_maps = []
    for b in range(NCORES):
        m = dict(sh)
        m["xT"] = np.ascontiguousarray(inp["x"][b].T.astype(np.float32))
        m["pT"] = np.ascontiguousarray(np.transpose(inp["p"][:, b], (0, 2, 1)).astype(np.float32))
        m["posr"] = np.ascontiguousarray(np.broadcast_to(inp["positions"][b].astype(np.int32)[None, :], (128, T)))
        in_maps.append(m)
    res = run_bass_kernel_spmd(_PROG, in_maps, core_ids=list(range(NCORES)))
    out = np.stack([np.ascontiguousarray(res.results[b]["outT"].T) for b in range(NCORES)], axis=0)
    return out.astype(np.float32)
```

```python
import math
from contextlib import ExitStack

import numpy as np
import concourse.bass as bass
import concourse.mybir as mybir
from concourse.bass_utils import run_bass_kernel_spmd

F32 = mybir.dt.float32
BF16 = mybir.dt.bfloat16
I32 = mybir.dt.int32
AF = mybir.ActivationFunctionType
ALU = mybir.AluOpType
AX = mybir.AxisListType

T = 2048
D = 2048
KC = 16
NTG = 4
NTT = 16
FF = 5632
FKC = 44
EPS = 1e-6
NCORES = 8
WSLOT = 8192
NWS = 2

ENGS = ("pe", "act", "dve", "pool", "sp")
BLOCKFN = {"pe": "tensor", "act": "scalar", "dve": "vector", "pool": "gpsimd", "sp": "sync"}
SAME_SYNC = True


class Op:
    __slots__ = ("eng", "fn", "deps", "needs_inc", "ticket", "key", "done")

    def __init__(self, eng, fn, key=None):
        self.eng = eng
        self.fn = fn
        self.deps = set()
        self.needs_inc = False
        self.ticket = None
        self.key = key
        self.done = False


class Buf:
    __slots__ = ("name", "w", "r", "war")

    def __init__(self, name=""):
        self.name = name
        self.w = []
        self.r = []
        self.war = []

    def reset(self):
        self.w = []
        self.r = []
        self.war = []


def _add(lst, op):
    if op.key is None:
        for i, o in enumerate(lst):
            if o.key is None and o.eng == op.eng:
                lst[i] = op
                return
    lst.append(op)


class Prog:
    def __init__(self, nc, stack):
        self.nc = nc
        self.stack = stack
        self.ops = {e: [] for e in ENGS}
        self.nsem = 0
        self.esem = {}
        self.ecount = {}
        for e in ENGS:
            self._new_esem(e)
        self.dsem = {}
        self.dcount = {}
        self.waited = {e: {} for e in ENGS}
        self.bufs = []
        self.persist_keys = set()
        self.persist_bufs = set()

    def _new_esem(self, e):
        self.nsem += 1
        self.esem[e] = self.stack.enter_context(self.nc.semaphore(f"se{self.nsem}"))
        self.ecount[e] = 0

    def buf(self, name=""):
        b = Buf(name)
        self.bufs.append(b)
        return b

    def _record(self, op, reads, writes, waw):
        for b in reads:
            for o in b.w:
                if o is not op:
                    op.deps.add(o)
            _add(b.r, op)
        for b in writes:
            rd = [o for o in b.r if o is not op]
            if rd:
                for o in rd:
                    op.deps.add(o)
                for o in b.w:
                    if o is not op:
                        op.deps.add(o)
                b.war = rd
                b.r = [o for o in b.r if o is op]
                b.w = [op]
            else:
                for o in b.war:
                    op.deps.add(o)
                if waw:
                    for o in b.w:
                        if o is not op:
                            op.deps.add(o)
                _add(b.w, op)
        for d in op.deps:
            d.needs_inc = True
        self.ops[op.eng].append(op)

    def op(self, eng, fn, reads=(), writes=(), waw=True):
        o = Op(eng, fn)
        self._record(o, reads, writes, waw)
        return o

    def dma(self, eng, out, in_, reads=(), writes=(), key="d", waw=False):
        if key not in self.dsem:
            self.nsem += 1
            self.dsem[key] = self.stack.enter_context(self.nc.semaphore(f"sd{self.nsem}"))
            self.dcount[key] = 0
        o = Op(eng, lambda e, out=out, in_=in_: e.dma_start(out=out, in_=in_), key=key)
        self.dcount[key] += 16
        o.ticket = self.dcount[key]
        self._record(o, reads, writes, waw)
        return o

    def flush(self):
        nc = self.nc
        for e in ENGS:
            last = None
            for o in self.ops[e]:
                if o.key is None:
                    last = o
            if last is not None:
                last.needs_inc = True
            for o in self.ops[e]:
                if o.key is None and o.needs_inc:
                    self.ecount[e] += 1
                    o.ticket = self.ecount[e]
        finals = {e: (self.esem[e], self.ecount[e]) for e in ENGS}
        dfinals = {k: (self.dsem[k], self.dcount[k]) for k in self.dsem if k not in self.persist_keys}
        with nc.Block() as block:
            for e in ENGS:
                def body(engine, e=e):
                    wd = self.waited[e]
                    for o in self.ops[e]:
                        waits = {}
                        for d in o.deps:
                            if d.done:
                                continue
                            if d.key is None:
                                if d.eng == e and (e == "pe" or not SAME_SYNC):
                                    continue
                                sem = self.esem[d.eng]
                            else:
                                sem = self.dsem[d.key]
                            v = d.ticket
                            if waits.get(sem, 0) < v:
                                waits[sem] = v
                        for sem, v in waits.items():
                            if wd.get(sem, 0) >= v:
                                continue
                            engine.wait_ge(sem, v)
                            wd[sem] = v
                        ins = o.fn(engine)
                        if o.key is not None:
                            ins.then_inc(self.dsem[o.key], 16)
                        elif o.needs_inc:
                            ins.then_inc(self.esem[e], 1)
                    for e2 in ENGS:
                        sem, v = finals[e2]
                        if e2 == e or v == 0:
                            continue
                        if wd.get(sem, 0) < v:
                            engine.wait_ge(sem, v)
                            wd[sem] = v
                    for k, (sem, v) in dfinals.items():
                        if v and wd.get(sem, 0) < v:
                            engine.wait_ge(sem, v)
                            wd[sem] = v
                getattr(block, BLOCKFN[e])(body)
        for e in ENGS:
            for o in self.ops[e]:
                if o.key is None or o.key not in self.persist_keys:
                    o.done = True
                o.deps = None
                o.fn = None
            self.ops[e] = []
            if self.ecount[e] > 12000:
                self._new_esem(e)
        for b in self.bufs:
            if b not in self.persist_bufs:
                b.reset()
            else:
                b.r = []
                b.war = []
                b.w = [o for o in b.w if not o.done]


def _blk(W, nbw):
    K, N = W.shape
    kc = K // 128
    nb = N // nbw
    return np.ascontiguousarray(W.reshape(kc, 128, nb, nbw).transpose(2, 1, 0, 3)).reshape(nb, 128, kc * nbw)


def _host_constants():
    c = {}
    half = 128
    lin = np.linspace(0.0, 1.0, half, dtype=np.float32)
    inv = (np.float32(1.0) / np.power(np.float32(10000.0), lin)).astype(np.float32)
    c["invf"] = inv.reshape(128, 1)
    c["ident"] = np.eye(128, dtype=np.float32)
    hh = np.arange(4, dtype=np.float64)
    gam = 1.0 - np.exp2(-5.0 - hh)
    jj = np.arange(128)[:, None].astype(np.float64)
    ii = np.arange(512)[None, :].astype(np.float64)
    rm = np.zeros((4, 128, 5, 512), np.float32)
    for h in range(4):
        rm[h, :, 0, :] = (gam[h] ** (ii - jj)) * 256 ** -0.5
        for m in range(4):
            d = ii - (m * 128 + jj)
            rm[h, :, 1 + m, :] = np.where(d >= 0, gam[h] ** np.maximum(d, 0), 0.0) * 256 ** -0.5
    c["rmask"] = rm.reshape(4, 128, 5 * 512)
    c["gam"] = gam
    tp = np.arange(128)[:, None]
    tt = np.arange(128)[None, :]
    same = (tp // 64) == (tt // 64)
    L = np.where(same & (tp <= tt), -1.0 / 16.0, 0.0)
    U = np.where(same & (tp > tt), -1.0 / 16.0, 0.0)
    M2 = np.where(same & (tp <= tt), 1.0, 0.0)
    c["tri"] = np.concatenate([L, U, M2], axis=1).astype(np.float32)
    am = np.zeros((128, 9, 512), np.float32)
    for idx, dl in enumerate([-3, -2, -1, 0, 1, 2, 3, 4, 5]):
        d = (dl * 128 + ii - jj).astype(np.int64)
        cnt = ((d >= 0) & (d <= 128)).astype(np.float32) + ((d >= 0) & (d % 4 == 0) & (d <= 512)).astype(np.float32) \
            + ((d >= 0) & (d % 16 == 0)).astype(np.float32)
        am[:, idx, :] = cnt
    c["amask"] = am.reshape(128, 9 * 512)
    return c


_CONST = None


def _prep_shared(inp):
    global _CONST
    if _CONST is None:
        _CONST = _host_constants()
    c = _CONST
    sh = {"invf": c["invf"], "ident": c["ident"], "rmask": c["rmask"], "tri": c["tri"], "amask": c["amask"]}
    f = np.float32
    nws = [inp["attn_norm_w"][0], inp["ffn_norm_w"][0], inp["ple_norm_w"][0],
           inp["attn_norm_w"][1], inp["ffn_norm_w"][1], inp["ple_norm_w"][1], inp["final_norm_w"]]
    sh["nw"] = np.ascontiguousarray(np.stack([np.asarray(v, f).reshape(16, 128).T for v in nws], axis=1)).reshape(128, 7 * 16)
    w_in = np.asarray(inp["ab_w_in"][0], f)
    perm = np.concatenate([np.arange(0, 256, 2), np.arange(1, 256, 2)])
    cols = []
    for h in range(4):
        cols.append(w_in[:, 0 + h * 256 + perm])
        cols.append(w_in[:, 1024 + h * 256 + perm])
        cols.append(w_in[:, 2048 + h * 256: 2048 + (h + 1) * 256])
        cols.append(w_in[:, 3072 + h * 256: 3072 + (h + 1) * 256])
    cols.append(w_in[:, 4096:4608])
    cols.append(w_in[:, 4608:5120])
    cols.append(w_in[:, 5120:6144])
    cols.append(w_in[:, 6144:7168])
    sh["w0"] = _blk(np.concatenate(cols, axis=1), 512)
    sh["wglr"] = _blk(w_in[:, 7168:7184], 16).reshape(128, 256)
    sh["gu"] = np.concatenate([np.asarray(inp["ab_gla_gate_up"][0], f), np.asarray(inp["ab_gla_gate_b"][0], f)[None, :]], axis=0)
    rw = np.concatenate([np.asarray(inp["ab_ret_norm_w"][0], f), np.asarray(inp["ab_gla_norm_w"][0], f)])
    sh["retw"] = np.ascontiguousarray(np.broadcast_to(rw[None, :], (128, 2048)))
    sh["wo0"] = _blk(np.asarray(inp["ab_w_out"][0], f), 512)
    wq = np.asarray(inp["c_w_qkv"][0], f)
    cols = []
    for h in range(16):
        cols.append(wq[:, h * 128:(h + 1) * 128])
        cols.append(wq[:, 2048 + h * 128: 2048 + (h + 1) * 128])
        cols.append(wq[:, 4096 + h * 128: 4096 + (h + 1) * 128])
    sh["w1"] = _blk(np.concatenate(cols, axis=1), 384)
    sh["wo1"] = _blk(np.asarray(inp["c_w_out"][0], f), 512)
    for i in range(2):
        g = np.asarray(inp["ffn_w_gate"][i], f)
        u = np.asarray(inp["ffn_w_up"][i], f)
        gu = np.concatenate([g.reshape(D, 22, 256), u.reshape(D, 22, 256)], axis=2).reshape(D, 22 * 512)
        sh[f"wgu{i}"] = _blk(gu, 512)
        sh[f"wd{i}"] = _blk(np.asarray(inp["ffn_w_down"][i], f), 128)
        sh[f"wpg{i}"] = _blk(np.asarray(inp["ple_w_gate"][i], f), 512)
        sh[f"wpp{i}"] = _blk(np.asarray(inp["ple_w_proj"][i], f), 2048).reshape(128, 2 * 2048)
    return sh


SHARED_SHAPES = {
    "invf": [128, 1], "ident": [128, 128], "rmask": [4, 128, 2560], "tri": [128, 384], "amask": [128, 4608],
    "nw": [128, 112], "w0": [14, 128, 8192], "wglr": [128, 256], "gu": [17, 512], "retw": [128, 2048],
    "wo0": [4, 128, 8192], "w1": [16, 128, 6144], "wo1": [4, 128, 8192],
    "wgu0": [22, 128, 8192], "wd0": [16, 128, 5632], "wpg0": [4, 128, 8192], "wpp0": [128, 4096],
    "wgu1": [22, 128, 8192], "wd1": [16, 128, 5632], "wpg1": [4, 128, 8192], "wpp1": [128, 4096],
}


def build_program(phases=None, dbg=False):
    nc = bass.Bass("TRN2", target_bir_lowering=False)
    es = ExitStack()
    with es:
        dr = {}
        for k, shp in SHARED_SHAPES.items():
            dr[k] = nc.dram_tensor(k, list(shp), F32, kind="ExternalInput").ap()
        xT = nc.dram_tensor("xT", [D, T], F32, kind="ExternalInput").ap()
        pT = nc.dram_tensor("pT", [2, 256, T], F32, kind="ExternalInput").ap()
        posr = nc.dram_tensor("posr", [128, T], I32, kind="ExternalInput").ap()
        outT = nc.dram_tensor("outT", [D, T], F32, kind="ExternalOutput").ap()
        skind = "ExternalOutput" if dbg else "Internal"
        R = nc.dram_tensor("R", [D, T], F32, kind=skind).ap()
        MIXT = nc.dram_tensor("MIXT", [D, T], BF16, kind=skind).ap()
        WDB = [nc.dram_tensor(f"wdb{i}", [KC, 128, FF], BF16, kind="Internal").ap() for i in range(2)]
        wdcast_n = [0, 0]

        def wd_cast(layer, n):
            issued = 0
            for _ in range(n):
                c = wdcast_n[layer]
                if c >= KC:
                    break
                wdcast_n[layer] += 1
                issued += 1
                P.dma("pool", WDB[layer][c], dr[f"wd{layer}"][c], key=f"wdc{c % 4}")
            return issued

        P = Prog(nc, es)

        tctr = [0]

        def sb(shape, dt, stack=es):
            tctr[0] += 1
            return stack.enter_context(nc.sbuf_tensor(f"t{tctr[0]}", list(shape), dt))

        A = sb([128, KC, T], BF16)
        BA = [P.buf(f"A{tg}") for tg in range(NTG)]
        wslots = [sb([128, WSLOT], BF16) for _ in range(NWS)]
        Bw = [P.buf(f"w{i}") for i in range(NWS)]
        wctr = [0]
        nwt = sb([128, 112], F32)
        identb = sb([128, 128], BF16)
        onesf = sb([128, 128], F32)
        cbias = sb([128, 4], F32)
        Bconst = P.buf("const")
        psf = [es.enter_context(nc.psum_tensor(f"psf{i}", [128, 512], F32)) for i in range(7)]
        psb = [es.enter_context(nc.psum_tensor(f"psb{i}", [128, 1024], BF16)) for i in range(1)]
        Bpsf = [P.buf(f"psf{i}") for i in range(7)]
        Bpsb = [P.buf(f"psb{i}") for i in range(1)]
        pctr = [0, 0]
        nrot = [7]

        def next_ps():
            i = pctr[0] % nrot[0]
            pctr[0] += 1
            return psf[i], Bpsf[i]

        def next_psb():
            return psb[0], Bpsb[0]

        prefetched = {}
        pending_pf = []
        for i_ in range(NWS):
            P.persist_keys.add(f"w{i_}pool")
            P.persist_keys.add(f"w{i_}sp")
            P.persist_bufs.add(Bw[i_])

        def wload(src_ap, ncols, eng="pool", key=None):
            if key is not None and key in prefetched:
                return prefetched.pop(key)
            s = wctr[0] % NWS
            wctr[0] += 1
            P.dma(eng, wslots[s][:, 0:ncols], src_ap, writes=[Bw[s]], key=f"w{s}{eng}")
            return wslots[s], Bw[s]

        def do_prefetch():
            while pending_pf:
                key, ap, ncols, eng = pending_pf.pop(0)
                prefetched[key] = wload(ap, ncols, eng)

        def wstream(items):
            st = {"next": 0, "slots": {}}

            def get(i):
                while st["next"] <= min(i + 1, len(items) - 1):
                    n = st["next"]
                    st["slots"][n] = wload(*items[n])
                    st["next"] += 1
                return st["slots"].pop(i)
            return get

        P.dma("sp", nwt[:], dr["nw"], writes=[Bconst], key="c0")
        P.dma("pool", identb[:], dr["ident"], writes=[Bconst], key="c1")
        P.op("dve", lambda e: e.memset(onesf[:], 1.0), writes=[Bconst])
        P.op("dve", lambda e: e.memset(cbias[:, 0:1], math.pi), writes=[Bconst])
        P.op("dve", lambda e: e.memset(cbias[:, 1:2], 1.0), writes=[Bconst])
        P.op("dve", lambda e: e.memset(cbias[:, 2:3], EPS), writes=[Bconst])
        P.flush()

        def tsl(tg):
            return slice(tg * 512, (tg + 1) * 512)

        rstdg = [None] * NTG
        Brg = [P.buf(f"rstdg{i}") for i in range(NTG)]
        accg = [None] * NTG
        Bag = [P.buf(f"accg{i}") for i in range(NTG)]
        chain = [None]

        def chain_open():
            chain[0] = ExitStack()
            for i in range(NTG):
                rstdg[i] = sb([128, 512], F32, chain[0])

        def chain_close():
            chain[0].close()
            chain[0] = None

        def make_stats(ls):
            sq = [sb([128, 512], F32, ls) for _ in range(2)]
            Bsq = [P.buf() for _ in range(2)]
            for i in range(NTG):
                accg[i] = sb([128, 512], F32, ls)
            cn = [0]

            def stat(h_ap, Bh, c, tg, widx, first, write_a=True):
                q = cn[0] % 2
                cn[0] += 1
                if write_a:
                    wcol = nwt[:, widx * 16 + c: widx * 16 + c + 1]
                    P.op("act", lambda e, h_ap=h_ap, c=c, tg=tg, wcol=wcol: e.activation(out=A[:, c, tsl(tg)], in_=h_ap, func=AF.Copy, scale=wcol),
                         reads=[Bh, Bconst], writes=[BA[tg]])
                P.op("act", lambda e, h_ap=h_ap, q=q: e.activation(out=sq[q][:], in_=h_ap, func=AF.Square), reads=[Bh], writes=[Bsq[q]])
                if first:
                    P.op("pool", lambda e, q=q, tg=tg: e.tensor_copy(out=accg[tg][:], in_=sq[q][:]), reads=[Bsq[q]], writes=[Bag[tg]])
                else:
                    P.op("pool", lambda e, q=q, tg=tg: e.tensor_tensor(out=accg[tg][:], in0=accg[tg][:], in1=sq[q][:], op=ALU.add),
                         reads=[Bsq[q], Bag[tg]], writes=[Bag[tg]])
            return stat

        def finish_stats(tg):
            ps, Bp = next_ps()
            P.op("pe", lambda e, ps=ps, tg=tg: e.matmul(ps[:], lhsT=onesf[:], rhs=accg[tg][:], start=True, stop=True),
                 reads=[Bag[tg], Bconst], writes=[Bp])
            P.op("act", lambda e, ps=ps, tg=tg: e.activation(out=rstdg[tg][:], in_=ps[:], func=AF.Sqrt, scale=1.0 / D, bias=cbias[:, 2:3]),
                 reads=[Bp, Bconst], writes=[Brg[tg]])
            P.op("dve", lambda e, tg=tg: e.reciprocal(out=rstdg[tg][:], in_=rstdg[tg][:]), reads=[Brg[tg]], writes=[Brg[tg]])

        def phase_norm(src, widx, to_out=None, pre_rstd=False):
            nrot[0] = 7
            with ExitStack() as ls:
                nxb = 3 if pre_rstd else 2
                xt = [sb([128, KC, 512], F32, ls) for _ in range(nxb)]
                Bxt = [P.buf() for _ in range(nxb)]
                if not pre_rstd:
                    sq = [sb([128, 512], F32, ls) for _ in range(2)]
                    Bsq = [P.buf() for _ in range(2)]
                    acc = [sb([128, 512], F32, ls) for _ in range(2)]
                    Bacc = [P.buf() for _ in range(2)]
                    rstd = [sb([128, 512], F32, ls) for _ in range(2)]
                    Brs = [P.buf() for _ in range(2)]
                else:
                    rstd = Brs = [None, None, None]
                srcv = src.rearrange("(c p) t -> p c t", p=128)
                outv = to_out.rearrange("(c p) t -> p c t", p=128) if to_out is not None else None
                if pre_rstd:
                    for tg in range(min(nxb, NTG)):
                        P.dma("sp", xt[tg][:], srcv[:, :, tsl(tg)], writes=[Bxt[tg]], key=f"nx{tg}")
                for tg in range(NTG):
                    s = tg % nxb
                    if not pre_rstd:
                        P.dma("sp", xt[s][:], srcv[:, :, tsl(tg)], writes=[Bxt[s]], key=f"nx{s}")
                    elif tg >= nxb:
                        P.dma("sp", xt[s][:], srcv[:, :, tsl(tg)], writes=[Bxt[s]], key=f"nx{s}")
                    rs_t, rs_B = (rstdg[tg], Brg[tg]) if pre_rstd else (rstd[s], Brs[s])
                    if not pre_rstd:
                        for c in range(KC):
                            q = c % 2
                            P.op("act", lambda e, s=s, c=c, q=q: e.activation(out=sq[q][:], in_=xt[s][:, c, :], func=AF.Square),
                                 reads=[Bxt[s]], writes=[Bsq[q]])
                            if c == 0:
                                P.op("pool", lambda e, s=s, q=q: e.tensor_copy(out=acc[s][:], in_=sq[q][:]),
                                     reads=[Bsq[q]], writes=[Bacc[s]])
                            else:
                                P.op("pool", lambda e, s=s, q=q: e.tensor_tensor(out=acc[s][:], in0=acc[s][:], in1=sq[q][:], op=ALU.add),
                                     reads=[Bsq[q], Bacc[s]], writes=[Bacc[s]])
                        ps, Bp = next_ps()
                        P.op("pe", lambda e, ps=ps, s=s: e.matmul(ps[:], lhsT=onesf[:], rhs=acc[s][:], start=True, stop=True),
                             reads=[Bacc[s], Bconst], writes=[Bp])
                        P.op("dve", lambda e, ps=ps, s=s: e.tensor_scalar(out=rstd[s][:], in0=ps[:], scalar1=1.0 / D, scalar2=EPS,
                                                                         op0=ALU.mult, op1=ALU.add),
                             reads=[Bp], writes=[Brs[s]])
                        P.op("act", lambda e, s=s: e.activation(out=rstd[s][:], in_=rstd[s][:], func=AF.Sqrt),
                             reads=[Brs[s]], writes=[Brs[s]])
                        P.op("dve", lambda e, s=s: e.reciprocal(out=rstd[s][:], in_=rstd[s][:]),
                             reads=[Brs[s]], writes=[Brs[s]])
                    for c in range(KC):
                        wcol = nwt[:, widx * 16 + c: widx * 16 + c + 1]
                        eng_ = "dve"
                        if to_out is None:
                            P.op(eng_, lambda e, s=s, c=c, tg=tg, wcol=wcol, rs_t=rs_t: e.scalar_tensor_tensor(
                                out=A[:, c, tsl(tg)], in0=xt[s][:, c, :], scalar=wcol, in1=rs_t[:],
                                op0=ALU.mult, op1=ALU.mult),
                                reads=[Bxt[s], rs_B, Bconst], writes=[BA[tg]])
                        else:
                            P.op(eng_, lambda e, s=s, c=c, wcol=wcol, rs_t=rs_t: e.scalar_tensor_tensor(
                                out=xt[s][:, c, :], in0=xt[s][:, c, :], scalar=wcol, in1=rs_t[:],
                                op0=ALU.mult, op1=ALU.mult),
                                reads=[Bxt[s], rs_B, Bconst], writes=[Bxt[s]])
                    if to_out is not None:
                        P.dma("act", outv[:, :, tsl(tg)], xt[s][:], reads=[Bxt[s]], key=f"no{s}")
                do_prefetch()
                P.flush()

        def fm_proj(wdram, nblocks, kc, evac, act=None, ncols=WSLOT, chunks_per_block=4, nbw=512, tgs=range(NTG), wkey=None):
            act = A if act is None else act
            ws = wstream([(wdram[b], ncols, "pool", f"{wkey}_{b}" if wkey else None) for b in range(nblocks)])
            for b in range(nblocks):
                slot, Bs = ws(b)
                sv = slot[:, 0:kc * nbw].rearrange("p (k n) -> p k n", k=kc)
                for j in range(chunks_per_block):
                    for tg in tgs:
                        ps, Bp = next_ps()
                        for k in range(kc):
                            P.op("pe", lambda e, ps=ps, sv=sv, k=k, j=j, tg=tg: e.matmul(
                                ps[:], lhsT=sv[:, k, j * 128:(j + 1) * 128], rhs=act[:, k, tsl(tg)],
                                start=(k == 0), stop=(k == kc - 1)),
                                reads=[Bs, BA[tg]], writes=[Bp])
                        evac(ps, Bp, b * chunks_per_block + j, tg)

        def phase_wout(wdram, res_src, widx_next, wkey):
            nrot[0] = 7
            with ExitStack() as ls:
                M = sb([128, KC, T], BF16, ls)
                BM = [P.buf() for _ in range(NTG)]
                mv = MIXT.rearrange("(c p) t -> p c t", p=128)
                for tg in range(NTG):
                    P.dma("sp", M[:, :, tsl(tg)], mv[:, :, tsl(tg)], writes=[BM[tg]], key=f"ml{tg}")
                xr = [sb([128, 512], F32, ls) for _ in range(3)]
                Bxr = [P.buf() for _ in range(3)]
                ho = [sb([128, 512], F32, ls) for _ in range(3)]
                Bho = [P.buf() for _ in range(3)]
                rv = res_src.rearrange("(c p) t -> p c t", p=128)
                Rv = R.rearrange("(c p) t -> p c t", p=128)
                stat = make_stats(ls)
                nxt = [None]
                ctr = [0]
                for b in range(4):
                    slot, Bs = nxt[0] if nxt[0] is not None else wload(wdram[b], WSLOT, "pool", f"{wkey}_{b}")
                    nxt[0] = wload(wdram[b + 1], WSLOT, "pool", f"{wkey}_{b + 1}") if b + 1 < 4 else None
                    sv = slot[:, :].rearrange("p (k n) -> p k n", k=KC)
                    for j in range(4):
                        c = b * 4 + j
                        for tg in range(NTG):
                            s_ = ctr[0] % 3
                            ctr[0] += 1
                            P.dma("sp", xr[s_][:], rv[:, c, tsl(tg)], writes=[Bxr[s_]], key=f"xr{s_}")
                            ps, Bp = next_ps()
                            for k in range(KC):
                                P.op("pe", lambda e, ps=ps, sv=sv, k=k, j=j, tg=tg: e.matmul(
                                    ps[:], lhsT=sv[:, k, j * 128:(j + 1) * 128], rhs=M[:, k, tsl(tg)],
                                    start=(k == 0), stop=(k == KC - 1)), reads=[Bs, BM[tg]], writes=[Bp])
                            P.op("dve", lambda e, ps=ps, s_=s_: e.tensor_tensor(out=ho[s_][:], in0=ps[:], in1=xr[s_][:], op=ALU.add),
                                 reads=[Bp, Bxr[s_]], writes=[Bho[s_]])
                            stat(ho[s_][:], Bho[s_], c, tg, widx_next, c == 0)
                            P.dma("act", Rv[:, c, tsl(tg)], ho[s_][:], reads=[Bho[s_]], key=f"ho{s_}")
                for tg in range(NTG):
                    finish_stats(tg)
                do_prefetch()
                P.flush()

        def phase_ffn(wgu, wd, widx_next, wkey):
            nrot[0] = 7
            with ExitStack() as ls:
                actT = sb([128, FKC, 512], BF16, ls)
                Bact = P.buf()
                sl = [sb([128, 512], F32, ls) for _ in range(2)]
                Bsl = [P.buf() for _ in range(2)]
                ul = [sb([128, 512], F32, ls) for _ in range(2)]
                Bul = [P.buf() for _ in range(2)]
                stat = make_stats(ls)
                hr = [sb([128, 512], F32, ls) for _ in range(3)]
                Bhr = [P.buf() for _ in range(3)]
                hn = [sb([128, 512], F32, ls) for _ in range(3)]
                Bhn = [P.buf() for _ in range(3)]
                Rv = R.rearrange("(c p) t -> p c t", p=128)
                cnt = [0]
                items = []
                for tg in range(NTG):
                    items += [(wgu[b], WSLOT, "pool", f"{wkey}_{b}" if (tg == 0 and b < 2) else None) for b in range(22)] + [(wd[c], FF, "sp") for c in range(KC)]
                ws = wstream(items)
                for tg in range(NTG):
                    P.dma("sp", hr[0][:], Rv[:, 0, tsl(tg)], writes=[Bhr[0]], key="hr0")
                    for b in range(22):
                        slot, Bs = ws(tg * 38 + b)
                        sv = slot[:, :].rearrange("p (k n) -> p k n", k=KC)
                        for j in range(2):
                            pg, Bg = next_ps()
                            for k in range(KC):
                                P.op("pe", lambda e, pg=pg, sv=sv, k=k, j=j, tg=tg: e.matmul(
                                    pg[:], lhsT=sv[:, k, j * 128:(j + 1) * 128], rhs=A[:, k, tsl(tg)],
                                    start=(k == 0), stop=(k == KC - 1)), reads=[Bs, BA[tg]], writes=[Bg])
                            pu, Bu = next_ps()
                            for k in range(KC):
                                P.op("pe", lambda e, pu=pu, sv=sv, k=k, j=j, tg=tg: e.matmul(
                                    pu[:], lhsT=sv[:, k, 256 + j * 128:256 + (j + 1) * 128], rhs=A[:, k, tsl(tg)],
                                    start=(k == 0), stop=(k == KC - 1)), reads=[Bs, BA[tg]], writes=[Bu])
                            q = cnt[0] % 2
                            cnt[0] += 1
                            P.op("dve", lambda e, pg=pg, q=q, tg=tg: e.tensor_tensor(out=sl[q][:], in0=pg[:], in1=rstdg[tg][:], op=ALU.mult),
                                 reads=[Bg, Brg[tg]], writes=[Bsl[q]])
                            P.op("act", lambda e, q=q: e.activation(out=sl[q][:], in_=sl[q][:], func=AF.Silu),
                                 reads=[Bsl[q]], writes=[Bsl[q]])
                            P.op("dve", lambda e, pu=pu, q=q, tg=tg: e.tensor_tensor(out=ul[q][:], in0=pu[:], in1=rstdg[tg][:], op=ALU.mult),
                                 reads=[Bu, Brg[tg]], writes=[Bul[q]])
                            P.op("dve", lambda e, q=q, b=b, j=j: e.tensor_tensor(
                                out=actT[:, b * 2 + j, :], in0=sl[q][:], in1=ul[q][:], op=ALU.mult),
                                reads=[Bul[q], Bsl[q]], writes=[Bact])
                    for c in range(KC):
                        slot, Bs = ws(tg * 38 + 22 + c)
                        sv = slot[:, 0:FF].rearrange("p (k n) -> p k n", k=FKC)
                        s = c % 3
                        if c + 1 < KC:
                            s1 = (c + 1) % 3
                            P.dma("sp", hr[s1][:], Rv[:, c + 1, tsl(tg)], writes=[Bhr[s1]], key=f"hr{s1}")
                        ps, Bp = next_ps()
                        for k in range(FKC):
                            P.op("pe", lambda e, ps=ps, sv=sv, k=k: e.matmul(
                                ps[:], lhsT=sv[:, k, :], rhs=actT[:, k, :], start=(k == 0), stop=(k == FKC - 1)),
                                reads=[Bs, Bact], writes=[Bp])
                        P.op("dve", lambda e, ps=ps, s=s: e.tensor_tensor(out=hn[s][:], in0=ps[:], in1=hr[s][:], op=ALU.add),
                             reads=[Bp, Bhr[s]], writes=[Bhn[s]])
                        stat(hn[s][:], Bhn[s], c, tg, widx_next, c == 0)
                        P.dma("act", Rv[:, c, tsl(tg)], hn[s][:], reads=[Bhn[s]], key=f"hn{s}")
                    finish_stats(tg)
                do_prefetch()
                P.flush()

        def phase_ple(wpg, wpp, pTl, wkey):
            nrot[0] = 7
            with ExitStack() as ls:
                ppb = sb([128, 2, T], BF16, ls)
                wppb = sb([128, 2, T], BF16, ls)
                Bpp = P.buf()
                P.dma("pool", ppb[:], pTl.rearrange("(k p) t -> p k t", p=128), writes=[Bpp], key="pp0")
                P.dma("pool", wppb[:], wpp.rearrange("p (k n) -> p k n", k=2), writes=[Bpp], key="pp1")
                xr = [sb([128, T], F32, ls) for _ in range(2)]
                Bxr = [P.buf() for _ in range(2)]
                ho = [sb([128, T], F32, ls) for _ in range(2)]
                Bho = [P.buf() for _ in range(2)]
                gt = [sb([128, 512], F32, ls) for _ in range(2)]
                Bgt = [P.buf() for _ in range(2)]
                Rv = R.rearrange("(c p) t -> p c t", p=128)
                cnt = [0]
                stat = make_stats(ls)

                def evac(ps, Bp, c, tg):
                    s = c % 2
                    if tg == 0:
                        P.dma("sp", xr[s][:], Rv[:, c, :], writes=[Bxr[s]], key=f"xr{s}")
                    q = cnt[0] % 2
                    cnt[0] += 1
                    P.op("dve", lambda e, ps=ps, q=q, tg=tg: e.tensor_tensor(out=gt[q][:], in0=ps[:], in1=rstdg[tg][:], op=ALU.mult),
                         reads=[Bp, Brg[tg]], writes=[Bgt[q]])
                    P.op("act", lambda e, q=q: e.activation(out=gt[q][:], in_=gt[q][:], func=AF.Sigmoid),
                         reads=[Bgt[q]], writes=[Bgt[q]])
                    p2, Bp2 = next_ps()
                    for k in range(2):
                        P.op("pe", lambda e, p2=p2, k=k, c=c, tg=tg: e.matmul(
                            p2[:], lhsT=wppb[:, k, c * 128:(c + 1) * 128], rhs=ppb[:, k, tsl(tg)],
                            start=(k == 0), stop=(k == 1)), reads=[Bpp], writes=[Bp2])
                    P.op("dve", lambda e, p2=p2, q=q: e.tensor_tensor(out=gt[q][:], in0=gt[q][:], in1=p2[:], op=ALU.mult),
                         reads=[Bp2, Bgt[q]], writes=[Bgt[q]])
                    P.op("pool", lambda e, q=q, s=s, tg=tg: e.tensor_tensor(out=ho[s][:, tsl(tg)], in0=gt[q][:], in1=xr[s][:, tsl(tg)], op=ALU.add),
                         reads=[Bgt[q], Bxr[s]], writes=[Bho[s]])
                    stat(ho[s][:, tsl(tg)], Bho[s], c, tg, None, c == 0, write_a=False)
                    if tg == NTG - 1:
                        P.dma("act", Rv[:, c, :], ho[s][:], reads=[Bho[s]], key=f"ho{s}")
                fm_proj(wpg, 4, KC, evac, wkey=wkey)
                for tg in range(NTG):
                    finish_stats(tg)
                do_prefetch()
                P.flush()

        def make_finisher(ls):
            osb = [sb([128, 256], F32, ls) for _ in range(3)]
            Bosb = [P.buf() for _ in range(3)]
            sqt = [sb([128, 256], F32, ls) for _ in range(2)]
            Bsqt = [P.buf() for _ in range(2)]
            ss = [sb([128, 1], F32, ls) for _ in range(3)]
            Bss = [P.buf() for _ in range(3)]
            yb = [sb([128, 256], BF16, ls) for _ in range(3)]
            Byb = [P.buf() for _ in range(3)]
            Bquad = [Bpsb[0]] * 4
            cnt = [0, 0]

            def finish(O_ap, Bo, sg_ap, Bsg, dst_ap, Bdst):
                q = cnt[0] % 3
                q2 = cnt[0] % 2
                cnt[0] += 1
                P.op("pool", lambda e, q=q: e.memset(ss[q][:], 0.0), writes=[Bss[q]])
                P.op("act", lambda e, q=q: e.activation(out=osb[q][:], in_=O_ap, func=AF.Copy), reads=[Bo], writes=[Bosb[q]])
                P.op("act", lambda e, q=q, q2=q2: e.activation(out=sqt[q2][:], in_=osb[q][:], func=AF.Square, accum_out=ss[q][:]),
                     reads=[Bosb[q], Bss[q]], writes=[Bsqt[q2], Bss[q]])
                P.op("act", lambda e, q=q: e.activation(out=ss[q][:], in_=ss[q][:], func=AF.Sqrt, scale=1.0 / 256.0, bias=cbias[:, 2:3]),
                     reads=[Bss[q], Bconst], writes=[Bss[q]])

                def stage_a2(q=q):
                    P.op("dve", lambda e, q=q: e.reciprocal(out=ss[q][:], in_=ss[q][:]),
                         reads=[Bss[q]], writes=[Bss[q]])
                    P.op("dve", lambda e, q=q: e.scalar_tensor_tensor(out=yb[q][:], in0=osb[q][:], scalar=ss[q][:, 0:1], in1=sg_ap,
                                                                     op0=ALU.mult, op1=ALU.mult),
                         reads=[Bosb[q], Bss[q], Bsg], writes=[Byb[q]])

                def stage_b(q=q):
                    r = cnt[1] % 4
                    cnt[1] += 1
                    pt = psb[0]
                    for j in range(2):
                        P.op("pe", lambda e, q=q, j=j, r=r: e.transpose(out=pt[:, r * 256 + j * 128:r * 256 + (j + 1) * 128],
                                                                        in_=yb[q][:, j * 128:(j + 1) * 128], identity=identb[:]),
                             reads=[Byb[q], Bconst], writes=[Bquad[r]])
                    P.op("act", lambda e, r=r: e.activation(out=dst_ap, in_=pt[:, r * 256:(r + 1) * 256].rearrange("p (j t) -> p j t", j=2), func=AF.Copy),
                         reads=[Bquad[r]], writes=[Bdst])
                return stage_a2, stage_b
            return finish

        def phase_ret():
            nrot[0] = 3
            gam = _CONST["gam"]
            with ExitStack() as ls:
                MV = MIXT.rearrange("(c p) t -> p c t", p=128)
                cosT = sb([128, T], F32, ls)
                sinT = sb([128, T], F32, ls)
                Bcs = P.buf()
                with ExitStack() as l2:
                    posi = sb([128, T], I32, l2)
                    ang = sb([128, T], F32, l2)
                    invf = sb([128, 1], F32, l2)
                    Bt = P.buf()
                    P.dma("sp", posi[:], posr, writes=[Bt], key="pos0")
                    P.dma("sp", invf[:], dr["invf"], writes=[Bt], key="pos1")
                    P.op("dve", lambda e: e.tensor_copy(out=ang[:], in_=posi[:]), reads=[Bt], writes=[Bt])
                    P.op("dve", lambda e: e.tensor_scalar(out=ang[:], in0=ang[:], scalar1=invf[:, 0:1], scalar2=None, op0=ALU.mult),
                         reads=[Bt], writes=[Bt])
                    ni = sb([128, T], I32, l2)
                    nf = sb([128, T], F32, l2)
                    C1 = 6.28125
                    C2 = 2.0 * math.pi - C1
                    for dst, shift in ((sinT, 0.0), (cosT, 0.5 * math.pi)):
                        if shift != 0.0:
                            P.op("dve", lambda e, shift=shift: e.tensor_scalar(out=ang[:], in0=ang[:], scalar1=shift, scalar2=None, op0=ALU.add),
                                 reads=[Bt], writes=[Bt])
                        P.op("dve", lambda e: e.tensor_scalar(out=nf[:], in0=ang[:], scalar1=1.0 / (2.0 * math.pi), scalar2=None, op0=ALU.mult),
                             reads=[Bt], writes=[Bt])
                        P.op("dve", lambda e: e.tensor_copy(out=ni[:], in_=nf[:]), reads=[Bt], writes=[Bt])
                        P.op("dve", lambda e: e.tensor_copy(out=nf[:], in_=ni[:]), reads=[Bt], writes=[Bt])
                        P.op("dve", lambda e, dst=dst: e.scalar_tensor_tensor(out=dst[:], in0=nf[:], scalar=-C1, in1=ang[:], op0=ALU.mult, op1=ALU.add),
                             reads=[Bt], writes=[Bcs])
                        P.op("dve", lambda e, dst=dst: e.scalar_tensor_tensor(out=dst[:], in0=nf[:], scalar=-C2, in1=dst[:], op0=ALU.mult, op1=ALU.add),
                             reads=[Bt, Bcs], writes=[Bcs])
                        P.op("dve", lambda e, dst=dst: e.tensor_single_scalar(out=nf[:], in_=dst[:], scalar=math.pi, op=ALU.is_gt),
                             reads=[Bcs], writes=[Bt])
                        P.op("dve", lambda e, dst=dst: e.scalar_tensor_tensor(out=dst[:], in0=nf[:], scalar=-2.0 * math.pi, in1=dst[:], op0=ALU.mult, op1=ALU.add),
                             reads=[Bt, Bcs], writes=[Bcs])
                        P.op("dve", lambda e, dst=dst: e.tensor_single_scalar(out=nf[:], in_=dst[:], scalar=-math.pi, op=ALU.is_lt),
                             reads=[Bcs], writes=[Bt])
                        P.op("dve", lambda e, dst=dst: e.scalar_tensor_tensor(out=dst[:], in0=nf[:], scalar=2.0 * math.pi, in1=dst[:], op0=ALU.mult, op1=ALU.add),
                             reads=[Bt, Bcs], writes=[Bcs])
                        P.op("dve", lambda e, dst=dst: e.tensor_scalar(out=dst[:], in0=dst[:], scalar1=-3.1415925, scalar2=3.1415925, op0=ALU.max, op1=ALU.min),
                             reads=[Bcs], writes=[Bcs])
                        P.op("act", lambda e, dst=dst: e.activation(out=dst[:], in_=dst[:], func=AF.Sin), reads=[Bcs], writes=[Bcs])
                    P.flush()
                retw = sb([128, 1024], F32, ls)
                P.dma("sp", retw[:], dr["retw"][:, 0:1024], writes=[Bcs], key="pos2")
                rm = sb([128, 5, 512], F32, ls)
                Brm = P.buf()
                qT = sb([128, 2, T], BF16, ls)
                kT = sb([128, 2, T], BF16, ls)
                Bq = P.buf()
                Bk = P.buf()
                vtm = sb([128, NTT, 256], BF16, ls)
                Bv = P.buf()
                sg = sb([128, NTT, 256], F32, ls)
                Bsg = P.buf()
                ev = [sb([128, 512], F32, ls) for _ in range(2)]
                od = [sb([128, 512], F32, ls) for _ in range(2)]
                Bev = [P.buf() for _ in range(2)]
                Bod = [P.buf() for _ in range(2)]
                tmp = [sb([128, 512], F32, ls) for _ in range(4)]
                Btmp = [P.buf() for _ in range(4)]
                sgt = [sb([128, 256], F32, ls) for _ in range(2)]
                Bsgt = [P.buf() for _ in range(2)]
                pt = [sb([128, 512], BF16, ls) for _ in range(3)]
                Bptt = [P.buf() for _ in range(3)]
                yT = sb([128, 2, T], BF16, ls)
                ByT = P.buf()
                finish = make_finisher(ls)
                rc = [0, 0]
                import os
                RST = int(os.environ.get("RET_STAGE", "9"))
                NH = int(os.environ.get("RET_HEADS", "4"))
                for h in range(NH if RST > 0 else 0):
                    P.dma("sp", rm[:], dr["rmask"][h].rearrange("p (m n) -> p m n", m=5), writes=[Brm], key="rm")
                    if h == 0:
                        nxt_fm = wload(dr["w0"][0], WSLOT, "pool", "w0_0")
                        nxt_tm = wload(dr["w0"][1], WSLOT, "pool", "w0_1")
                    slot, Bs = nxt_fm
                    sv = slot[:, :].rearrange("p (k n) -> p k n", k=KC)
                    for qk in range(2):
                        dst, Bd = (qT, Bq) if qk == 0 else (kT, Bk)
                        for tg in range(NTG):
                            s = rc[0] % 2
                            rc[0] += 1
                            for eo in range(2):
                                ps, Bp = next_ps()
                                j = qk * 2 + eo
                                for k in range(KC):
                                    P.op("pe", lambda e, ps=ps, sv=sv, k=k, j=j, tg=tg: e.matmul(
                                        ps[:], lhsT=sv[:, k, j * 128:(j + 1) * 128], rhs=A[:, k, tsl(tg)],
                                        start=(k == 0), stop=(k == KC - 1)), reads=[Bs, BA[tg]], writes=[Bp])
                                tgt, Btg = (ev[s], Bev[s]) if eo == 0 else (od[s], Bod[s])
                                P.op("act", lambda e, ps=ps, tgt=tgt: e.activation(out=tgt[:], in_=ps[:], func=AF.Copy),
                                     reads=[Bp], writes=[Btg])
                            E_, O_ = ev[s], od[s]
                            cs, sn = cosT[:, tsl(tg)], sinT[:, tsl(tg)]
                            P.op("dve", lambda e, E_=E_, cs=cs: e.tensor_tensor(out=tmp[0][:], in0=E_[:], in1=cs, op=ALU.mult),
                                 reads=[Bev[s], Bcs], writes=[Btmp[0]])
                            P.op("pool", lambda e, O_=O_, sn=sn: e.tensor_tensor(out=tmp[1][:], in0=O_[:], in1=sn, op=ALU.mult),
                                 reads=[Bod[s], Bcs], writes=[Btmp[1]])
                            P.op("dve", lambda e, dst=dst, tg=tg: e.tensor_tensor(out=dst[:, 0, tsl(tg)], in0=tmp[0][:], in1=tmp[1][:], op=ALU.subtract),
                                 reads=[Btmp[0], Btmp[1]], writes=[Bd])
                            P.op("pool", lambda e, E_=E_, sn=sn: e.tensor_tensor(out=tmp[2][:], in0=E_[:], in1=sn, op=ALU.mult),
                                 reads=[Bev[s], Bcs], writes=[Btmp[2]])
                            P.op("dve", lambda e, O_=O_, cs=cs: e.tensor_tensor(out=tmp[3][:], in0=O_[:], in1=cs, op=ALU.mult),
                                 reads=[Bod[s], Bcs], writes=[Btmp[3]])
                            P.op("pool", lambda e, dst=dst, tg=tg: e.tensor_tensor(out=dst[:, 1, tsl(tg)], in0=tmp[2][:], in1=tmp[3][:], op=ALU.add),
                                 reads=[Btmp[2], Btmp[3]], writes=[Bd])
                    if RST < 2:
                        continue
                    if h + 1 < NH:
                        nxt_fm = wload(dr["w0"][2 * h + 2], WSLOT)
                    wd_cast(0, 2)
                    slot, Bs = nxt_tm
                    sv = slot[:, :].rearrange("p (k n) -> p k n", k=KC)
                    for tt in range(NTT):
                        ps, Bp = next_ps()
                        tg = tt // 4
                        for k in range(KC):
                            P.op("pe", lambda e, ps=ps, sv=sv, k=k, tt=tt: e.matmul(
                                ps[:], lhsT=A[:, k, tt * 128:(tt + 1) * 128], rhs=sv[:, k, :],
                                start=(k == 0), stop=(k == KC - 1)), reads=[Bs, BA[tg]], writes=[Bp])
                        P.op("act", lambda e, ps=ps, tt=tt: e.activation(out=vtm[:, tt, :], in_=ps[:, 0:256], func=AF.Copy),
                             reads=[Bp], writes=[Bv])
                        q = tt % 2
                        P.op("act", lambda e, ps=ps, q=q: e.activation(out=sgt[q][:], in_=ps[:, 256:512], func=AF.Silu),
                             reads=[Bp], writes=[Bsgt[q]])
                        P.op("pool", lambda e, q=q, tt=tt, h=h: e.tensor_tensor(out=sg[:, tt, :], in0=sgt[q][:], in1=retw[:, h * 256:(h + 1) * 256], op=ALU.mult),
                             reads=[Bsgt[q], Bcs], writes=[Bsg])
                    if RST < 3:
                        continue
                    if h + 1 < NH:
                        nxt_tm = wload(dr["w0"][2 * h + 3], WSLOT)
                    wd_cast(0, 2)
                    LA = 2
                    blocks = [(G, J) for G in range(4) for J in range(4 * G + 4)]
                    Sps = {}
                    issued = 0
                    pending = []
                    Ot = [(psf[3 + i], Bpsf[3 + i]) for i in range(4)]
                    for bi, (G, J) in enumerate(blocks):
                        while issued < min(bi + 1 + LA, len(blocks)):
                            G2, J2 = blocks[issued]
                            ps2, Bp2 = next_ps()
                            for eo in range(2):
                                P.op("pe", lambda e, ps2=ps2, eo=eo, J2=J2, G2=G2: e.matmul(
                                    ps2[:], lhsT=kT[:, eo, J2 * 128:(J2 + 1) * 128], rhs=qT[:, eo, tsl(G2)],
                                    start=(eo == 0), stop=(eo == 1)), reads=[Bq, Bk], writes=[Bp2])
                            Sps[issued] = (ps2, Bp2)
                            issued += 1
                        ps, Bp = Sps.pop(bi)
                        if J < 4 * G:
                            scal = float(gam[h] ** ((4 * G - J) * 128))
                            mk = rm[:, 0, :]
                        else:
                            scal = 1.0
                            mk = rm[:, 1 + (J - 4 * G), :]
                        s = rc[1] % 3
                        rc[1] += 1
                        P.op("dve", lambda e, ps=ps, s=s, scal=scal, mk=mk: e.scalar_tensor_tensor(
                            out=pt[s][:], in0=ps[:], scalar=scal, in1=mk, op0=ALU.mult, op1=ALU.mult),
                            reads=[Bp, Brm], writes=[Bptt[s]])
                        for i in range(4):
                            if 4 * G + i >= J:
                                ob, Bob = Ot[i]
                                P.op("pe", lambda e, ob=ob, s=s, i=i, J=J, G=G: e.matmul(
                                    ob[:, 0:256], lhsT=pt[s][:, i * 128:(i + 1) * 128], rhs=vtm[:, J, :],
                                    start=(J == 0), stop=(J == 4 * G + i)), reads=[Bptt[s], Bv], writes=[Bob])
                        pending.sort(key=lambda t: (t[0], t[1]))
                        while pending and pending[0][0] <= bi:
                            pending.pop(0)[2]()
                        if J >= 4 * G:
                            i = J - 4 * G
                            ob, Bob = Ot[i]
                            tt = 4 * G + i
                            sta2, stb = finish(ob[:, 0:256], Bob, sg[:, tt, :], Bsg, yT[:, :, tt * 128:(tt + 1) * 128], ByT)
                            pending.append((bi + 1, 2 * bi, sta2))
                            pending.append((bi + 3, 2 * bi + 1, stb))
                    pending.sort(key=lambda t: (t[0], t[1]))
                    while pending:
                        pending.pop(0)[2]()
                    P.dma("sp", MV[:, 2 * h:2 * h + 2, :], yT[:], reads=[ByT], key="yT")
                do_prefetch()
                P.flush()

        def phase_gla():
            nrot[0] = 3
            with ExitStack() as ls:
                MV = MIXT.rearrange("(c p) t -> p c t", p=128)
                Bg = P.buf()
                retw = sb([128, 1024], F32, ls)
                P.dma("sp", retw[:], dr["retw"][:, 1024:2048], writes=[Bg], key="g0a")
                tri = sb([128, 3, 128], F32, ls)
                P.dma("sp", tri[:], dr["tri"].rearrange("p (m n) -> p m n", m=3), writes=[Bg], key="g0b")
                GU = sb([32, 512], F32, ls)
                P.dma("sp", GU[0:17, :], dr["gu"], writes=[Bg], key="g0c")
                wg = sb([128, KC, 16], BF16, ls)
                P.dma("pool", wg[:], dr["wglr"].rearrange("p (k n) -> p k n", k=KC), writes=[Bg], key="g1")
                glrT = sb([32, T], F32, ls)
                Bglr = P.buf()
                P.op("dve", lambda e: e.memset(glrT[:], 1.0), writes=[Bglr])
                for tg in range(NTG):
                    ps, Bp = next_ps()
                    for k in range(KC):
                        P.op("pe", lambda e, ps=ps, k=k, tg=tg: e.matmul(ps[0:16, :], lhsT=wg[:, k, :], rhs=A[:, k, tsl(tg)],
                                                                       start=(k == 0), stop=(k == KC - 1)),
                             reads=[Bg, BA[tg]], writes=[Bp])
                    P.op("act", lambda e, ps=ps, tg=tg: e.activation(out=glrT[0:16, tsl(tg)], in_=ps[0:16, :], func=AF.Copy),
                         reads=[Bp], writes=[Bglr])
                L_ = tri[:, 0, :]
                U_ = tri[:, 1, :]
                M2 = tri[:, 2, :]
                sp_tm = sb([128, 4, 512], F32, ls)
                Bsp = P.buf()
                ext = [sb([128, 512], F32, ls) for _ in range(2)]
                Bext = [P.buf() for _ in range(2)]

                epos = sb([128, 4, 512], F32, ls)
                Bep = P.buf()
                qfull = sb([128, 4, 512], BF16, ls)
                Bqf = P.buf()
                qa = sb([128, 4, 4, 128], BF16, ls)
                qb = sb([128, 4, 4, 128], BF16, ls)
                Bqa = P.buf()
                ktl = sb([128, 4, 512], BF16, ls)
                Bkt = P.buf()
                kk = sb([128, 4, 512], BF16, ls)
                Bkk = P.buf()
                vtm = sb([128, 4, 1024], BF16, ls)
                Bv = P.buf()
                sgg = sb([128, 4, 1024], BF16, ls)
                Bsgg = P.buf()
                St = sb([128, 4, 256], F32, ls)
                BSt = [P.buf() for _ in range(4)]
                Sbf = [sb([128, 4, 256], BF16, ls) for _ in range(2)]
                BSbf = [[P.buf() for _ in range(4)] for _ in range(2)]
                ptile = [sb([128, 128], BF16, ls) for _ in range(4)]
                Bpti = [P.buf() for _ in range(4)]
                yTg = sb([128, 8, 512], BF16, ls)
                ByT = P.buf()
                finish = make_finisher(ls)
                P.op("pool", lambda e: e.memset(qa[:], 0.0), writes=[Bqa])
                P.op("pool", lambda e: e.memset(qb[:], 0.0), writes=[Bqa])
                P.op("pool", lambda e: e.memset(St[:], 0.0), writes=BSt)
                P.op("pool", lambda e: e.memset(Sbf[0][:], 0.0), writes=BSbf[0])
                gws = wstream([(dr["w0"][8 + i], WSLOT, "pool", f"w0_{8 + i}" if (t_ == 0 and i < 2) else None) for t_ in range(NTG) for i in range(6)])
                sbi = [0, 0, 0, 0]
                gpend = []
                cnt = [0, 0]
                for tg in range(NTG):
                    for u in range(4):
                        tt = 4 * tg + u
                        ps, Bp = next_ps()
                        P.op("pe", lambda e, ps=ps, tt=tt: e.matmul(ps[:], lhsT=glrT[0:17, tt * 128:(tt + 1) * 128], rhs=GU[0:17, :],
                                                                   start=True, stop=True), reads=[Bglr, Bg], writes=[Bp])
                        q = cnt[0] % 2
                        cnt[0] += 1
                        P.op("act", lambda e, ps=ps, q=q: e.activation(out=ext[q][:], in_=ps[:], func=AF.Exp, scale=-1.0),
                             reads=[Bp], writes=[Bext[q]])
                        P.op("act", lambda e, q=q, u=u: e.activation(out=sp_tm[:, u, :], in_=ext[q][:], func=AF.Ln, bias=cbias[:, 1:2]),
                             reads=[Bext[q], Bconst], writes=[Bsp])
                    for h in range(4):
                        ps, Bp = next_ps()
                        for u in range(4):
                            P.op("pe", lambda e, ps=ps, u=u, h=h: e.matmul(ps[:, u * 128:(u + 1) * 128], lhsT=sp_tm[:, u, h * 128:(h + 1) * 128],
                                                                          rhs=L_, start=True, stop=True), reads=[Bsp, Bg], writes=[Bp])
                        P.op("act", lambda e, ps=ps, h=h: e.activation(out=epos[:, h, :], in_=ps[:], func=AF.Exp), reads=[Bp], writes=[Bep])
                    slot, Bs = gws(tg * 6 + 0)
                    sv = slot[:, :].rearrange("p (k n) -> p k n", k=KC)
                    for h in range(4):
                        ps, Bp = next_ps()
                        for k in range(KC):
                            P.op("pe", lambda e, ps=ps, sv=sv, k=k, h=h, tg=tg: e.matmul(
                                ps[:], lhsT=sv[:, k, h * 128:(h + 1) * 128], rhs=A[:, k, tsl(tg)], start=(k == 0), stop=(k == KC - 1)),
                                reads=[Bs, BA[tg]], writes=[Bp])
                        P.op("dve", lambda e, ps=ps, h=h: e.scalar_tensor_tensor(out=qfull[:, h, :], in0=ps[:], scalar=128 ** -0.5, in1=epos[:, h, :],
                                                                                op0=ALU.mult, op1=ALU.mult), reads=[Bp, Bep], writes=[Bqf])
                        qv = qfull[:, h, :].rearrange("p (u t) -> p u t", u=4)
                        P.op("pool", lambda e, qv=qv, h=h: e.tensor_copy(out=qa[:, h, :, 0:64], in_=qv[:, :, 0:64]), reads=[Bqf], writes=[Bqa])
                        P.op("pool", lambda e, qv=qv, h=h: e.tensor_copy(out=qb[:, h, :, 64:128], in_=qv[:, :, 64:128]), reads=[Bqf], writes=[Bqa])
                    slotk, Bsk = gws(tg * 6 + 1)
                    svk = slotk[:, :].rearrange("p (k n) -> p k n", k=KC)
                    for h in range(4):
                        ps, Bp = next_ps()
                        for k in range(KC):
                            P.op("pe", lambda e, ps=ps, svk=svk, k=k, h=h, tg=tg: e.matmul(
                                ps[:], lhsT=svk[:, k, h * 128:(h + 1) * 128], rhs=A[:, k, tsl(tg)], start=(k == 0), stop=(k == KC - 1)),
                                reads=[Bsk, BA[tg]], writes=[Bp])
                        q = cnt[0] % 2
                        cnt[0] += 1
                        P.op("dve", lambda e, q=q, h=h: e.reciprocal(out=ext[q][:], in_=epos[:, h, :]), reads=[Bep], writes=[Bext[q]])
                        P.op("dve", lambda e, ps=ps, h=h, q=q: e.tensor_tensor(out=ktl[:, h, :], in0=ps[:], in1=ext[q][:], op=ALU.mult),
                             reads=[Bp, Bext[q]], writes=[Bkt])
                    for u in range(4):
                        tt = 4 * tg + u
                        ps, Bp = next_ps()
                        for k in range(KC):
                            P.op("pe", lambda e, ps=ps, svk=svk, k=k, tt=tt: e.matmul(
                                ps[:], lhsT=A[:, k, tt * 128:(tt + 1) * 128], rhs=svk[:, k, :], start=(k == 0), stop=(k == KC - 1)),
                                reads=[Bsk, BA[tg]], writes=[Bp])
                        ps2, Bp2 = next_ps()
                        P.op("pe", lambda e, ps2=ps2, u=u: e.matmul(ps2[:], lhsT=U_, rhs=sp_tm[:, u, :], start=True, stop=True),
                             reads=[Bsp, Bg], writes=[Bp2])
                        q = cnt[0] % 2
                        cnt[0] += 1
                        P.op("act", lambda e, ps2=ps2, q=q: e.activation(out=ext[q][:], in_=ps2[:], func=AF.Exp), reads=[Bp2], writes=[Bext[q]])
                        P.op("dve", lambda e, ps=ps, q=q, u=u: e.tensor_tensor(out=kk[:, u, :], in0=ps[:], in1=ext[q][:], op=ALU.mult),
                             reads=[Bp, Bext[q]], writes=[Bkk])
                    for half in range(2):
                        slot, Bs = gws(tg * 6 + 2 + half)
                        sv = slot[:, :].rearrange("p (k n) -> p k n", k=KC)
                        for u in range(4):
                            tt = 4 * tg + u
                            ps, Bp = next_ps()
                            for k in range(KC):
                                P.op("pe", lambda e, ps=ps, sv=sv, k=k, tt=tt: e.matmul(
                                    ps[:], lhsT=A[:, k, tt * 128:(tt + 1) * 128], rhs=sv[:, k, :], start=(k == 0), stop=(k == KC - 1)),
                                    reads=[Bs, BA[tg]], writes=[Bp])
                            P.op("act", lambda e, ps=ps, u=u, half=half: e.activation(out=vtm[:, u, half * 512:(half + 1) * 512], in_=ps[:], func=AF.Copy),
                                 reads=[Bp], writes=[Bv])
                    for half in range(2):
                        slot, Bs = gws(tg * 6 + 4 + half)
                        sv = slot[:, :].rearrange("p (k n) -> p k n", k=KC)
                        for u in range(4):
                            tt = 4 * tg + u
                            ps, Bp = next_ps()
                            for k in range(KC):
                                P.op("pe", lambda e, ps=ps, sv=sv, k=k, tt=tt: e.matmul(
                                    ps[:], lhsT=A[:, k, tt * 128:(tt + 1) * 128], rhs=sv[:, k, :], start=(k == 0), stop=(k == KC - 1)),
                                    reads=[Bs, BA[tg]], writes=[Bp])
                            q = cnt[0] % 2
                            cnt[0] += 1
                            P.op("act", lambda e, ps=ps, q=q: e.activation(out=ext[q][:], in_=ps[:], func=AF.Silu), reads=[Bp], writes=[Bext[q]])
                            P.op("pool", lambda e, q=q, u=u, half=half: e.tensor_tensor(out=sgg[:, u, half * 512:(half + 1) * 512], in0=ext[q][:],
                                                                                        in1=retw[:, half * 512:(half + 1) * 512], op=ALU.mult),
                                 reads=[Bext[q], Bg], writes=[Bsgg])
                    for u in range(4):
                        tt = 4 * tg + u
                        psS, BpS = psf[0], Bpsf[0]
                        for h in range(4):
                            P.op("pe", lambda e, u=u, h=h: e.matmul(psS[:, h * 128:(h + 1) * 128], lhsT=ktl[:, h, u * 128:(u + 1) * 128],
                                                                   rhs=qfull[:, h, u * 128:(u + 1) * 128], start=True, stop=True),
                                 reads=[Bkt, Bqf], writes=[BpS])
                        for h in range(4):
                            P.op("dve", lambda e, h=h: e.tensor_tensor(out=ptile[h][:], in0=psS[:, h * 128:(h + 1) * 128], in1=M2, op=ALU.mult),
                                 reads=[BpS, Bg], writes=[Bpti[h]])
                        pos_ = [(psf[3 + h], Bpsf[3 + h]) for h in range(4)]
                        for h in range(4):
                            po, Bpo = pos_[h]
                            vv = vtm[:, u, h * 256:(h + 1) * 256]
                            cur = sbi[h]
                            P.op("pe", lambda e, po=po, h=h, vv=vv: e.matmul(po[:, 0:256], lhsT=ptile[h][:], rhs=vv, start=True, stop=False),
                                 reads=[Bpti[h], Bv], writes=[Bpo])
                            P.op("pe", lambda e, po=po, u=u, h=h, cur=cur: e.matmul(po[:, 0:256], lhsT=qa[:, h, u, :], rhs=Sbf[cur][:, h, :],
                                                                                 start=False, stop=False),
                                 reads=[Bqa, BSbf[cur][h]], writes=[Bpo])
                        for half in range(2):
                            lo = half * 64
                            for h in range(4):
                                pst, Bpst = psf[1 + h // 2], Bpsf[1 + h // 2]
                                off = (h % 2) * 256
                                P.op("pe", lambda e, pst=pst, off=off, u=u, h=h, lo=lo: e.matmul(
                                    pst[:, off:off + 256], lhsT=kk[lo:lo + 64, u, h * 128:(h + 1) * 128], rhs=vtm[lo:lo + 64, u, h * 256:(h + 1) * 256],
                                    start=True, stop=True), reads=[Bkk, Bv], writes=[Bpst])
                            for h in range(4):
                                pst, Bpst = psf[1 + h // 2], Bpsf[1 + h // 2]
                                off = (h % 2) * 256
                                col = u * 128 + lo + 63
                                P.op("dve", lambda e, pst=pst, off=off, h=h, col=col: e.scalar_tensor_tensor(
                                    out=St[:, h, :], in0=St[:, h, :], scalar=epos[:, h, col:col + 1], in1=pst[:, off:off + 256],
                                    op0=ALU.mult, op1=ALU.add), reads=[Bpst, Bep, BSt[h]], writes=[BSt[h]])
                                nxt = 1 - sbi[h]
                                P.op("act", lambda e, h=h, nxt=nxt: e.activation(out=Sbf[nxt][:, h, :], in_=St[:, h, :], func=AF.Copy),
                                     reads=[BSt[h]], writes=[BSbf[nxt][h]])
                                sbi[h] = nxt
                            if half == 0:
                                for h in range(4):
                                    po, Bpo = pos_[h]
                                    nxt = sbi[h]
                                    P.op("pe", lambda e, po=po, u=u, h=h, nxt=nxt: e.matmul(po[:, 0:256], lhsT=qb[:, h, u, :], rhs=Sbf[nxt][:, h, :],
                                                                                         start=False, stop=True),
                                         reads=[Bqa, BSbf[nxt][h]], writes=[Bpo])
                                for h in range(4):
                                    po, Bpo = pos_[h]
                                    gi = cnt[1]
                                    cnt[1] += 1
                                    gpend.sort(key=lambda t: (t[0], t[1]))
                                    while gpend and gpend[0][0] <= gi:
                                        gpend.pop(0)[2]()
                                    sta2, stb = finish(po[:, 0:256], Bpo, sgg[:, u, h * 256:(h + 1) * 256], Bsgg,
                                                       yTg[:, 2 * h:2 * h + 2, u * 128:(u + 1) * 128], ByT)
                                    gpend.append((gi + 2, 2 * gi, sta2))
                                    gpend.append((gi + 4, 2 * gi + 1, stb))
                    gpend.sort(key=lambda t: (t[0], t[1]))
                    while gpend:
                        gpend.pop(0)[2]()
                    P.dma("sp", MV[:, 8:16, tsl(tg)], yTg[:], reads=[ByT], key="yTg")
                do_prefetch()
                P.flush()

        def phase_attn():
            nrot[0] = 3
            with ExitStack() as ls:
                MV = MIXT.rearrange("(c p) t -> p c t", p=128)
                am = sb([128, 9, 512], BF16, ls)
                Bam = P.buf()
                P.dma("pool", am[:], dr["amask"].rearrange("p (m n) -> p m n", m=9), writes=[Bam], key="am")
                qTs = [sb([128, T], BF16, ls) for _ in range(2)]
                kTs = [sb([128, T], BF16, ls) for _ in range(2)]
                Bqs = [P.buf() for _ in range(2)]
                Bks = [P.buf() for _ in range(2)]
                vtms = [sb([128, NTT, 130], BF16, ls) for _ in range(2)]
                Bvs = [P.buf() for _ in range(2)]
                for i in range(2):
                    P.op("dve", lambda e, i=i: e.memset(vtms[i][:], 0.0), writes=[Bvs[i]])
                    P.op("dve", lambda e, i=i: e.memset(vtms[i][:, :, 128:129], 1.0), writes=[Bvs[i]])
                ex = [sb([128, 512], BF16, ls) for _ in range(3)]
                Bex = [P.buf() for _ in range(3)]
                pt = [sb([128, 512], BF16, ls) for _ in range(3)]
                Bptt = [P.buf() for _ in range(3)]
                rden = [sb([128, 1], F32, ls) for _ in range(2)]
                Brd = [P.buf() for _ in range(2)]
                yb = [sb([128, 128], BF16, ls) for _ in range(2)]
                Byb = [P.buf() for _ in range(2)]
                yT = [sb([128, T], BF16, ls) for _ in range(2)]
                ByT = [P.buf() for _ in range(2)]
                Bhalf = [Bpsb[0]] * 2
                cnt = [0, 0]
                LA = 2
                blocks = [(G, J) for G in range(4) for J in range(4 * G + 4)]
                nxt_w = wload(dr["w1"][0], 6144, "pool", "w1_0")
                for h in range(16):
                    slot, Bs = nxt_w
                    hb = h % 2
                    qT, kT, vtm, Bq, Bk, Bv = qTs[hb], kTs[hb], vtms[hb], Bqs[hb], Bks[hb], Bvs[hb]
                    sv = slot[:, 0:6144].rearrange("p (k n) -> p k n", k=KC)
                    for j in range(2):
                        dst, Bd = (qT, Bq) if j == 0 else (kT, Bk)
                        sc = 128 ** -0.5 if j == 0 else 1.0
                        for tg in range(NTG):
                            ps, Bp = next_ps()
                            for k in range(KC):
                                P.op("pe", lambda e, ps=ps, sv=sv, k=k, j=j, tg=tg: e.matmul(
                                    ps[:], lhsT=sv[:, k, j * 128:(j + 1) * 128], rhs=A[:, k, tsl(tg)], start=(k == 0), stop=(k == KC - 1)),
                                    reads=[Bs, BA[tg]], writes=[Bp])
                            P.op("act", lambda e, ps=ps, dst=dst, tg=tg, sc=sc: e.activation(out=dst[:, tsl(tg)], in_=ps[:], func=AF.Copy, scale=sc),
                                 reads=[Bp], writes=[Bd])
                    for g4 in range(4):
                        ps, Bp = next_ps()
                        for u in range(4):
                            tt = g4 * 4 + u
                            for k in range(KC):
                                P.op("pe", lambda e, ps=ps, sv=sv, k=k, tt=tt, u=u: e.matmul(
                                    ps[:, u * 128:(u + 1) * 128], lhsT=A[:, k, tt * 128:(tt + 1) * 128], rhs=sv[:, k, 256:384],
                                    start=(k == 0), stop=(k == KC - 1)), reads=[Bs, BA[g4]], writes=[Bp])
                        P.op("act", lambda e, ps=ps, g4=g4, vtm=vtm: e.activation(out=vtm[:, g4 * 4:(g4 + 1) * 4, 0:128],
                                                                                 in_=ps[:].rearrange("p (u n) -> p u n", u=4), func=AF.Copy),
                             reads=[Bp], writes=[Bv])
                    if h + 1 < 16:
                        nxt_w = wload(dr["w1"][h + 1], 6144, "pool", f"w1_{h + 1}")
                    wd_cast(1, 1)
                    yq = h % 2
                    Sps = {}
                    issued = 0
                    apend = []
                    for bi, (G, J) in enumerate(blocks):
                        while issued < min(bi + 1 + LA, len(blocks)):
                            G2, J2 = blocks[issued]
                            ps2, Bp2 = next_ps()
                            P.op("pe", lambda e, ps2=ps2, J2=J2, G2=G2, kT=kT, qT=qT: e.matmul(
                                ps2[:], lhsT=kT[:, J2 * 128:(J2 + 1) * 128], rhs=qT[:, tsl(G2)], start=True, stop=True),
                                reads=[Bq, Bk], writes=[Bp2])
                            Sps[issued] = (ps2, Bp2)
                            issued += 1
                        ps, Bp = Sps.pop(bi)
                        Ot = [(psf[3 + i], Bpsf[3 + i]) for i in range(4)]
                        s = cnt[0] % 3
                        cnt[0] += 1
                        P.op("act", lambda e, ps=ps, s=s: e.activation(out=ex[s][:], in_=ps[:], func=AF.Exp), reads=[Bp], writes=[Bex[s]])
                        mi = min(4 * G - J, 5) + 3
                        P.op("dve", lambda e, s=s, mi=mi: e.tensor_tensor(out=pt[s][:], in0=ex[s][:], in1=am[:, mi, :], op=ALU.mult),
                             reads=[Bex[s], Bam], writes=[Bptt[s]])
                        apend.sort(key=lambda t: (t[0], t[1]))
                        while apend and apend[0][0] <= bi:
                            apend.pop(0)[2]()
                        for i in range(4):
                            if 4 * G + i >= J:
                                ob, Bob = Ot[i]
                                P.op("pe", lambda e, ob=ob, s=s, i=i, J=J, G=G, vtm=vtm: e.matmul(
                                    ob[:, 0:130], lhsT=pt[s][:, i * 128:(i + 1) * 128], rhs=vtm[:, J, :],
                                    start=(J == 0), stop=(J == 4 * G + i)), reads=[Bptt[s], Bv], writes=[Bob])
                        if J >= 4 * G:
                            i = J - 4 * G
                            ob, Bob = Ot[i]
                            q = cnt[1] % 2
                            cnt[1] += 1
                            hf = G % 2

                            def fa(ob=ob, Bob=Bob, q=q):
                                P.op("dve", lambda e, ob=ob, q=q: e.reciprocal(out=rden[q][:], in_=ob[:, 128:129]),
                                     reads=[Bob], writes=[Brd[q]])
                                P.op("dve", lambda e, ob=ob, q=q: e.tensor_scalar(out=yb[q][:], in0=ob[:, 0:128], scalar1=rden[q][:, 0:1], scalar2=None,
                                                                                 op0=ALU.mult),
                                     reads=[Bob, Brd[q]], writes=[Byb[q]])

                            def fb(i=i, q=q, hf=hf, G=G, yq=yq):
                                ptp = psb[0]
                                P.op("pe", lambda e, ptp=ptp, i=i, q=q, hf=hf: e.transpose(out=ptp[:, hf * 512 + i * 128:hf * 512 + (i + 1) * 128],
                                                                                          in_=yb[q][:], identity=identb[:]),
                                     reads=[Byb[q], Bconst], writes=[Bhalf[hf]])
                                if i == 3:
                                    P.op("dve", lambda e, ptp=ptp, G=G, yq=yq, hf=hf: e.tensor_copy(out=yT[yq][:, tsl(G)], in_=ptp[:, hf * 512:(hf + 1) * 512]),
                                         reads=[Bhalf[hf]], writes=[ByT[yq]])
                            apend.append((bi + 1, 2 * bi, fa))
                            apend.append((bi + 2, 2 * bi + 1, fb))
                    apend.sort(key=lambda t: (t[0], t[1]))
                    while apend:
                        apend.pop(0)[2]()
                    P.dma("sp", MV[:, h, :], yT[yq][:], reads=[ByT[yq]], key=f"yTa{yq}")
                do_prefetch()
                P.flush()

        allp = ["norm0a", "ret", "gla", "wout0", "ffn0", "ple0",
                "norm1a", "attn", "wout1", "ffn1", "ple1", "final"]
        run = allp if phases is None else phases
        PF = {
            "norm0a": [("w0_0", dr["w0"][0], WSLOT, "pool"), ("w0_1", dr["w0"][1], WSLOT, "pool")],
            "ret": [("w0_8", dr["w0"][8], WSLOT, "pool"), ("w0_9", dr["w0"][9], WSLOT, "pool")],
            "gla": [("wo0_0", dr["wo0"][0], WSLOT, "pool"), ("wo0_1", dr["wo0"][1], WSLOT, "pool")],
            "wout0": [("wgu0_0", dr["wgu0"][0], WSLOT, "pool"), ("wgu0_1", dr["wgu0"][1], WSLOT, "pool")],
            "ffn0": [("wpg0_0", dr["wpg0"][0], WSLOT, "pool"), ("wpg0_1", dr["wpg0"][1], WSLOT, "pool")],
            "ple0": [("w1_0", dr["w1"][0], 6144, "pool"), ("w1_1", dr["w1"][1], 6144, "pool")],
            "attn": [("wo1_0", dr["wo1"][0], WSLOT, "pool"), ("wo1_1", dr["wo1"][1], WSLOT, "pool")],
            "wout1": [("wgu1_0", dr["wgu1"][0], WSLOT, "pool"), ("wgu1_1", dr["wgu1"][1], WSLOT, "pool")],
            "ffn1": [("wpg1_0", dr["wpg1"][0], WSLOT, "pool"), ("wpg1_1", dr["wpg1"][1], WSLOT, "pool")],
        }
        for pi, ph in enumerate(run):
            nxt_ph = [p2 for p2 in run[pi + 1:] if not p2.startswith("norm")][:1]
            want = {"norm0a": "ret", "ret": "gla", "gla": "wout0", "wout0": "ffn0", "ffn0": "ple0", "ple0": "attn",
                    "attn": "wout1", "wout1": "ffn1", "ffn1": "ple1"}.get(ph)
            if want is not None and nxt_ph and nxt_ph[0] == want:
                pending_pf.extend(PF[ph])
            if ph == "norm0a":
                phase_norm(xT, 0)
            elif ph == "ret":
                phase_ret()
            elif ph == "gla":
                phase_gla()
            elif ph == "wout0":
                chain_open()
                phase_wout(dr["wo0"], xT, 1, "wo0")
            elif ph == "ffn0":
                if wd_cast(0, KC):
                    P.flush()
                phase_ffn(dr["wgu0"], WDB[0], 2, "wgu0")
            elif ph == "ple0":
                phase_ple(dr["wpg0"], dr["wpp0"], pT[0], "wpg0")
                if not (pi + 1 < len(run) and run[pi + 1] == "norm1a"):
                    chain_close()
            elif ph == "norm1a":
                if chain[0] is not None:
                    phase_norm(R, 3, pre_rstd=True)
                    chain_close()
                else:
                    phase_norm(R, 3)
            elif ph == "attn":
                phase_attn()
            elif ph == "wout1":
                chain_open()
                phase_wout(dr["wo1"], R, 4, "wo1")
            elif ph == "ffn1":
                if wd_cast(1, KC):
                    P.flush()
                phase_ffn(dr["wgu1"], WDB[1], 5, "wgu1")
            elif ph == "ple1":
                phase_ple(dr["wpg1"], dr["wpp1"], pT[1], "wpg1")
                if not (pi + 1 < len(run) and run[pi + 1] == "final"):
                    chain_close()
            elif ph == "final":
                if chain[0] is not None:
                    phase_norm(R, 6, to_out=outT, pre_rstd=True)
                    chain_close()
                else:
                    phase_norm(R, 6, to_out=outT)
            elif ph == "copyx":
                with ExitStack() as ls:
                    t = sb([128, KC, 512], F32, ls)
                    Bt = P.buf()
                    for tg in range(NTG):
                        P.dma("sp", t[:], xT.rearrange("(c p) t -> p c t", p=128)[:, :, tsl(tg)], writes=[Bt], key="cx", waw=True)
                        P.dma("sp", R.rearrange("(c p) t -> p c t", p=128)[:, :, tsl(tg)], t[:], reads=[Bt], key="cy")
                    P.flush()
    return nc


_PROG = None


def kernel(**inputs):
    global _PROG
    inp = {k: np.asarray(v) for k, v in inputs.items()}
    sh = _prep_shared(inp)
    if _PROG is None:
        _PROG = build_program()
    in_maps = []
    for b in range(NCORES):
        m = dict(sh)
        m["xT"] = np.ascontiguousarray(inp["x"][b].T.astype(np.float32))
        m["pT"] = np.ascontiguousarray(np.transpose(inp["p"][:, b], (0, 2, 1)).astype(np.float32))
        m["posr"] = np.ascontiguousarray(np.broadcast_to(inp["positions"][b].astype(np.int32)[None, :], (128, T)))
        in_maps.append(m)
    res = run_bass_kernel_spmd(_PROG, in_maps, core_ids=list(range(NCORES)))
    out = np.stack([np.ascontiguousarray(res.results[b]["outT"].T) for b in range(NCORES)], axis=0)
    return out.astype(np.float32)
```

```python
import math
from contextlib import ExitStack

import numpy as np
import concourse.bass as bass
import concourse.mybir as mybir
from concourse.bass_utils import run_bass_kernel_spmd

F32 = mybir.dt.float32
BF16 = mybir.dt.bfloat16
I32 = mybir.dt.int32
AF = mybir.ActivationFunctionType
ALU = mybir.AluOpType
AX = mybir.AxisListType

T = 2048
D = 2048
KC = 16
NTG = 4
NTT = 16
FF = 5632
FKC = 44
EPS = 1e-6
NCORES = 8
WSLOT = 8192
NWS = 2

ENGS = ("pe", "act", "dve", "pool", "sp")
BLOCKFN = {"pe": "tensor", "act": "scalar", "dve": "vector", "pool": "gpsimd", "sp": "sync"}
SAME_SYNC = True


class Op:
    __slots__ = ("eng", "fn", "deps", "needs_inc", "ticket", "key", "done")

    def __init__(self, eng, fn, key=None):
        self.eng = eng
        self.fn = fn
        self.deps = set()
        self.needs_inc = False
        self.ticket = None
        self.key = key
        self.done = False


class Buf:
    __slots__ = ("name", "w", "r", "war")

    def __init__(self, name=""):
        self.name = name
        self.w = []
        self.r = []
        self.war = []

    def reset(self):
        self.w = []
        self.r = []
        self.war = []


def _add(lst, op):
    if op.key is None:
        for i, o in enumerate(lst):
            if o.key is None and o.eng == op.eng:
                lst[i] = op
                return
    lst.append(op)


class Prog:
    def __init__(self, nc, stack):
        self.nc = nc
        self.stack = stack
        self.ops = {e: [] for e in ENGS}
        self.nsem = 0
        self.esem = {}
        self.ecount = {}
        for e in ENGS:
            self._new_esem(e)
        self.dsem = {}
        self.dcount = {}
        self.waited = {e: {} for e in ENGS}
        self.bufs = []
        self.persist_keys = set()
        self.persist_bufs = set()

    def _new_esem(self, e):
        self.nsem += 1
        self.esem[e] = self.stack.enter_context(self.nc.semaphore(f"se{self.nsem}"))
        self.ecount[e] = 0

    def buf(self, name=""):
        b = Buf(name)
        self.bufs.append(b)
        return b

    def _record(self, op, reads, writes, waw):
        for b in reads:
            for o in b.w:
                if o is not op:
                    op.deps.add(o)
            _add(b.r, op)
        for b in writes:
            rd = [o for o in b.r if o is not op]
            if rd:
                for o in rd:
                    op.deps.add(o)
                for o in b.w:
                    if o is not op:
                        op.deps.add(o)
                b.war = rd
                b.r = [o for o in b.r if o is op]
                b.w = [op]
            else:
                for o in b.war:
                    op.deps.add(o)
                if waw:
                    for o in b.w:
                        if o is not op:
                            op.deps.add(o)
                _add(b.w, op)
        for d in op.deps:
            d.needs_inc = True
        self.ops[op.eng].append(op)

    def op(self, eng, fn, reads=(), writes=(), waw=True):
        o = Op(eng, fn)
        self._record(o, reads, writes, waw)
        return o

    def dma(self, eng, out, in_, reads=(), writes=(), key="d", waw=False):
        if key not in self.dsem:
            self.nsem += 1
            self.dsem[key] = self.stack.enter_context(self.nc.semaphore(f"sd{self.nsem}"))
            self.dcount[key] = 0
        o = Op(eng, lambda e, out=out, in_=in_: e.dma_start(out=out, in_=in_), key=key)
        self.dcount[key] += 16
        o.ticket = self.dcount[key]
        self._record(o, reads, writes, waw)
        return o

    def flush(self):
        nc = self.nc
        for e in ENGS:
            last = None
            for o in self.ops[e]:
                if o.key is None:
                    last = o
            if last is not None:
                last.needs_inc = True
            for o in self.ops[e]:
                if o.key is None and o.needs_inc:
                    self.ecount[e] += 1
                    o.ticket = self.ecount[e]
        finals = {e: (self.esem[e], self.ecount[e]) for e in ENGS}
        dfinals = {k: (self.dsem[k], self.dcount[k]) for k in self.dsem if k not in self.persist_keys}
        with nc.Block() as block:
            for e in ENGS:
                def body(engine, e=e):
                    wd = self.waited[e]
                    for o in self.ops[e]:
                        waits = {}
                        for d in o.deps:
                            if d.done:
                                continue
                            if d.key is None:
                                if d.eng == e and (e == "pe" or not SAME_SYNC):
                                    continue
                                sem = self.esem[d.eng]
                            else:
                                sem = self.dsem[d.key]
                            v = d.ticket
                            if waits.get(sem, 0) < v:
                                waits[sem] = v
                        for sem, v in waits.items():
                            if wd.get(sem, 0) >= v:
                                continue
                            engine.wait_ge(sem, v)
                            wd[sem] = v
                        ins = o.fn(engine)
                        if o.key is not None:
                            ins.then_inc(self.dsem[o.key], 16)
                        elif o.needs_inc:
                            ins.then_inc(self.esem[e], 1)
                    for e2 in ENGS:
                        sem, v = finals[e2]
                        if e2 == e or v == 0:
                            continue
                        if wd.get(sem, 0) < v:
                            engine.wait_ge(sem, v)
                            wd[sem] = v
                    for k, (sem, v) in dfinals.items():
                        if v and wd.get(sem, 0) < v:
                            engine.wait_ge(sem, v)
                            wd[sem] = v
                getattr(block, BLOCKFN[e])(body)
        for e in ENGS:
            for o in self.ops[e]:
                if o.key is None or o.key not in self.persist_keys:
                    o.done = True
                o.deps = None
                o.fn = None
            self.ops[e] = []
            if self.ecount[e] > 12000:
                self._new_esem(e)
        for b in self.bufs:
            if b not in self.persist_bufs:
                b.reset()
            else:
                b.r = []
                b.war = []
                b.w = [o for o in b.w if not o.done]


def _blk(W, nbw):
    K, N = W.shape
    kc = K // 128
    nb = N // nbw
    return np.ascontiguousarray(W.reshape(kc, 128, nb, nbw).transpose(2, 1, 0, 3)).reshape(nb, 128, kc * nbw)


def _host_constants():
    c = {}
    half = 128
    lin = np.linspace(0.0, 1.0, half, dtype=np.float32)
    inv = (np.float32(1.0) / np.power(np.float32(10000.0), lin)).astype(np.float32)
    c["invf"] = inv.reshape(128, 1)
    c["ident"] = np.eye(128, dtype=np.float32)
    hh = np.arange(4, dtype=np.float64)
    gam = 1.0 - np.exp2(-5.0 - hh)
    jj = np.arange(128)[:, None].astype(np.float64)
    ii = np.arange(512)[None, :].astype(np.float64)
    rm = np.zeros((4, 128, 5, 512), np.float32)
    for h in range(4):
        rm[h, :, 0, :] = (gam[h] ** (ii - jj)) * 256 ** -0.5
        for m in range(4):
            d = ii - (m * 128 + jj)
            rm[h, :, 1 + m, :] = np.where(d >= 0, gam[h] ** np.maximum(d, 0), 0.0) * 256 ** -0.5
    c["rmask"] = rm.reshape(4, 128, 5 * 512)
    c["gam"] = gam
    tp = np.arange(128)[:, None]
    tt = np.arange(128)[None, :]
    same = (tp // 64) == (tt // 64)
    L = np.where(same & (tp <= tt), -1.0 / 16.0, 0.0)
    U = np.where(same & (tp > tt), -1.0 / 16.0, 0.0)
    M2 = np.where(same & (tp <= tt), 1.0, 0.0)
    c["tri"] = np.concatenate([L, U, M2], axis=1).astype(np.float32)
    am = np.zeros((128, 9, 512), np.float32)
    for idx, dl in enumerate([-3, -2, -1, 0, 1, 2, 3, 4, 5]):
        d = (dl * 128 + ii - jj).astype(np.int64)
        cnt = ((d >= 0) & (d <= 128)).astype(np.float32) + ((d >= 0) & (d % 4 == 0) & (d <= 512)).astype(np.float32) \
            + ((d >= 0) & (d % 16 == 0)).astype(np.float32)
        am[:, idx, :] = cnt
    c["amask"] = am.reshape(128, 9 * 512)
    return c


_CONST = None


def _prep_shared(inp):
    global _CONST
    if _CONST is None:
        _CONST = _host_constants()
    c = _CONST
    sh = {"invf": c["invf"], "ident": c["ident"], "rmask": c["rmask"], "tri": c["tri"], "amask": c["amask"]}
    f = np.float32
    nws = [inp["attn_norm_w"][0], inp["ffn_norm_w"][0], inp["ple_norm_w"][0],
           inp["attn_norm_w"][1], inp["ffn_norm_w"][1], inp["ple_norm_w"][1], inp["final_norm_w"]]
    sh["nw"] = np.ascontiguousarray(np.stack([np.asarray(v, f).reshape(16, 128).T for v in nws], axis=1)).reshape(128, 7 * 16)
    w_in = np.asarray(inp["ab_w_in"][0], f)
    perm = np.concatenate([np.arange(0, 256, 2), np.arange(1, 256, 2)])
    cols = []
    for h in range(4):
        cols.append(w_in[:, 0 + h * 256 + perm])
        cols.append(w_in[:, 1024 + h * 256 + perm])
        cols.append(w_in[:, 2048 + h * 256: 2048 + (h + 1) * 256])
        cols.append(w_in[:, 3072 + h * 256: 3072 + (h + 1) * 256])
    cols.append(w_in[:, 4096:4608])
    cols.append(w_in[:, 4608:5120])
    cols.append(w_in[:, 5120:6144])
    cols.append(w_in[:, 6144:7168])
    sh["w0"] = _blk(np.concatenate(cols, axis=1), 512)
    sh["wglr"] = _blk(w_in[:, 7168:7184], 16).reshape(128, 256)
    sh["gu"] = np.concatenate([np.asarray(inp["ab_gla_gate_up"][0], f), np.asarray(inp["ab_gla_gate_b"][0], f)[None, :]], axis=0)
    rw = np.concatenate([np.asarray(inp["ab_ret_norm_w"][0], f), np.asarray(inp["ab_gla_norm_w"][0], f)])
    sh["retw"] = np.ascontiguousarray(np.broadcast_to(rw[None, :], (128, 2048)))
    sh["wo0"] = _blk(np.asarray(inp["ab_w_out"][0], f), 512)
    wq = np.asarray(inp["c_w_qkv"][0], f)
    cols = []
    for h in range(16):
        cols.append(wq[:, h * 128:(h + 1) * 128])
        cols.append(wq[:, 2048 + h * 128: 2048 + (h + 1) * 128])
        cols.append(wq[:, 4096 + h * 128: 4096 + (h + 1) * 128])
    sh["w1"] = _blk(np.concatenate(cols, axis=1), 384)
    sh["wo1"] = _blk(np.asarray(inp["c_w_out"][0], f), 512)
    for i in range(2):
        g = np.asarray(inp["ffn_w_gate"][i], f)
        u = np.asarray(inp["ffn_w_up"][i], f)
        gu = np.concatenate([g.reshape(D, 22, 256), u.reshape(D, 22, 256)], axis=2).reshape(D, 22 * 512)
        sh[f"wgu{i}"] = _blk(gu, 512)
        sh[f"wd{i}"] = _blk(np.asarray(inp["ffn_w_down"][i], f), 128)
        sh[f"wpg{i}"] = _blk(np.asarray(inp["ple_w_gate"][i], f), 512)
        sh[f"wpp{i}"] = _blk(np.asarray(inp["ple_w_proj"][i], f), 2048).reshape(128, 2 * 2048)
    return sh


SHARED_SHAPES = {
    "invf": [128, 1], "ident": [128, 128], "rmask": [4, 128, 2560], "tri": [128, 384], "amask": [128, 4608],
    "nw": [128, 112], "w0": [14, 128, 8192], "wglr": [128, 256], "gu": [17, 512], "retw": [128, 2048],
    "wo0": [4, 128, 8192], "w1": [16, 128, 6144], "wo1": [4, 128, 8192],
    "wgu0": [22, 128, 8192], "wd0": [16, 128, 5632], "wpg0": [4, 128, 8192], "wpp0": [128, 4096],
    "wgu1": [22, 128, 8192], "wd1": [16, 128, 5632], "wpg1": [4, 128, 8192], "wpp1": [128, 4096],
}


def build_program(phases=None, dbg=False):
    nc = bass.Bass("TRN2", target_bir_lowering=False)
    es = ExitStack()
    with es:
        dr = {}
        for k, shp in SHARED_SHAPES.items():
            dr[k] = nc.dram_tensor(k, list(shp), F32, kind="ExternalInput").ap()
        xT = nc.dram_tensor("xT", [D, T], F32, kind="ExternalInput").ap()
        pT = nc.dram_tensor("pT", [2, 256, T], F32, kind="ExternalInput").ap()
        posr = nc.dram_tensor("posr", [128, T], I32, kind="ExternalInput").ap()
        outT = nc.dram_tensor("outT", [D, T], F32, kind="ExternalOutput").ap()
        skind = "ExternalOutput" if dbg else "Internal"
        R = nc.dram_tensor("R", [D, T], F32, kind=skind).ap()
        MIXT = nc.dram_tensor("MIXT", [D, T], BF16, kind=skind).ap()
        WDB = [nc.dram_tensor(f"wdb{i}", [KC, 128, FF], BF16, kind="Internal").ap() for i in range(2)]
        wdcast_n = [0, 0]

        def wd_cast(layer, n):
            issued = 0
            for _ in range(n):
                c = wdcast_n[layer]
                if c >= KC:
                    break
                wdcast_n[layer] += 1
                issued += 1
                P.dma("pool", WDB[layer][c], dr[f"wd{layer}"][c], key=f"wdc{c % 4}")
            return issued

        P = Prog(nc, es)

        tctr = [0]

        def sb(shape, dt, stack=es):
            tctr[0] += 1
            return stack.enter_context(nc.sbuf_tensor(f"t{tctr[0]}", list(shape), dt))

        A = sb([128, KC, T], BF16)
        BA = [P.buf(f"A{tg}") for tg in range(NTG)]
        wslots = [sb([128, WSLOT], BF16) for _ in range(NWS)]
        Bw = [P.buf(f"w{i}") for i in range(NWS)]
        wctr = [0]
        nwt = sb([128, 112], F32)
        identb = sb([128, 128], BF16)
        onesf = sb([128, 128], F32)
        cbias = sb([128, 4], F32)
        Bconst = P.buf("const")
        psf = [es.enter_context(nc.psum_tensor(f"psf{i}", [128, 512], F32)) for i in range(7)]
        psb = [es.enter_context(nc.psum_tensor(f"psb{i}", [128, 1024], BF16)) for i in range(1)]
        Bpsf = [P.buf(f"psf{i}") for i in range(7)]
        Bpsb = [P.buf(f"psb{i}") for i in range(1)]
        pctr = [0, 0]
        nrot = [7]

        def next_ps():
            i = pctr[0] % nrot[0]
            pctr[0] += 1
            return psf[i], Bpsf[i]

        def next_psb():
            return psb[0], Bpsb[0]

        prefetched = {}
        pending_pf = []
        for i_ in range(NWS):
            P.persist_keys.add(f"w{i_}pool")
            P.persist_keys.add(f"w{i_}sp")
            P.persist_bufs.add(Bw[i_])

        def wload(src_ap, ncols, eng="pool", key=None):
            if key is not None and key in prefetched:
                return prefetched.pop(key)
            s = wctr[0] % NWS
            wctr[0] += 1
            P.dma(eng, wslots[s][:, 0:ncols], src_ap, writes=[Bw[s]], key=f"w{s}{eng}")
            return wslots[s], Bw[s]

        def do_prefetch():
            while pending_pf:
                key, ap, ncols, eng = pending_pf.pop(0)
                prefetched[key] = wload(ap, ncols, eng)

        def wstream(items):
            st = {"next": 0, "slots": {}}

            def get(i):
                while st["next"] <= min(i + 1, len(items) - 1):
                    n = st["next"]
                    st["slots"][n] = wload(*items[n])
                    st["next"] += 1
                return st["slots"].pop(i)
            return get

        P.dma("sp", nwt[:], dr["nw"], writes=[Bconst], key="c0")
        P.dma("pool", identb[:], dr["ident"], writes=[Bconst], key="c1")
        P.op("dve", lambda e: e.memset(onesf[:], 1.0), writes=[Bconst])
        P.op("dve", lambda e: e.memset(cbias[:, 0:1], math.pi), writes=[Bconst])
        P.op("dve", lambda e: e.memset(cbias[:, 1:2], 1.0), writes=[Bconst])
        P.op("dve", lambda e: e.memset(cbias[:, 2:3], EPS), writes=[Bconst])
        P.flush()

        def tsl(tg):
            return slice(tg * 512, (tg + 1) * 512)

        rstdg = [None] * NTG
        Brg = [P.buf(f"rstdg{i}") for i in range(NTG)]
        accg = [None] * NTG
        Bag = [P.buf(f"accg{i}") for i in range(NTG)]
        chain = [None]

        def chain_open():
            chain[0] = ExitStack()
            for i in range(NTG):
                rstdg[i] = sb([128, 512], F32, chain[0])

        def chain_close():
            chain[0].close()
            chain[0] = None

        def make_stats(ls):
            sq = [sb([128, 512], F32, ls) for _ in range(2)]
            Bsq = [P.buf() for _ in range(2)]
            for i in range(NTG):
                accg[i] = sb([128, 512], F32, ls)
            cn = [0]

            def stat(h_ap, Bh, c, tg, widx, first, write_a=True):
                q = cn[0] % 2
                cn[0] += 1
                if write_a:
                    wcol = nwt[:, widx * 16 + c: widx * 16 + c + 1]
                    P.op("act", lambda e, h_ap=h_ap, c=c, tg=tg, wcol=wcol: e.activation(out=A[:, c, tsl(tg)], in_=h_ap, func=AF.Copy, scale=wcol),
                         reads=[Bh, Bconst], writes=[BA[tg]])
                P.op("act", lambda e, h_ap=h_ap, q=q: e.activation(out=sq[q][:], in_=h_ap, func=AF.Square), reads=[Bh], writes=[Bsq[q]])
                if first:
                    P.op("pool", lambda e, q=q, tg=tg: e.tensor_copy(out=accg[tg][:], in_=sq[q][:]), reads=[Bsq[q]], writes=[Bag[tg]])
                else:
                    P.op("pool", lambda e, q=q, tg=tg: e.tensor_tensor(out=accg[tg][:], in0=accg[tg][:], in1=sq[q][:], op=ALU.add),
                         reads=[Bsq[q], Bag[tg]], writes=[Bag[tg]])
            return stat

        def finish_stats(tg):
            ps, Bp = next_ps()
            P.op("pe", lambda e, ps=ps, tg=tg: e.matmul(ps[:], lhsT=onesf[:], rhs=accg[tg][:], start=True, stop=True),
                 reads=[Bag[tg], Bconst], writes=[Bp])
            P.op("act", lambda e, ps=ps, tg=tg: e.activation(out=rstdg[tg][:], in_=ps[:], func=AF.Sqrt, scale=1.0 / D, bias=cbias[:, 2:3]),
                 reads=[Bp, Bconst], writes=[Brg[tg]])
            P.op("dve", lambda e, tg=tg: e.reciprocal(out=rstdg[tg][:], in_=rstdg[tg][:]), reads=[Brg[tg]], writes=[Brg[tg]])

        def phase_norm(src, widx, to_out=None, pre_rstd=False):
            nrot[0] = 7
            with ExitStack() as ls:
                nxb = 3 if pre_rstd else 2
                xt = [sb([128, KC, 512], F32, ls) for _ in range(nxb)]
                Bxt = [P.buf() for _ in range(nxb)]
                if not pre_rstd:
                    sq = [sb([128, 512], F32, ls) for _ in range(4)]
                    Bsq = [P.buf() for _ in range(4)]
                    rstd = [sb([128, 512], F32, ls) for _ in range(2)]
                    Brs = [P.buf() for _ in range(2)]
                else:
                    rstd = Brs = [None, None, None]
                srcv = src.rearrange("(c p) t -> p c t", p=128)
                outv = to_out.rearrange("(c p) t -> p c t", p=128) if to_out is not None else None
                if pre_rstd:
                    for tg in range(min(nxb, NTG)):
                        P.dma("sp", xt[tg][:], srcv[:, :, tsl(tg)], writes=[Bxt[tg]], key=f"nx{tg}")
                for tg in range(NTG):
                    s = tg % nxb
                    if not pre_rstd:
                        P.dma("sp", xt[s][:], srcv[:, :, tsl(tg)], writes=[Bxt[s]], key=f"nx{s}")
                    elif tg >= nxb:
                        P.dma("sp", xt[s][:], srcv[:, :, tsl(tg)], writes=[Bxt[s]], key=f"nx{s}")
                    rs_t, rs_B = (rstdg[tg], Brg[tg]) if pre_rstd else (rstd[s], Brs[s])
                    if not pre_rstd:
                        ps, Bp = next_ps()
                        for c in range(KC):
                            q = c % 4
                            P.op("act", lambda e, s=s, c=c, q=q: e.activation(out=sq[q][:], in_=xt[s][:, c, :], func=AF.Square),
                                 reads=[Bxt[s]], writes=[Bsq[q]])
                            P.op("pe", lambda e, ps=ps, q=q, c=c: e.matmul(ps[:], lhsT=onesf[:], rhs=sq[q][:], start=(c == 0), stop=(c == KC - 1)),
                                 reads=[Bsq[q], Bconst], writes=[Bp])
                        P.op("dve", lambda e, ps=ps, s=s: e.tensor_scalar(out=rstd[s][:], in0=ps[:], scalar1=1.0 / D, scalar2=EPS,
                                                                         op0=ALU.mult, op1=ALU.add),
                             reads=[Bp], writes=[Brs[s]])
                        P.op("act", lambda e, s=s: e.activation(out=rstd[s][:], in_=rstd[s][:], func=AF.Sqrt),
                             reads=[Brs[s]], writes=[Brs[s]])
                        P.op("dve", lambda e, s=s: e.reciprocal(out=rstd[s][:], in_=rstd[s][:]),
                             reads=[Brs[s]], writes=[Brs[s]])
                    for c in range(KC):
                        wcol = nwt[:, widx * 16 + c: widx * 16 + c + 1]
                        eng_ = "dve"
                        if to_out is None:
                            P.op(eng_, lambda e, s=s, c=c, tg=tg, wcol=wcol, rs_t=rs_t: e.scalar_tensor_tensor(
                                out=A[:, c, tsl(tg)], in0=xt[s][:, c, :], scalar=wcol, in1=rs_t[:],
                                op0=ALU.mult, op1=ALU.mult),
                                reads=[Bxt[s], rs_B, Bconst], writes=[BA[tg]])
                        else:
                            P.op(eng_, lambda e, s=s, c=c, wcol=wcol, rs_t=rs_t: e.scalar_tensor_tensor(
                                out=xt[s][:, c, :], in0=xt[s][:, c, :], scalar=wcol, in1=rs_t[:],
                                op0=ALU.mult, op1=ALU.mult),
                                reads=[Bxt[s], rs_B, Bconst], writes=[Bxt[s]])
                    if to_out is not None:
                        P.dma("act", outv[:, :, tsl(tg)], xt[s][:], reads=[Bxt[s]], key=f"no{s}")
                do_prefetch()
                P.flush()

        def fm_proj(wdram, nblocks, kc, evac, act=None, ncols=WSLOT, chunks_per_block=4, nbw=512, tgs=range(NTG), wkey=None):
            act = A if act is None else act
            ws = wstream([(wdram[b], ncols, "pool", f"{wkey}_{b}" if wkey else None) for b in range(nblocks)])
            for b in range(nblocks):
                slot, Bs = ws(b)
                sv = slot[:, 0:kc * nbw].rearrange("p (k n) -> p k n", k=kc)
                for j in range(chunks_per_block):
                    for tg in tgs:
                        ps, Bp = next_ps()
                        for k in range(kc):
                            P.op("pe", lambda e, ps=ps, sv=sv, k=k, j=j, tg=tg: e.matmul(
                                ps[:], lhsT=sv[:, k, j * 128:(j + 1) * 128], rhs=act[:, k, tsl(tg)],
                                start=(k == 0), stop=(k == kc - 1)),
                                reads=[Bs, BA[tg]], writes=[Bp])
                        evac(ps, Bp, b * chunks_per_block + j, tg)

        def phase_wout(wdram, res_src, widx_next, wkey):
            nrot[0] = 7
            with ExitStack() as ls:
                M = sb([128, KC, T], BF16, ls)
                BM = [P.buf() for _ in range(NTG)]
                mv = MIXT.rearrange("(c p) t -> p c t", p=128)
                for tg in range(NTG):
                    P.dma("sp", M[:, :, tsl(tg)], mv[:, :, tsl(tg)], writes=[BM[tg]], key=f"ml{tg}")
                xr = [sb([128, 512], F32, ls) for _ in range(3)]
                Bxr = [P.buf() for _ in range(3)]
                ho = [sb([128, 512], F32, ls) for _ in range(3)]
                Bho = [P.buf() for _ in range(3)]
                rv = res_src.rearrange("(c p) t -> p c t", p=128)
                Rv = R.rearrange("(c p) t -> p c t", p=128)
                stat = make_stats(ls)
                nxt = [None]
                ctr = [0]
                for b in range(4):
                    slot, Bs = nxt[0] if nxt[0] is not None else wload(wdram[b], WSLOT, "pool", f"{wkey}_{b}")
                    nxt[0] = wload(wdram[b + 1], WSLOT, "pool", f"{wkey}_{b + 1}") if b + 1 < 4 else None
                    sv = slot[:, :].rearrange("p (k n) -> p k n", k=KC)
                    for j in range(4):
                        c = b * 4 + j
                        for tg in range(NTG):
                            s_ = ctr[0] % 3
                            ctr[0] += 1
                            P.dma("sp", xr[s_][:], rv[:, c, tsl(tg)], writes=[Bxr[s_]], key=f"xr{s_}")
                            ps, Bp = next_ps()
                            for k in range(KC):
                                P.op("pe", lambda e, ps=ps, sv=sv, k=k, j=j, tg=tg: e.matmul(
                                    ps[:], lhsT=sv[:, k, j * 128:(j + 1) * 128], rhs=M[:, k, tsl(tg)],
                                    start=(k == 0), stop=(k == KC - 1)), reads=[Bs, BM[tg]], writes=[Bp])
                            P.op("dve", lambda e, ps=ps, s_=s_: e.tensor_tensor(out=ho[s_][:], in0=ps[:], in1=xr[s_][:], op=ALU.add),
                                 reads=[Bp, Bxr[s_]], writes=[Bho[s_]])
                            stat(ho[s_][:], Bho[s_], c, tg, widx_next, c == 0)
                            P.dma("act", Rv[:, c, tsl(tg)], ho[s_][:], reads=[Bho[s_]], key=f"ho{s_}")
                for tg in range(NTG):
                    finish_stats(tg)
                do_prefetch()
                P.flush()

        def phase_ffn(wgu, wd, widx_next, wkey):
            nrot[0] = 7
            with ExitStack() as ls:
                actT = sb([128, FKC, 512], BF16, ls)
                Bact = P.buf()
                sl = [sb([128, 512], F32, ls) for _ in range(2)]
                Bsl = [P.buf() for _ in range(2)]
                ul = [sb([128, 512], F32, ls) for _ in range(2)]
                Bul = [P.buf() for _ in range(2)]
                stat = make_stats(ls)
                hr = [sb([128, 512], F32, ls) for _ in range(3)]
                Bhr = [P.buf() for _ in range(3)]
                hn = [sb([128, 512], F32, ls) for _ in range(3)]
                Bhn = [P.buf() for _ in range(3)]
                Rv = R.rearrange("(c p) t -> p c t", p=128)
                cnt = [0]
                items = []
                for tg in range(NTG):
                    items += [(wgu[b], WSLOT, "pool", f"{wkey}_{b}" if (tg == 0 and b < 2) else None) for b in range(22)] + [(wd[c], FF, "sp") for c in range(KC)]
                ws = wstream(items)
                for tg in range(NTG):
                    P.dma("sp", hr[0][:], Rv[:, 0, tsl(tg)], writes=[Bhr[0]], key="hr0")
                    for b in range(22):
                        slot, Bs = ws(tg * 38 + b)
                        sv = slot[:, :].rearrange("p (k n) -> p k n", k=KC)
                        for j in range(2):
                            pg, Bg = next_ps()
                            for k in range(KC):
                                P.op("pe", lambda e, pg=pg, sv=sv, k=k, j=j, tg=tg: e.matmul(
                                    pg[:], lhsT=sv[:, k, j * 128:(j + 1) * 128], rhs=A[:, k, tsl(tg)],
                                    start=(k == 0), stop=(k == KC - 1)), reads=[Bs, BA[tg]], writes=[Bg])
                            pu, Bu = next_ps()
                            for k in range(KC):
                                P.op("pe", lambda e, pu=pu, sv=sv, k=k, j=j, tg=tg: e.matmul(
                                    pu[:], lhsT=sv[:, k, 256 + j * 128:256 + (j + 1) * 128], rhs=A[:, k, tsl(tg)],
                                    start=(k == 0), stop=(k == KC - 1)), reads=[Bs, BA[tg]], writes=[Bu])
                            q = cnt[0] % 2
                            cnt[0] += 1
                            P.op("dve", lambda e, pg=pg, q=q, tg=tg: e.tensor_tensor(out=sl[q][:], in0=pg[:], in1=rstdg[tg][:], op=ALU.mult),
                                 reads=[Bg, Brg[tg]], writes=[Bsl[q]])
                            P.op("act", lambda e, q=q: e.activation(out=sl[q][:], in_=sl[q][:], func=AF.Silu),
                                 reads=[Bsl[q]], writes=[Bsl[q]])
                            P.op("dve", lambda e, pu=pu, q=q, tg=tg: e.tensor_tensor(out=ul[q][:], in0=pu[:], in1=rstdg[tg][:], op=ALU.mult),
                                 reads=[Bu, Brg[tg]], writes=[Bul[q]])
                            P.op("dve", lambda e, q=q, b=b, j=j: e.tensor_tensor(
                                out=actT[:, b * 2 + j, :], in0=sl[q][:], in1=ul[q][:], op=ALU.mult),
                                reads=[Bul[q], Bsl[q]], writes=[Bact])
                    for c in range(KC):
                        slot, Bs = ws(tg * 38 + 22 + c)
                        sv = slot[:, 0:FF].rearrange("p (k n) -> p k n", k=FKC)
                        s = c % 3
                        if c + 1 < KC:
                            s1 = (c + 1) % 3
                            P.dma("sp", hr[s1][:], Rv[:, c + 1, tsl(tg)], writes=[Bhr[s1]], key=f"hr{s1}")
                        ps, Bp = next_ps()
                        for k in range(FKC):
                            P.op("pe", lambda e, ps=ps, sv=sv, k=k: e.matmul(
                                ps[:], lhsT=sv[:, k, :], rhs=actT[:, k, :], start=(k == 0), stop=(k == FKC - 1)),
                                reads=[Bs, Bact], writes=[Bp])
                        P.op("dve", lambda e, ps=ps, s=s: e.tensor_tensor(out=hn[s][:], in0=ps[:], in1=hr[s][:], op=ALU.add),
                             reads=[Bp, Bhr[s]], writes=[Bhn[s]])
                        stat(hn[s][:], Bhn[s], c, tg, widx_next, c == 0)
                        P.dma("act", Rv[:, c, tsl(tg)], hn[s][:], reads=[Bhn[s]], key=f"hn{s}")
                    finish_stats(tg)
                do_prefetch()
                P.flush()

        def phase_ple(wpg, wpp, pTl, wkey):
            nrot[0] = 7
            with ExitStack() as ls:
                ppb = sb([128, 2, T], BF16, ls)
                wppb = sb([128, 2, T], BF16, ls)
                Bpp = P.buf()
                P.dma("pool", ppb[:], pTl.rearrange("(k p) t -> p k t", p=128), writes=[Bpp], key="pp0")
                P.dma("pool", wppb[:], wpp.rearrange("p (k n) -> p k n", k=2), writes=[Bpp], key="pp1")
                xr = [sb([128, T], F32, ls) for _ in range(2)]
                Bxr = [P.buf() for _ in range(2)]
                ho = [sb([128, T], F32, ls) for _ in range(2)]
                Bho = [P.buf() for _ in range(2)]
                gt = [sb([128, 512], F32, ls) for _ in range(2)]
                Bgt = [P.buf() for _ in range(2)]
                Rv = R.rearrange("(c p) t -> p c t", p=128)
                cnt = [0]
                stat = make_stats(ls)

                def evac(ps, Bp, c, tg):
                    s = c % 2
                    if tg == 0:
                        P.dma("sp", xr[s][:], Rv[:, c, :], writes=[Bxr[s]], key=f"xr{s}")
                    q = cnt[0] % 2
                    cnt[0] += 1
                    P.op("dve", lambda e, ps=ps, q=q, tg=tg: e.tensor_tensor(out=gt[q][:], in0=ps[:], in1=rstdg[tg][:], op=ALU.mult),
                         reads=[Bp, Brg[tg]], writes=[Bgt[q]])
                    P.op("act", lambda e, q=q: e.activation(out=gt[q][:], in_=gt[q][:], func=AF.Sigmoid),
                         reads=[Bgt[q]], writes=[Bgt[q]])
                    p2, Bp2 = next_ps()
                    for k in range(2):
                        P.op("pe", lambda e, p2=p2, k=k, c=c, tg=tg: e.matmul(
                            p2[:], lhsT=wppb[:, k, c * 128:(c + 1) * 128], rhs=ppb[:, k, tsl(tg)],
                            start=(k == 0), stop=(k == 1)), reads=[Bpp], writes=[Bp2])
                    P.op("dve", lambda e, p2=p2, q=q: e.tensor_tensor(out=gt[q][:], in0=gt[q][:], in1=p2[:], op=ALU.mult),
                         reads=[Bp2, Bgt[q]], writes=[Bgt[q]])
                    P.op("pool", lambda e, q=q, s=s, tg=tg: e.tensor_tensor(out=ho[s][:, tsl(tg)], in0=gt[q][:], in1=xr[s][:, tsl(tg)], op=ALU.add),
                         reads=[Bgt[q], Bxr[s]], writes=[Bho[s]])
                    stat(ho[s][:, tsl(tg)], Bho[s], c, tg, None, c == 0, write_a=False)
                    if tg == NTG - 1:
                        P.dma("act", Rv[:, c, :], ho[s][:], reads=[Bho[s]], key=f"ho{s}")
                fm_proj(wpg, 4, KC, evac, wkey=wkey)
                for tg in range(NTG):
                    finish_stats(tg)
                do_prefetch()
                P.flush()

        def make_finisher(ls):
            osb = [sb([128, 256], F32, ls) for _ in range(3)]
            Bosb = [P.buf() for _ in range(3)]
            sqt = [sb([128, 256], F32, ls) for _ in range(2)]
            Bsqt = [P.buf() for _ in range(2)]
            ss = [sb([128, 1], F32, ls) for _ in range(3)]
            Bss = [P.buf() for _ in range(3)]
            yb = [sb([128, 256], BF16, ls) for _ in range(3)]
            Byb = [P.buf() for _ in range(3)]
            Bquad = [Bpsb[0]] * 4
            cnt = [0, 0]

            def finish(O_ap, Bo, sg_ap, Bsg, dst_ap, Bdst):
                q = cnt[0] % 3
                q2 = cnt[0] % 2
                cnt[0] += 1
                P.op("pool", lambda e, q=q: e.memset(ss[q][:], 0.0), writes=[Bss[q]])
                P.op("act", lambda e, q=q: e.activation(out=osb[q][:], in_=O_ap, func=AF.Copy), reads=[Bo], writes=[Bosb[q]])
                P.op("act", lambda e, q=q, q2=q2: e.activation(out=sqt[q2][:], in_=osb[q][:], func=AF.Square, accum_out=ss[q][:]),
                     reads=[Bosb[q], Bss[q]], writes=[Bsqt[q2], Bss[q]])
                P.op("act", lambda e, q=q: e.activation(out=ss[q][:], in_=ss[q][:], func=AF.Sqrt, scale=1.0 / 256.0, bias=cbias[:, 2:3]),
                     reads=[Bss[q], Bconst], writes=[Bss[q]])

                def stage_a2(q=q):
                    P.op("dve", lambda e, q=q: e.reciprocal(out=ss[q][:], in_=ss[q][:]),
                         reads=[Bss[q]], writes=[Bss[q]])
                    P.op("dve", lambda e, q=q: e.scalar_tensor_tensor(out=yb[q][:], in0=osb[q][:], scalar=ss[q][:, 0:1], in1=sg_ap,
                                                                     op0=ALU.mult, op1=ALU.mult),
                         reads=[Bosb[q], Bss[q], Bsg], writes=[Byb[q]])

                def stage_b(q=q):
                    r = cnt[1] % 4
                    cnt[1] += 1
                    pt = psb[0]
                    for j in range(2):
                        P.op("pe", lambda e, q=q, j=j, r=r: e.transpose(out=pt[:, r * 256 + j * 128:r * 256 + (j + 1) * 128],
                                                                        in_=yb[q][:, j * 128:(j + 1) * 128], identity=identb[:]),
                             reads=[Byb[q], Bconst], writes=[Bquad[r]])
                    P.op("act", lambda e, r=r: e.activation(out=dst_ap, in_=pt[:, r * 256:(r + 1) * 256].rearrange("p (j t) -> p j t", j=2), func=AF.Copy),
                         reads=[Bquad[r]], writes=[Bdst])
                return stage_a2, stage_b
            return finish

        def phase_ret():
            nrot[0] = 3
            gam = _CONST["gam"]
            with ExitStack() as ls:
                MV = MIXT.rearrange("(c p) t -> p c t", p=128)
                cosT = sb([128, T], F32, ls)
                sinT = sb([128, T], F32, ls)
                Bcs = P.buf()
                with ExitStack() as l2:
                    posi = sb([128, T], I32, l2)
                    ang = sb([128, T], F32, l2)
                    invf = sb([128, 1], F32, l2)
                    Bt = P.buf()
                    P.dma("sp", posi[:], posr, writes=[Bt], key="pos0")
                    P.dma("sp", invf[:], dr["invf"], writes=[Bt], key="pos1")
                    P.op("dve", lambda e: e.tensor_copy(out=ang[:], in_=posi[:]), reads=[Bt], writes=[Bt])
                    P.op("dve", lambda e: e.tensor_scalar(out=ang[:], in0=ang[:], scalar1=invf[:, 0:1], scalar2=None, op0=ALU.mult),
                         reads=[Bt], writes=[Bt])
                    ni = sb([128, T], I32, l2)
                    nf = sb([128, T], F32, l2)
                    C1 = 6.28125
                    C2 = 2.0 * math.pi - C1
                    for dst, shift in ((sinT, 0.0), (cosT, 0.5 * math.pi)):
                        if shift != 0.0:
                            P.op("dve", lambda e, shift=shift: e.tensor_scalar(out=ang[:], in0=ang[:], scalar1=shift, scalar2=None, op0=ALU.add),
                                 reads=[Bt], writes=[Bt])
                        P.op("dve", lambda e: e.tensor_scalar(out=nf[:], in0=ang[:], scalar1=1.0 / (2.0 * math.pi), scalar2=None, op0=ALU.mult),
                             reads=[Bt], writes=[Bt])
                        P.op("dve", lambda e: e.tensor_copy(out=ni[:], in_=nf[:]), reads=[Bt], writes=[Bt])
                        P.op("dve", lambda e: e.tensor_copy(out=nf[:], in_=ni[:]), reads=[Bt], writes=[Bt])
                        P.op("dve", lambda e, dst=dst: e.scalar_tensor_tensor(out=dst[:], in0=nf[:], scalar=-C1, in1=ang[:], op0=ALU.mult, op1=ALU.add),
                             reads=[Bt], writes=[Bcs])
                        P.op("dve", lambda e, dst=dst: e.scalar_tensor_tensor(out=dst[:], in0=nf[:], scalar=-C2, in1=dst[:], op0=ALU.mult, op1=ALU.add),
                             reads=[Bt, Bcs], writes=[Bcs])
                        P.op("dve", lambda e, dst=dst: e.tensor_single_scalar(out=nf[:], in_=dst[:], scalar=math.pi, op=ALU.is_gt),
                             reads=[Bcs], writes=[Bt])
                        P.op("dve", lambda e, dst=dst: e.scalar_tensor_tensor(out=dst[:], in0=nf[:], scalar=-2.0 * math.pi, in1=dst[:], op0=ALU.mult, op1=ALU.add),
                             reads=[Bt, Bcs], writes=[Bcs])
                        P.op("dve", lambda e, dst=dst: e.tensor_single_scalar(out=nf[:], in_=dst[:], scalar=-math.pi, op=ALU.is_lt),
                             reads=[Bcs], writes=[Bt])
                        P.op("dve", lambda e, dst=dst: e.scalar_tensor_tensor(out=dst[:], in0=nf[:], scalar=2.0 * math.pi, in1=dst[:], op0=ALU.mult, op1=ALU.add),
                             reads=[Bt, Bcs], writes=[Bcs])
                        P.op("dve", lambda e, dst=dst: e.tensor_scalar(out=dst[:], in0=dst[:], scalar1=-3.1415925, scalar2=3.1415925, op0=ALU.max, op1=ALU.min),
                             reads=[Bcs], writes=[Bcs])
                        P.op("act", lambda e, dst=dst: e.activation(out=dst[:], in_=dst[:], func=AF.Sin), reads=[Bcs], writes=[Bcs])
                    P.flush()
                retw = sb([128, 1024], F32, ls)
                P.dma("sp", retw[:], dr["retw"][:, 0:1024], writes=[Bcs], key="pos2")
                rm = sb([128, 5, 512], F32, ls)
                Brm = P.buf()
                qT = sb([128, 2, T], BF16, ls)
                kT = sb([128, 2, T], BF16, ls)
                Bq = P.buf()
                Bk = P.buf()
                vtm = sb([128, NTT, 256], BF16, ls)
                Bv = P.buf()
                sg = sb([128, NTT, 256], F32, ls)
                Bsg = P.buf()
                ev = [sb([128, 512], F32, ls) for _ in range(2)]
                od = [sb([128, 512], F32, ls) for _ in range(2)]
                Bev = [P.buf() for _ in range(2)]
                Bod = [P.buf() for _ in range(2)]
                tmp = [sb([128, 512], F32, ls) for _ in range(4)]
                Btmp = [P.buf() for _ in range(4)]
                sgt = [sb([128, 256], F32, ls) for _ in range(2)]
                Bsgt = [P.buf() for _ in range(2)]
                pt = [sb([128, 512], BF16, ls) for _ in range(3)]
                Bptt = [P.buf() for _ in range(3)]
                yT = sb([128, 2, T], BF16, ls)
                ByT = P.buf()
                finish = make_finisher(ls)
                rc = [0, 0]
                import os
                RST = int(os.environ.get("RET_STAGE", "9"))
                NH = int(os.environ.get("RET_HEADS", "4"))
                for h in range(NH if RST > 0 else 0):
                    P.dma("sp", rm[:], dr["rmask"][h].rearrange("p (m n) -> p m n", m=5), writes=[Brm], key="rm")
                    if h == 0:
                        nxt_fm = wload(dr["w0"][0], WSLOT, "pool", "w0_0")
                        nxt_tm = wload(dr["w0"][1], WSLOT, "pool", "w0_1")
                    slot, Bs = nxt_fm
                    sv = slot[:, :].rearrange("p (k n) -> p k n", k=KC)
                    for qk in range(2):
                        dst, Bd = (qT, Bq) if qk == 0 else (kT, Bk)
                        for tg in range(NTG):
                            s = rc[0] % 2
                            rc[0] += 1
                            for eo in range(2):
                                ps, Bp = next_ps()
                                j = qk * 2 + eo
                                for k in range(KC):
                                    P.op("pe", lambda e, ps=ps, sv=sv, k=k, j=j, tg=tg: e.matmul(
                                        ps[:], lhsT=sv[:, k, j * 128:(j + 1) * 128], rhs=A[:, k, tsl(tg)],
                                        start=(k == 0), stop=(k == KC - 1)), reads=[Bs, BA[tg]], writes=[Bp])
                                tgt, Btg = (ev[s], Bev[s]) if eo == 0 else (od[s], Bod[s])
                                P.op("act", lambda e, ps=ps, tgt=tgt: e.activation(out=tgt[:], in_=ps[:], func=AF.Copy),
                                     reads=[Bp], writes=[Btg])
                            E_, O_ = ev[s], od[s]
                            cs, sn = cosT[:, tsl(tg)], sinT[:, tsl(tg)]
                            P.op("dve", lambda e, E_=E_, cs=cs: e.tensor_tensor(out=tmp[0][:], in0=E_[:], in1=cs, op=ALU.mult),
                                 reads=[Bev[s], Bcs], writes=[Btmp[0]])
                            P.op("pool", lambda e, O_=O_, sn=sn: e.tensor_tensor(out=tmp[1][:], in0=O_[:], in1=sn, op=ALU.mult),
                                 reads=[Bod[s], Bcs], writes=[Btmp[1]])
                            P.op("dve", lambda e, dst=dst, tg=tg: e.tensor_tensor(out=dst[:, 0, tsl(tg)], in0=tmp[0][:], in1=tmp[1][:], op=ALU.subtract),
                                 reads=[Btmp[0], Btmp[1]], writes=[Bd])
                            P.op("pool", lambda e, E_=E_, sn=sn: e.tensor_tensor(out=tmp[2][:], in0=E_[:], in1=sn, op=ALU.mult),
                                 reads=[Bev[s], Bcs], writes=[Btmp[2]])
                            P.op("dve", lambda e, O_=O_, cs=cs: e.tensor_tensor(out=tmp[3][:], in0=O_[:], in1=cs, op=ALU.mult),
                                 reads=[Bod[s], Bcs], writes=[Btmp[3]])
                            P.op("pool", lambda e, dst=dst, tg=tg: e.tensor_tensor(out=dst[:, 1, tsl(tg)], in0=tmp[2][:], in1=tmp[3][:], op=ALU.add),
                                 reads=[Btmp[2], Btmp[3]], writes=[Bd])
                    if RST < 2:
                        continue
                    if h + 1 < NH:
                        nxt_fm = wload(dr["w0"][2 * h + 2], WSLOT)
                    wd_cast(0, 2)
                    slot, Bs = nxt_tm
                    sv = slot[:, :].rearrange("p (k n) -> p k n", k=KC)
                    for tt in range(NTT):
                        ps, Bp = next_ps()
                        tg = tt // 4
                        for k in range(KC):
                            P.op("pe", lambda e, ps=ps, sv=sv, k=k, tt=tt: e.matmul(
                                ps[:], lhsT=A[:, k, tt * 128:(tt + 1) * 128], rhs=sv[:, k, :],
                                start=(k == 0), stop=(k == KC - 1)), reads=[Bs, BA[tg]], writes=[Bp])
                        P.op("act", lambda e, ps=ps, tt=tt: e.activation(out=vtm[:, tt, :], in_=ps[:, 0:256], func=AF.Copy),
                             reads=[Bp], writes=[Bv])
                        q = tt % 2
                        P.op("act", lambda e, ps=ps, q=q: e.activation(out=sgt[q][:], in_=ps[:, 256:512], func=AF.Silu),
                             reads=[Bp], writes=[Bsgt[q]])
                        P.op("pool", lambda e, q=q, tt=tt, h=h: e.tensor_tensor(out=sg[:, tt, :], in0=sgt[q][:], in1=retw[:, h * 256:(h + 1) * 256], op=ALU.mult),
                             reads=[Bsgt[q], Bcs], writes=[Bsg])
                    if RST < 3:
                        continue
                    if h + 1 < NH:
                        nxt_tm = wload(dr["w0"][2 * h + 3], WSLOT)
                    wd_cast(0, 2)
                    LA = 2
                    blocks = [(G, J) for G in range(4) for J in range(4 * G + 4)]
                    Sps = {}
                    issued = 0
                    pending = []
                    Ot = [(psf[3 + i], Bpsf[3 + i]) for i in range(4)]
                    for bi, (G, J) in enumerate(blocks):
                        while issued < min(bi + 1 + LA, len(blocks)):
                            G2, J2 = blocks[issued]
                            ps2, Bp2 = next_ps()
                            for eo in range(2):
                                P.op("pe", lambda e, ps2=ps2, eo=eo, J2=J2, G2=G2: e.matmul(
                                    ps2[:], lhsT=kT[:, eo, J2 * 128:(J2 + 1) * 128], rhs=qT[:, eo, tsl(G2)],
                                    start=(eo == 0), stop=(eo == 1)), reads=[Bq, Bk], writes=[Bp2])
                            Sps[issued] = (ps2, Bp2)
                            issued += 1
                        ps, Bp = Sps.pop(bi)
                        if J < 4 * G:
                            scal = float(gam[h] ** ((4 * G - J) * 128))
                            mk = rm[:, 0, :]
                        else:
                            scal = 1.0
                            mk = rm[:, 1 + (J - 4 * G), :]
                        s = rc[1] % 3
                        rc[1] += 1
                        P.op("dve", lambda e, ps=ps, s=s, scal=scal, mk=mk: e.scalar_tensor_tensor(
                            out=pt[s][:], in0=ps[:], scalar=scal, in1=mk, op0=ALU.mult, op1=ALU.mult),
                            reads=[Bp, Brm], writes=[Bptt[s]])
                        for i in range(4):
                            if 4 * G + i >= J:
                                ob, Bob = Ot[i]
                                P.op("pe", lambda e, ob=ob, s=s, i=i, J=J, G=G: e.matmul(
                                    ob[:, 0:256], lhsT=pt[s][:, i * 128:(i + 1) * 128], rhs=vtm[:, J, :],
                                    start=(J == 0), stop=(J == 4 * G + i)), reads=[Bptt[s], Bv], writes=[Bob])
                        pending.sort(key=lambda t: (t[0], t[1]))
                        while pending and pending[0][0] <= bi:
                            pending.pop(0)[2]()
                        if J >= 4 * G:
                            i = J - 4 * G
                            ob, Bob = Ot[i]
                            tt = 4 * G + i
                            sta2, stb = finish(ob[:, 0:256], Bob, sg[:, tt, :], Bsg, yT[:, :, tt * 128:(tt + 1) * 128], ByT)
                            pending.append((bi + 1, 2 * bi, sta2))
                            pending.append((bi + 3, 2 * bi + 1, stb))
                    pending.sort(key=lambda t: (t[0], t[1]))
                    while pending:
                        pending.pop(0)[2]()
                    P.dma("sp", MV[:, 2 * h:2 * h + 2, :], yT[:], reads=[ByT], key="yT")
                do_prefetch()
                P.flush()

        def phase_gla():
            nrot[0] = 3
            with ExitStack() as ls:
                MV = MIXT.rearrange("(c p) t -> p c t", p=128)
                Bg = P.buf()
                retw = sb([128, 1024], F32, ls)
                P.dma("sp", retw[:], dr["retw"][:, 1024:2048], writes=[Bg], key="g0a")
                tri = sb([128, 3, 128], F32, ls)
                P.dma("sp", tri[:], dr["tri"].rearrange("p (m n) -> p m n", m=3), writes=[Bg], key="g0b")
                GU = sb([32, 512], F32, ls)
                P.dma("sp", GU[0:17, :], dr["gu"], writes=[Bg], key="g0c")
                wg = sb([128, KC, 16], BF16, ls)
                P.dma("pool", wg[:], dr["wglr"].rearrange("p (k n) -> p k n", k=KC), writes=[Bg], key="g1")
                glrT = sb([32, T], F32, ls)
                Bglr = P.buf()
                P.op("dve", lambda e: e.memset(glrT[:], 1.0), writes=[Bglr])
                for tg in range(NTG):
                    ps, Bp = next_ps()
                    for k in range(KC):
                        P.op("pe", lambda e, ps=ps, k=k, tg=tg: e.matmul(ps[0:16, :], lhsT=wg[:, k, :], rhs=A[:, k, tsl(tg)],
                                                                       start=(k == 0), stop=(k == KC - 1)),
                             reads=[Bg, BA[tg]], writes=[Bp])
                    P.op("act", lambda e, ps=ps, tg=tg: e.activation(out=glrT[0:16, tsl(tg)], in_=ps[0:16, :], func=AF.Copy),
                         reads=[Bp], writes=[Bglr])
                L_ = tri[:, 0, :]
                U_ = tri[:, 1, :]
                M2 = tri[:, 2, :]
                sp_tm = sb([128, 4, 512], F32, ls)
                Bsp = P.buf()
                ext = [sb([128, 512], F32, ls) for _ in range(2)]
                Bext = [P.buf() for _ in range(2)]

                epos = sb([128, 4, 512], F32, ls)
                Bep = P.buf()
                qfull = sb([128, 4, 512], BF16, ls)
                Bqf = P.buf()
                qa = sb([128, 4, 4, 128], BF16, ls)
                qb = sb([128, 4, 4, 128], BF16, ls)
                Bqa = P.buf()
                ktl = sb([128, 4, 512], BF16, ls)
                Bkt = P.buf()
                kk = sb([128, 4, 512], BF16, ls)
                Bkk = P.buf()
                vtm = sb([128, 4, 1024], BF16, ls)
                Bv = P.buf()
                sgg = sb([128, 4, 1024], BF16, ls)
                Bsgg = P.buf()
                St = sb([128, 4, 256], F32, ls)
                BSt = [P.buf() for _ in range(4)]
                Sbf = [sb([128, 4, 256], BF16, ls) for _ in range(2)]
                BSbf = [[P.buf() for _ in range(4)] for _ in range(2)]
                ptile = [sb([128, 128], BF16, ls) for _ in range(4)]
                Bpti = [P.buf() for _ in range(4)]
                yTg = sb([128, 8, 512], BF16, ls)
                ByT = P.buf()
                finish = make_finisher(ls)
                P.op("pool", lambda e: e.memset(qa[:], 0.0), writes=[Bqa])
                P.op("pool", lambda e: e.memset(qb[:], 0.0), writes=[Bqa])
                P.op("pool", lambda e: e.memset(St[:], 0.0), writes=BSt)
                P.op("pool", lambda e: e.memset(Sbf[0][:], 0.0), writes=BSbf[0])
                gws = wstream([(dr["w0"][8 + i], WSLOT, "pool", f"w0_{8 + i}" if (t_ == 0 and i < 2) else None) for t_ in range(NTG) for i in range(6)])
                sbi = [0, 0, 0, 0]
                gpend = []
                cnt = [0, 0]
                for tg in range(NTG):
                    for u in range(4):
                        tt = 4 * tg + u
                        ps, Bp = next_ps()
                        P.op("pe", lambda e, ps=ps, tt=tt: e.matmul(ps[:], lhsT=glrT[0:17, tt * 128:(tt + 1) * 128], rhs=GU[0:17, :],
                                                                   start=True, stop=True), reads=[Bglr, Bg], writes=[Bp])
                        q = cnt[0] % 2
                        cnt[0] += 1
                        P.op("act", lambda e, ps=ps, q=q: e.activation(out=ext[q][:], in_=ps[:], func=AF.Exp, scale=-1.0),
                             reads=[Bp], writes=[Bext[q]])
                        P.op("act", lambda e, q=q, u=u: e.activation(out=sp_tm[:, u, :], in_=ext[q][:], func=AF.Ln, bias=cbias[:, 1:2]),
                             reads=[Bext[q], Bconst], writes=[Bsp])
                    for h in range(4):
                        ps, Bp = next_ps()
                        for u in range(4):
                            P.op("pe", lambda e, ps=ps, u=u, h=h: e.matmul(ps[:, u * 128:(u + 1) * 128], lhsT=sp_tm[:, u, h * 128:(h + 1) * 128],
                                                                          rhs=L_, start=True, stop=True), reads=[Bsp, Bg], writes=[Bp])
                        P.op("act", lambda e, ps=ps, h=h: e.activation(out=epos[:, h, :], in_=ps[:], func=AF.Exp), reads=[Bp], writes=[Bep])
                    slot, Bs = gws(tg * 6 + 0)
                    sv = slot[:, :].rearrange("p (k n) -> p k n", k=KC)
                    for h in range(4):
                        ps, Bp = next_ps()
                        for k in range(KC):
                            P.op("pe", lambda e, ps=ps, sv=sv, k=k, h=h, tg=tg: e.matmul(
                                ps[:], lhsT=sv[:, k, h * 128:(h + 1) * 128], rhs=A[:, k, tsl(tg)], start=(k == 0), stop=(k == KC - 1)),
                                reads=[Bs, BA[tg]], writes=[Bp])
                        P.op("dve", lambda e, ps=ps, h=h: e.scalar_tensor_tensor(out=qfull[:, h, :], in0=ps[:], scalar=128 ** -0.5, in1=epos[:, h, :],
                                                                                op0=ALU.mult, op1=ALU.mult), reads=[Bp, Bep], writes=[Bqf])
                        qv = qfull[:, h, :].rearrange("p (u t) -> p u t", u=4)
                        P.op("pool", lambda e, qv=qv, h=h: e.tensor_copy(out=qa[:, h, :, 0:64], in_=qv[:, :, 0:64]), reads=[Bqf], writes=[Bqa])
                        P.op("pool", lambda e, qv=qv, h=h: e.tensor_copy(out=qb[:, h, :, 64:128], in_=qv[:, :, 64:128]), reads=[Bqf], writes=[Bqa])
                    slotk, Bsk = gws(tg * 6 + 1)
                    svk = slotk[:, :].rearrange("p (k n) -> p k n", k=KC)
                    for h in range(4):
                        ps, Bp = next_ps()
                        for k in range(KC):
                            P.op("pe", lambda e, ps=ps, svk=svk, k=k, h=h, tg=tg: e.matmul(
                                ps[:], lhsT=svk[:, k, h * 128:(h + 1) * 128], rhs=A[:, k, tsl(tg)], start=(k == 0), stop=(k == KC - 1)),
                                reads=[Bsk, BA[tg]], writes=[Bp])
                        q = cnt[0] % 2
                        cnt[0] += 1
                        P.op("dve", lambda e, q=q, h=h: e.reciprocal(out=ext[q][:], in_=epos[:, h, :]), reads=[Bep], writes=[Bext[q]])
                        P.op("dve", lambda e, ps=ps, h=h, q=q: e.tensor_tensor(out=ktl[:, h, :], in0=ps[:], in1=ext[q][:], op=ALU.mult),
                             reads=[Bp, Bext[q]], writes=[Bkt])
                    for u in range(4):
                        tt = 4 * tg + u
                        ps, Bp = next_ps()
                        for k in range(KC):
                            P.op("pe", lambda e, ps=ps, svk=svk, k=k, tt=tt: e.matmul(
                                ps[:], lhsT=A[:, k, tt * 128:(tt + 1) * 128], rhs=svk[:, k, :], start=(k == 0), stop=(k == KC - 1)),
                                reads=[Bsk, BA[tg]], writes=[Bp])
                        ps2, Bp2 = next_ps()
                        P.op("pe", lambda e, ps2=ps2, u=u: e.matmul(ps2[:], lhsT=U_, rhs=sp_tm[:, u, :], start=True, stop=True),
                             reads=[Bsp, Bg], writes=[Bp2])
                        q = cnt[0] % 2
                        cnt[0] += 1
                        P.op("act", lambda e, ps2=ps2, q=q: e.activation(out=ext[q][:], in_=ps2[:], func=AF.Exp), reads=[Bp2], writes=[Bext[q]])
                        P.op("dve", lambda e, ps=ps, q=q, u=u: e.tensor_tensor(out=kk[:, u, :], in0=ps[:], in1=ext[q][:], op=ALU.mult),
                             reads=[Bp, Bext[q]], writes=[Bkk])
                    for half in range(2):
                        slot, Bs = gws(tg * 6 + 2 + half)
                        sv = slot[:, :].rearrange("p (k n) -> p k n", k=KC)
                        for u in range(4):
                            tt = 4 * tg + u
                            ps, Bp = next_ps()
                            for k in range(KC):
                                P.op("pe", lambda e, ps=ps, sv=sv, k=k, tt=tt: e.matmul(
                                    ps[:], lhsT=A[:, k, tt * 128:(tt + 1) * 128], rhs=sv[:, k, :], start=(k == 0), stop=(k == KC - 1)),
                                    reads=[Bs, BA[tg]], writes=[Bp])
                            P.op("act", lambda e, ps=ps, u=u, half=half: e.activation(out=vtm[:, u, half * 512:(half + 1) * 512], in_=ps[:], func=AF.Copy),
                                 reads=[Bp], writes=[Bv])
                    for half in range(2):
                        slot, Bs = gws(tg * 6 + 4 + half)
                        sv = slot[:, :].rearrange("p (k n) -> p k n", k=KC)
                        for u in range(4):
                            tt = 4 * tg + u
                            ps, Bp = next_ps()
                            for k in range(KC):
                                P.op("pe", lambda e, ps=ps, sv=sv, k=k, tt=tt: e.matmul(
                                    ps[:], lhsT=A[:, k, tt * 128:(tt + 1) * 128], rhs=sv[:, k, :], start=(k == 0), stop=(k == KC - 1)),
                                    reads=[Bs, BA[tg]], writes=[Bp])
                            q = cnt[0] % 2
                            cnt[0] += 1
                            P.op("act", lambda e, ps=ps, q=q: e.activation(out=ext[q][:], in_=ps[:], func=AF.Silu), reads=[Bp], writes=[Bext[q]])
                            P.op("pool", lambda e, q=q, u=u, half=half: e.tensor_tensor(out=sgg[:, u, half * 512:(half + 1) * 512], in0=ext[q][:],
                                                                                        in1=retw[:, half * 512:(half + 1) * 512], op=ALU.mult),
                                 reads=[Bext[q], Bg], writes=[Bsgg])
                    for u in range(4):
                        tt = 4 * tg + u
                        psS, BpS = psf[0], Bpsf[0]
                        for h in range(4):
                            P.op("pe", lambda e, u=u, h=h: e.matmul(psS[:, h * 128:(h + 1) * 128], lhsT=ktl[:, h, u * 128:(u + 1) * 128],
                                                                   rhs=qfull[:, h, u * 128:(u + 1) * 128], start=True, stop=True),
                                 reads=[Bkt, Bqf], writes=[BpS])
                        for h in range(4):
                            P.op("dve", lambda e, h=h: e.tensor_tensor(out=ptile[h][:], in0=psS[:, h * 128:(h + 1) * 128], in1=M2, op=ALU.mult),
                                 reads=[BpS, Bg], writes=[Bpti[h]])
                        pos_ = [(psf[3 + h], Bpsf[3 + h]) for h in range(4)]
                        for h in range(4):
                            po, Bpo = pos_[h]
                            vv = vtm[:, u, h * 256:(h + 1) * 256]
                            cur = sbi[h]
                            P.op("pe", lambda e, po=po, h=h, vv=vv: e.matmul(po[:, 0:256], lhsT=ptile[h][:], rhs=vv, start=True, stop=False),
                                 reads=[Bpti[h], Bv], writes=[Bpo])
                            P.op("pe", lambda e, po=po, u=u, h=h, cur=cur: e.matmul(po[:, 0:256], lhsT=qa[:, h, u, :], rhs=Sbf[cur][:, h, :],
                                                                                 start=False, stop=False),
                                 reads=[Bqa, BSbf[cur][h]], writes=[Bpo])
                        for half in range(2):
                            lo = half * 64
                            for h in range(4):
                                pst, Bpst = psf[1 + h // 2], Bpsf[1 + h // 2]
                                off = (h % 2) * 256
                                P.op("pe", lambda e, pst=pst, off=off, u=u, h=h, lo=lo: e.matmul(
                                    pst[:, off:off + 256], lhsT=kk[lo:lo + 64, u, h * 128:(h + 1) * 128], rhs=vtm[lo:lo + 64, u, h * 256:(h + 1) * 256],
                                    start=True, stop=True), reads=[Bkk, Bv], writes=[Bpst])
                            for h in range(4):
                                pst, Bpst = psf[1 + h // 2], Bpsf[1 + h // 2]
                                off = (h % 2) * 256
                                col = u * 128 + lo + 63
                                P.op("dve", lambda e, pst=pst, off=off, h=h, col=col: e.scalar_tensor_tensor(
                                    out=St[:, h, :], in0=St[:, h, :], scalar=epos[:, h, col:col + 1], in1=pst[:, off:off + 256],
                                    op0=ALU.mult, op1=ALU.add), reads=[Bpst, Bep, BSt[h]], writes=[BSt[h]])
                                nxt = 1 - sbi[h]
                                P.op("act", lambda e, h=h, nxt=nxt: e.activation(out=Sbf[nxt][:, h, :], in_=St[:, h, :], func=AF.Copy),
                                     reads=[BSt[h]], writes=[BSbf[nxt][h]])
                                sbi[h] = nxt
                            if half == 0:
                                for h in range(4):
                                    po, Bpo = pos_[h]
                                    nxt = sbi[h]
                                    P.op("pe", lambda e, po=po, u=u, h=h, nxt=nxt: e.matmul(po[:, 0:256], lhsT=qb[:, h, u, :], rhs=Sbf[nxt][:, h, :],
                                                                                         start=False, stop=True),
                                         reads=[Bqa, BSbf[nxt][h]], writes=[Bpo])
                                for h in range(4):
                                    po, Bpo = pos_[h]
                                    gi = cnt[1]
                                    cnt[1] += 1
                                    gpend.sort(key=lambda t: (t[0], t[1]))
                                    while gpend and gpend[0][0] <= gi:
                                        gpend.pop(0)[2]()
                                    sta2, stb = finish(po[:, 0:256], Bpo, sgg[:, u, h * 256:(h + 1) * 256], Bsgg,
                                                       yTg[:, 2 * h:2 * h + 2, u * 128:(u + 1) * 128], ByT)
                                    gpend.append((gi + 2, 2 * gi, sta2))
                                    gpend.append((gi + 4, 2 * gi + 1, stb))
                    gpend.sort(key=lambda t: (t[0], t[1]))
                    while gpend:
                        gpend.pop(0)[2]()
                    P.dma("sp", MV[:, 8:16, tsl(tg)], yTg[:], reads=[ByT], key="yTg")
                do_prefetch()
                P.flush()

        def phase_attn():
            nrot[0] = 3
            with ExitStack() as ls:
                MV = MIXT.rearrange("(c p) t -> p c t", p=128)
                am = sb([128, 9, 512], BF16, ls)
                Bam = P.buf()
                P.dma("pool", am[:], dr["amask"].rearrange("p (m n) -> p m n", m=9), writes=[Bam], key="am")
                qTs = [sb([128, T], BF16, ls) for _ in range(2)]
                kTs = [sb([128, T], BF16, ls) for _ in range(2)]
                Bqs = [P.buf() for _ in range(2)]
                Bks = [P.buf() for _ in range(2)]
                vtms = [sb([128, NTT, 130], BF16, ls) for _ in range(2)]
                Bvs = [P.buf() for _ in range(2)]
                for i in range(2):
                    P.op("dve", lambda e, i=i: e.memset(vtms[i][:], 0.0), writes=[Bvs[i]])
                    P.op("dve", lambda e, i=i: e.memset(vtms[i][:, :, 128:129], 1.0), writes=[Bvs[i]])
                ex = [sb([128, 512], BF16, ls) for _ in range(3)]
                Bex = [P.buf() for _ in range(3)]
                pt = [sb([128, 512], BF16, ls) for _ in range(3)]
                Bptt = [P.buf() for _ in range(3)]
                rden = [sb([128, 1], F32, ls) for _ in range(2)]
                Brd = [P.buf() for _ in range(2)]
                yb = [sb([128, 128], BF16, ls) for _ in range(2)]
                Byb = [P.buf() for _ in range(2)]
                yT = [sb([128, T], BF16, ls) for _ in range(2)]
                ByT = [P.buf() for _ in range(2)]
                Bhalf = [Bpsb[0]] * 2
                cnt = [0, 0]
                LA = 2
                blocks = [(G, J) for G in range(4) for J in range(4 * G + 4)]
                nxt_w = wload(dr["w1"][0], 6144, "pool", "w1_0")
                for h in range(16):
                    slot, Bs = nxt_w
                    hb = h % 2
                    qT, kT, vtm, Bq, Bk, Bv = qTs[hb], kTs[hb], vtms[hb], Bqs[hb], Bks[hb], Bvs[hb]
                    sv = slot[:, 0:6144].rearrange("p (k n) -> p k n", k=KC)
                    for j in range(2):
                        dst, Bd = (qT, Bq) if j == 0 else (kT, Bk)
                        sc = 128 ** -0.5 if j == 0 else 1.0
                        for tg in range(NTG):
                            ps, Bp = next_ps()
                            for k in range(KC):
                                P.op("pe", lambda e, ps=ps, sv=sv, k=k, j=j, tg=tg: e.matmul(
                                    ps[:], lhsT=sv[:, k, j * 128:(j + 1) * 128], rhs=A[:, k, tsl(tg)], start=(k == 0), stop=(k == KC - 1)),
                                    reads=[Bs, BA[tg]], writes=[Bp])
                            P.op("act", lambda e, ps=ps, dst=dst, tg=tg, sc=sc: e.activation(out=dst[:, tsl(tg)], in_=ps[:], func=AF.Copy, scale=sc),
                                 reads=[Bp], writes=[Bd])
                    for g4 in range(4):
                        ps, Bp = next_ps()
                        for u in range(4):
                            tt = g4 * 4 + u
                            for k in range(KC):
                                P.op("pe", lambda e, ps=ps, sv=sv, k=k, tt=tt, u=u: e.matmul(
                                    ps[:, u * 128:(u + 1) * 128], lhsT=A[:, k, tt * 128:(tt + 1) * 128], rhs=sv[:, k, 256:384],
                                    start=(k == 0), stop=(k == KC - 1)), reads=[Bs, BA[g4]], writes=[Bp])
                        P.op("act", lambda e, ps=ps, g4=g4, vtm=vtm: e.activation(out=vtm[:, g4 * 4:(g4 + 1) * 4, 0:128],
                                                                                 in_=ps[:].rearrange("p (u n) -> p u n", u=4), func=AF.Copy),
                             reads=[Bp], writes=[Bv])
                    if h + 1 < 16:
                        nxt_w = wload(dr["w1"][h + 1], 6144, "pool", f"w1_{h + 1}")
                    wd_cast(1, 1)
                    yq = h % 2
                    Sps = {}
                    issued = 0
                    apend = []
                    for bi, (G, J) in enumerate(blocks):
                        while issued < min(bi + 1 + LA, len(blocks)):
                            G2, J2 = blocks[issued]
                            ps2, Bp2 = next_ps()
                            P.op("pe", lambda e, ps2=ps2, J2=J2, G2=G2, kT=kT, qT=qT: e.matmul(
                                ps2[:], lhsT=kT[:, J2 * 128:(J2 + 1) * 128], rhs=qT[:, tsl(G2)], start=True, stop=True),
                                reads=[Bq, Bk], writes=[Bp2])
                            Sps[issued] = (ps2, Bp2)
                            issued += 1
                        ps, Bp = Sps.pop(bi)
                        Ot = [(psf[3 + i], Bpsf[3 + i]) for i in range(4)]
                        s = cnt[0] % 3
                        cnt[0] += 1
                        P.op("act", lambda e, ps=ps, s=s: e.activation(out=ex[s][:], in_=ps[:], func=AF.Exp), reads=[Bp], writes=[Bex[s]])
                        mi = min(4 * G - J, 5) + 3
                        P.op("dve", lambda e, s=s, mi=mi: e.tensor_tensor(out=pt[s][:], in0=ex[s][:], in1=am[:, mi, :], op=ALU.mult),
                             reads=[Bex[s], Bam], writes=[Bptt[s]])
                        apend.sort(key=lambda t: (t[0], t[1]))
                        while apend and apend[0][0] <= bi:
                            apend.pop(0)[2]()
                        for i in range(4):
                            if 4 * G + i >= J:
                                ob, Bob = Ot[i]
                                P.op("pe", lambda e, ob=ob, s=s, i=i, J=J, G=G, vtm=vtm: e.matmul(
                                    ob[:, 0:130], lhsT=pt[s][:, i * 128:(i + 1) * 128], rhs=vtm[:, J, :],
                                    start=(J == 0), stop=(J == 4 * G + i)), reads=[Bptt[s], Bv], writes=[Bob])
                        if J >= 4 * G:
                            i = J - 4 * G
                            ob, Bob = Ot[i]
                            q = cnt[1] % 2
                            cnt[1] += 1
                            hf = G % 2

                            def fa(ob=ob, Bob=Bob, q=q):
                                P.op("dve", lambda e, ob=ob, q=q: e.reciprocal(out=rden[q][:], in_=ob[:, 128:129]),
                                     reads=[Bob], writes=[Brd[q]])
                                P.op("dve", lambda e, ob=ob, q=q: e.tensor_scalar(out=yb[q][:], in0=ob[:, 0:128], scalar1=rden[q][:, 0:1], scalar2=None,
                                                                                 op0=ALU.mult),
                                     reads=[Bob, Brd[q]], writes=[Byb[q]])

                            def fb(i=i, q=q, hf=hf, G=G, yq=yq):
                                ptp = psb[0]
                                P.op("pe", lambda e, ptp=ptp, i=i, q=q, hf=hf: e.transpose(out=ptp[:, hf * 512 + i * 128:hf * 512 + (i + 1) * 128],
                                                                                          in_=yb[q][:], identity=identb[:]),
                                     reads=[Byb[q], Bconst], writes=[Bhalf[hf]])
                                if i == 3:
                                    P.op("dve", lambda e, ptp=ptp, G=G, yq=yq, hf=hf: e.tensor_copy(out=yT[yq][:, tsl(G)], in_=ptp[:, hf * 512:(hf + 1) * 512]),
                                         reads=[Bhalf[hf]], writes=[ByT[yq]])
                            apend.append((bi + 1, 2 * bi, fa))
                            apend.append((bi + 2, 2 * bi + 1, fb))
                    apend.sort(key=lambda t: (t[0], t[1]))
                    while apend:
                        apend.pop(0)[2]()
                    P.dma("sp", MV[:, h, :], yT[yq][:], reads=[ByT[yq]], key=f"yTa{yq}")
                do_prefetch()
                P.flush()

        allp = ["norm0a", "ret", "gla", "wout0", "ffn0", "ple0",
                "norm1a", "attn", "wout1", "ffn1", "ple1", "final"]
        run = allp if phases is None else phases
        PF = {
            "norm0a": [("w0_0", dr["w0"][0], WSLOT, "pool"), ("w0_1", dr["w0"][1], WSLOT, "pool")],
            "ret": [("w0_8", dr["w0"][8], WSLOT, "pool"), ("w0_9", dr["w0"][9], WSLOT, "pool")],
            "gla": [("wo0_0", dr["wo0"][0], WSLOT, "pool"), ("wo0_1", dr["wo0"][1], WSLOT, "pool")],
            "wout0": [("wgu0_0", dr["wgu0"][0], WSLOT, "pool"), ("wgu0_1", dr["wgu0"][1], WSLOT, "pool")],
            "ffn0": [("wpg0_0", dr["wpg0"][0], WSLOT, "pool"), ("wpg0_1", dr["wpg0"][1], WSLOT, "pool")],
            "ple0": [("w1_0", dr["w1"][0], 6144, "pool"), ("w1_1", dr["w1"][1], 6144, "pool")],
            "attn": [("wo1_0", dr["wo1"][0], WSLOT, "pool"), ("wo1_1", dr["wo1"][1], WSLOT, "pool")],
            "wout1": [("wgu1_0", dr["wgu1"][0], WSLOT, "pool"), ("wgu1_1", dr["wgu1"][1], WSLOT, "pool")],
            "ffn1": [("wpg1_0", dr["wpg1"][0], WSLOT, "pool"), ("wpg1_1", dr["wpg1"][1], WSLOT, "pool")],
        }
        for pi, ph in enumerate(run):
            nxt_ph = [p2 for p2 in run[pi + 1:] if not p2.startswith("norm")][:1]
            want = {"norm0a": "ret", "ret": "gla", "gla": "wout0", "wout0": "ffn0", "ffn0": "ple0", "ple0": "attn",
                    "attn": "wout1", "wout1": "ffn1", "ffn1": "ple1"}.get(ph)
            if want is not None and nxt_ph and nxt_ph[0] == want:
                pending_pf.extend(PF[ph])
            if ph == "norm0a":
                phase_norm(xT, 0)
            elif ph == "ret":
                phase_ret()
            elif ph == "gla":
                phase_gla()
            elif ph == "wout0":
                chain_open()
                phase_wout(dr["wo0"], xT, 1, "wo0")
            elif ph == "ffn0":
                if wd_cast(0, KC):
                    P.flush()
                phase_ffn(dr["wgu0"], WDB[0], 2, "wgu0")
            elif ph == "ple0":
                phase_ple(dr["wpg0"], dr["wpp0"], pT[0], "wpg0")
                if not (pi + 1 < len(run) and run[pi + 1] == "norm1a"):
                    chain_close()
            elif ph == "norm1a":
                if chain[0] is not None:
                    phase_norm(R, 3, pre_rstd=True)
                    chain_close()
                else:
                    phase_norm(R, 3)
            elif ph == "attn":
                phase_attn()
            elif ph == "wout1":
                chain_open()
                phase_wout(dr["wo1"], R, 4, "wo1")
            elif ph == "ffn1":
                if wd_cast(1, KC):
                    P.flush()
                phase_ffn(dr["wgu1"], WDB[1], 5, "wgu1")
            elif ph == "ple1":
                phase_ple(dr["wpg1"], dr["wpp1"], pT[1], "wpg1")
                if not (pi + 1 < len(run) and run[pi + 1] == "final"):
                    chain_close()
            elif ph == "final":
                if chain[0] is not None:
                    phase_norm(R, 6, to_out=outT, pre_rstd=True)
                    chain_close()
                else:
                    phase_norm(R, 6, to_out=outT)
            elif ph == "copyx":
                with ExitStack() as ls:
                    t = sb([128, KC, 512], F32, ls)
                    Bt = P.buf()
                    for tg in range(NTG):
                        P.dma("sp", t[:], xT.rearrange("(c p) t -> p c t", p=128)[:, :, tsl(tg)], writes=[Bt], key="cx", waw=True)
                        P.dma("sp", R.rearrange("(c p) t -> p c t", p=128)[:, :, tsl(tg)], t[:], reads=[Bt], key="cy")
                    P.flush()
    return nc


_PROG = None


def kernel(**inputs):
    global _PROG
    inp = {k: np.asarray(v) for k, v in inputs.items()}
    sh = _prep_shared(inp)
    if _PROG is None:
        _PROG = build_program()
    in_maps = []
    for b in range(NCORES):
        m = dict(sh)
        m["xT"] = np.ascontiguousarray(inp["x"][b].T.astype(np.float32))
        m["pT"] = np.ascontiguousarray(np.transpose(inp["p"][:, b], (0, 2, 1)).astype(np.float32))
        m["posr"] = np.ascontiguousarray(np.broadcast_to(inp["positions"][b].astype(np.int32)[None, :], (128, T)))
        in_maps.append(m)
    res = run_bass_kernel_spmd(_PROG, in_maps, core_ids=list(range(NCORES)))
    out = np.stack([np.ascontiguousarray(res.results[b]["outT"].T) for b in range(NCORES)], axis=0)
    return out.astype(np.float32)
```
